# Optimizing a Trainium2 kernel written in Bass

```python
import jax, jax.numpy as jnp
from jax import lax
import numpy as np

D_MODEL = 1024
BATCH = 8
SEQ = 2048
DEPTH = 4
DEC_BATCH = 128
DEC_SEQ = 8
PAST_LEN = 16384
PAGE_SIZE = 128

N_MIXERS = 2
N_LRU = (DEPTH + 1) // 2
N_RWKV = DEPTH // 2
D_RNN = D_MODEL
LRU_BLOCKS = 4
LRU_BLOCK_W = D_RNN // LRU_BLOCKS
CONV_W = 4
RG_C = 8.0
HEAD_SIZE = 64
RWKV_HEADS = D_MODEL // HEAD_SIZE
D_DECAY_LORA = 64
D_AAA_LORA = 64
D_MV_LORA = 32
D_GATE_LORA = 160
D_FF = 2816
FFN_RES = 0.5
N_MOD = 9
NORM_EPS = 1e-6
GN_EPS = HEAD_SIZE * 1e-5

kernel_name = 'hybrid_rglru_rwkv7_adaln_macaron_step'


def rms_norm(x):
    xf = x.astype(jnp.float32)
    return (xf * lax.rsqrt(jnp.mean(xf * xf, axis=-1, keepdims=True) + NORM_EPS)).astype(x.dtype)


def modulate(x, shift, scale):
    return rms_norm(x) * (1.0 + scale) + shift


def swiglu(h, w_in, w_out):
    gate, up = jnp.split(h @ w_in, 2, axis=-1)
    return (jax.nn.silu(gate) * up) @ w_out


def causal_conv(x, buf, w, b):
    T = x.shape[1]
    xp = jnp.concatenate([buf.astype(x.dtype), x], axis=1)
    y = b
    for j in range(CONV_W):
        y = y + xp[:, j:j + T] * w[j]
    return y, xp[:, T:]


def linear_scan(a, u, h0):
    def comb(l, r):
        return l[0] * r[0], r[0] * l[1] + r[1]
    a_cum, u_cum = lax.associative_scan(comb, (a, u), axis=1)
    hs = a_cum * h0[:, None, :] + u_cum
    return hs, hs[:, -1]


def rglru_block(h, h0, conv_buf, w_in, conv_w, conv_b, gate_w, gate_b, lam, w_out):
    B, T, _ = h.shape
    gate_branch, rec = jnp.split(h @ w_in, 2, axis=-1)
    xc, new_buf = causal_conv(rec, conv_buf, conv_w, conv_b)
    xf = xc.astype(jnp.float32)
    gates = jnp.einsum('btnc,ncg->btng', xf.reshape(B, T, LRU_BLOCKS, LRU_BLOCK_W), gate_w) + gate_b
    r = jax.nn.sigmoid(gates[..., :LRU_BLOCK_W]).reshape(B, T, D_RNN)
    i = jax.nn.sigmoid(gates[..., LRU_BLOCK_W:]).reshape(B, T, D_RNN)
    log_a = RG_C * r * jax.nn.log_sigmoid(lam.astype(jnp.float32))
    a = jnp.exp(log_a)
    u = jnp.sqrt(-jnp.expm1(2.0 * log_a)) * (i * xf)
    hs, h_last = linear_scan(a, u, h0.astype(jnp.float32))
    y = (hs.astype(h.dtype) * jax.nn.gelu(gate_branch)) @ w_out
    return y, h_last, new_buf


def wkv_scan(r, w, k, v, a, b, S0):
    def step(S, inp):
        r_t, w_t, k_t, v_t, a_t, b_t = inp
        sa = jnp.einsum('bhvk,bhk->bhv', S, a_t)
        S = S * w_t[:, :, None, :] + sa[..., None] * b_t[:, :, None, :] + v_t[..., None] * k_t[:, :, None, :]
        return S, jnp.einsum('bhvk,bhk->bhv', S, r_t)
    seq = tuple(jnp.moveaxis(t, 1, 0) for t in (r, w, k, v, a, b))
    S_T, ys = lax.scan(step, S0, seq)
    return jnp.moveaxis(ys, 0, 1), S_T


def rwkv7_time_mix(h, shift0, S0, mu, w_rkv, w_o, w0, w1, w2, a0, a1, a2, g1, g2,
                   k_k, k_a, r_k, ln_w, ln_b, v_first, vres):
    B, T, D = h.shape
    xf = h.astype(jnp.float32)
    x_prev = jnp.concatenate([shift0.astype(jnp.float32)[:, None, :], xf[:, :-1]], axis=1)
    xm = xf[None] + (x_prev - xf)[None] * mu[:, None, None, :]
    r, k, v = jnp.einsum('jbtd,jde->jbte', xm[:3], w_rkv)
    w_log = -jax.nn.softplus(-(w0 + jnp.tanh(xm[3] @ w1) @ w2)) - 0.5
    decay = jnp.exp(-jnp.exp(w_log))
    if vres is not None:
        v0, v1, v2 = vres
        v = v + (v_first - v) * jax.nn.sigmoid(v0 + (xm[2] @ v1) @ v2)
    a = jax.nn.sigmoid(a0 + (xm[4] @ a1) @ a2)
    g = jax.nn.sigmoid(xm[5] @ g1) @ g2
    kk = (k * k_k).reshape(B, T, RWKV_HEADS, HEAD_SIZE)
    kk = kk / jnp.maximum(jnp.sqrt(jnp.sum(kk * kk, axis=-1, keepdims=True)), 1e-12)
    k = k * (1.0 + (a - 1.0) * k_a)
    rh, wh, kh, vh, ah = (t.reshape(B, T, RWKV_HEADS, HEAD_SIZE) for t in (r, decay, k, v, a))
    ys, S_T = wkv_scan(rh, wh, kh, vh, -kk, kk * ah, S0.astype(jnp.float32))
    mean = jnp.mean(ys, axis=-1, keepdims=True)
    var = jnp.mean(jnp.square(ys - mean), axis=-1, keepdims=True)
    yn = ((ys - mean) * lax.rsqrt(var + GN_EPS)).reshape(B, T, D) * ln_w + ln_b
    bonus = jnp.sum(rh * kh * r_k, axis=-1, keepdims=True) * vh
    out = ((yn + bonus.reshape(B, T, D)) * g) @ w_o
    return out.astype(h.dtype), xf[:, -1], S_T, v


def trunk(x, c, lru_h0, lru_conv0, rwkv_shift0, rwkv_wkv0, P):
    B = x.shape[0]
    new_h, new_conv, new_shift, new_wkv = [], [], [], []
    v_first = None
    for layer in range(DEPTH):
        j = layer // N_MIXERS
        mod = (jax.nn.silu(c) @ P['ada_w'][layer] + P['ada_b'][layer]).reshape(B, N_MOD, D_MODEL)
        sh1, sc1, g1, sh2, sc2, g2, sh3, sc3, g3 = [mod[:, m, None, :] for m in range(N_MOD)]
        x = x + FFN_RES * (1.0 + g1) * swiglu(modulate(x, sh1, sc1), P['ffn_w_in'][layer, 0], P['ffn_w_out'][layer, 0])
        h = modulate(x, sh2, sc2)
        if layer % N_MIXERS == 0:
            y, h_last, buf = rglru_block(h, lru_h0[j], lru_conv0[j], P['lru_w_in'][j], P['lru_conv_w'][j],
                                         P['lru_conv_b'][j], P['lru_gate_w'][j], P['lru_gate_b'][j],
                                         P['lru_lambda'][j], P['lru_w_out'][j])
            new_h.append(h_last)
            new_conv.append(buf)
        else:
            vres = None if j == 0 else (P['rwkv_v0'][j - 1], P['rwkv_v1'][j - 1], P['rwkv_v2'][j - 1])
            y, shift_last, S_T, v = rwkv7_time_mix(
                h, rwkv_shift0[j], rwkv_wkv0[j], P['rwkv_mu'][j], P['rwkv_w_rkv'][j], P['rwkv_w_o'][j],
                P['rwkv_w0'][j], P['rwkv_w1'][j], P['rwkv_w2'][j], P['rwkv_a0'][j], P['rwkv_a1'][j],
                P['rwkv_a2'][j], P['rwkv_g1'][j], P['rwkv_g2'][j], P['rwkv_k_k'][j], P['rwkv_k_a'][j],
                P['rwkv_r_k'][j], P['rwkv_ln_w'][j], P['rwkv_ln_b'][j], v_first, vres)
            if v_first is None:
                v_first = v
            new_shift.append(shift_last)
            new_wkv.append(S_T)
        x = x + (1.0 + g2) * y
        x = x + FFN_RES * (1.0 + g3) * swiglu(modulate(x, sh3, sc3), P['ffn_w_in'][layer, 1], P['ffn_w_out'][layer, 1])
    y = rms_norm(x) * P['final_gain']
    return y, jnp.stack(new_h), jnp.stack(new_conv), jnp.stack(new_shift), jnp.stack(new_wkv)


def setup_inputs(seed: int = 0) -> dict:
    key = jax.random.key(seed)
    ks = iter(jax.random.split(key, 48))

    def nrm(shape, scale):
        return jax.random.normal(next(ks), shape, jnp.float32) * scale

    def unif(shape, lo, hi):
        return jax.random.uniform(next(ks), shape, jnp.float32, lo, hi)

    D = D_MODEL
    x_prompt = nrm((BATCH, SEQ, D), 1.0)
    x_sample = nrm((DEC_BATCH, DEC_SEQ, D), 1.0)
    state_lru_h = nrm((N_LRU, DEC_BATCH, D_RNN), 0.5)
    state_lru_conv = nrm((N_LRU, DEC_BATCH, CONV_W - 1, D_RNN), 1.0)
    state_rwkv_shift = nrm((N_RWKV, DEC_BATCH, D), 1.0)
    state_rwkv_wkv = nrm((N_RWKV, DEC_BATCH, RWKV_HEADS, HEAD_SIZE, HEAD_SIZE), 1.0)
    c_prompt = nrm((BATCH, D), 1.0)
    c_sample = nrm((DEC_BATCH, D), 1.0)
    ada_w = nrm((DEPTH, D, N_MOD * D), 0.2 * D ** -0.5)
    ada_b = nrm((DEPTH, N_MOD * D), 0.02)
    ffn_w_in = nrm((DEPTH, 2, D, 2 * D_FF), D ** -0.5)
    ffn_w_out = nrm((DEPTH, 2, D_FF, D), D_FF ** -0.5)
    lru_w_in = nrm((N_LRU, D, 2 * D_RNN), D ** -0.5)
    lru_conv_w = nrm((N_LRU, CONV_W, D_RNN), CONV_W ** -0.5)
    lru_conv_b = nrm((N_LRU, D_RNN), 0.02)
    lru_gate_w = nrm((N_LRU, LRU_BLOCKS, LRU_BLOCK_W, 2 * LRU_BLOCK_W), LRU_BLOCK_W ** -0.5)
    lru_gate_b = nrm((N_LRU, LRU_BLOCKS, 2 * LRU_BLOCK_W), 0.02)
    a_target = unif((N_LRU, D_RNN), 0.9, 0.999)
    p = a_target ** (1.0 / RG_C)
    lru_lambda = jnp.log(p) - jnp.log1p(-p)
    lru_w_out = nrm((N_LRU, D_RNN, D), D_RNN ** -0.5)
    rwkv_mu = unif((N_RWKV, 6, D), 0.0, 1.0)
    rwkv_w_rkv = nrm((N_RWKV, 3, D, D), D ** -0.5)
    rwkv_w_o = nrm((N_RWKV, D, D), D ** -0.5)
    rwkv_w0 = unif((N_RWKV, D), -6.0, 1.0)
    rwkv_w1 = nrm((N_RWKV, D, D_DECAY_LORA), D ** -0.5)
    rwkv_w2 = nrm((N_RWKV, D_DECAY_LORA, D), 0.5 * D_DECAY_LORA ** -0.5)
    rwkv_a0 = nrm((N_RWKV, D), 0.1)
    rwkv_a1 = nrm((N_RWKV, D, D_AAA_LORA), D ** -0.5)
    rwkv_a2 = nrm((N_RWKV, D_AAA_LORA, D), 0.5 * D_AAA_LORA ** -0.5)
    rwkv_v0 = nrm((N_RWKV - 1, D), 0.1)
    rwkv_v1 = nrm((N_RWKV - 1, D, D_MV_LORA), D ** -0.5)
    rwkv_v2 = nrm((N_RWKV - 1, D_MV_LORA, D), 0.5 * D_MV_LORA ** -0.5)
    rwkv_g1 = nrm((N_RWKV, D, D_GATE_LORA), D ** -0.5)
    rwkv_g2 = nrm((N_RWKV, D_GATE_LORA, D), D_GATE_LORA ** -0.5)
    rwkv_k_k = 0.85 + nrm((N_RWKV, D), 0.05)
    rwkv_k_a = 1.0 + nrm((N_RWKV, D), 0.05)
    rwkv_r_k = nrm((N_RWKV, RWKV_HEADS, HEAD_SIZE), 0.1)
    rwkv_ln_w = 1.0 + nrm((N_RWKV, D), 0.05)
    rwkv_ln_b = nrm((N_RWKV, D), 0.02)
    final_gain = 1.0 + nrm((D,), 0.02)
    return {'x_prompt': x_prompt, 'x_sample': x_sample,
            'state_lru_h': state_lru_h, 'state_lru_conv': state_lru_conv,
            'state_rwkv_shift': state_rwkv_shift, 'state_rwkv_wkv': state_rwkv_wkv,
            'c_prompt': c_prompt, 'c_sample': c_sample,
            'ada_w': ada_w, 'ada_b': ada_b, 'ffn_w_in': ffn_w_in, 'ffn_w_out': ffn_w_out,
            'lru_w_in': lru_w_in, 'lru_conv_w': lru_conv_w, 'lru_conv_b': lru_conv_b,
            'lru_gate_w': lru_gate_w, 'lru_gate_b': lru_gate_b, 'lru_lambda': lru_lambda, 'lru_w_out': lru_w_out,
            'rwkv_mu': rwkv_mu, 'rwkv_w_rkv': rwkv_w_rkv, 'rwkv_w_o': rwkv_w_o,
            'rwkv_w0': rwkv_w0, 'rwkv_w1': rwkv_w1, 'rwkv_w2': rwkv_w2,
            'rwkv_a0': rwkv_a0, 'rwkv_a1': rwkv_a1, 'rwkv_a2': rwkv_a2,
            'rwkv_v0': rwkv_v0, 'rwkv_v1': rwkv_v1, 'rwkv_v2': rwkv_v2,
            'rwkv_g1': rwkv_g1, 'rwkv_g2': rwkv_g2, 'rwkv_k_k': rwkv_k_k, 'rwkv_k_a': rwkv_k_a,
            'rwkv_r_k': rwkv_r_k, 'rwkv_ln_w': rwkv_ln_w, 'rwkv_ln_b': rwkv_ln_b,
            'final_gain': final_gain}


def reference(x_prompt, x_sample, state_lru_h, state_lru_conv, state_rwkv_shift, state_rwkv_wkv,
              c_prompt, c_sample, ada_w, ada_b, ffn_w_in, ffn_w_out,
              lru_w_in, lru_conv_w, lru_conv_b, lru_gate_w, lru_gate_b, lru_lambda, lru_w_out,
              rwkv_mu, rwkv_w_rkv, rwkv_w_o, rwkv_w0, rwkv_w1, rwkv_w2, rwkv_a0, rwkv_a1, rwkv_a2,
              rwkv_v0, rwkv_v1, rwkv_v2, rwkv_g1, rwkv_g2, rwkv_k_k, rwkv_k_a, rwkv_r_k,
              rwkv_ln_w, rwkv_ln_b, final_gain):
    P = dict(ada_w=ada_w, ada_b=ada_b, ffn_w_in=ffn_w_in, ffn_w_out=ffn_w_out,
             lru_w_in=lru_w_in, lru_conv_w=lru_conv_w, lru_conv_b=lru_conv_b, lru_gate_w=lru_gate_w,
             lru_gate_b=lru_gate_b, lru_lambda=lru_lambda, lru_w_out=lru_w_out,
             rwkv_mu=rwkv_mu, rwkv_w_rkv=rwkv_w_rkv, rwkv_w_o=rwkv_w_o, rwkv_w0=rwkv_w0, rwkv_w1=rwkv_w1,
             rwkv_w2=rwkv_w2, rwkv_a0=rwkv_a0, rwkv_a1=rwkv_a1, rwkv_a2=rwkv_a2, rwkv_v0=rwkv_v0,
             rwkv_v1=rwkv_v1, rwkv_v2=rwkv_v2, rwkv_g1=rwkv_g1, rwkv_g2=rwkv_g2, rwkv_k_k=rwkv_k_k,
             rwkv_k_a=rwkv_k_a, rwkv_r_k=rwkv_r_k, rwkv_ln_w=rwkv_ln_w, rwkv_ln_b=rwkv_ln_b,
             final_gain=final_gain)
    B = x_prompt.shape[0]
    h0 = jnp.zeros((N_LRU, B, D_RNN), jnp.float32)
    conv0 = jnp.zeros((N_LRU, B, CONV_W - 1, D_RNN), jnp.float32)
    shift0 = jnp.zeros((N_RWKV, B, D_MODEL), jnp.float32)
    wkv0 = jnp.zeros((N_RWKV, B, RWKV_HEADS, HEAD_SIZE, HEAD_SIZE), jnp.float32)
    y_prompt, p_lru_h, p_lru_conv, p_rwkv_shift, p_rwkv_wkv = trunk(x_prompt, c_prompt, h0, conv0, shift0, wkv0, P)
    y_sample, s_lru_h, s_lru_conv, s_rwkv_shift, s_rwkv_wkv = trunk(
        x_sample, c_sample, state_lru_h, state_lru_conv, state_rwkv_shift, state_rwkv_wkv, P)
    return (y_prompt, y_sample, p_lru_h, p_lru_conv, p_rwkv_shift, p_rwkv_wkv,
            s_lru_h, s_lru_conv, s_rwkv_shift, s_rwkv_wkv)
```

```python
import numpy as np
from contextlib import ExitStack
import concourse.bass as bass
import concourse.mybir as mybir
from concourse.bass_utils import run_bass_kernel_spmd

F32 = mybir.dt.float32
BF16 = mybir.dt.bfloat16
AF = mybir.ActivationFunctionType
ALU = mybir.AluOpType

D = 1024; KC = 8; DFF = 2816; NS = 16; TS = 8; NSEQ = 17; NCORE = 8
NDS = 40
UID = [0]
SEMLIM = 30000


class Tl:
    __slots__ = ("name", "w", "r")

    def __init__(s, name=""):
        s.name = name; s.w = None; s.r = {}


class Sch:
    def __init__(s, nc, st):
        s.nc = nc; s.st = st
        s.E = {"pe": nc.tensor, "act": nc.scalar, "dve": nc.vector, "pool": nc.gpsimd, "sp": nc.sync}
        s.sems = {e: [st.enter_context(nc.semaphore("c_%s0" % e))] for e in s.E}
        s.cnt = {e: 0 for e in s.E}
        s.pend = {e: [] for e in s.E}
        s.known = {e: {} for e in s.E}
        s.dsem = [st.enter_context(nc.semaphore("dq%d" % i)) for i in range(NDS)]
        s.dcnt = [0] * NDS
        s.dnx = {"pool": 0, "sp": NDS // 2, "act": NDS // 2}
        s.nins = 0

    def _wait(s, e, deps):
        need = {}
        for tok in deps:
            if tok[2] == "pe" and e == "pe":
                continue
            assert tok[1] is not None, "dependency on unsignaled instruction"
            k = id(tok[0])
            if k not in need or need[k][1] < tok[1]:
                need[k] = (tok[0], tok[1])
        for k, (sem, val) in need.items():
            if s.known[e].get(k, 0) >= val:
                continue
            s.E[e].wait_ge(sem, val)
            s.known[e][k] = val

    @staticmethod
    def _deps(R, W):
        deps = []
        for t in R:
            if t.w is not None:
                deps.append(t.w)
        for t in W:
            if t.w is not None:
                deps.append(t.w)
            deps.extend(t.r.values())
        return deps

    def op(s, e, fn, R=(), W=(), sig=True):
        s._wait(e, s._deps(R, W))
        ins = fn(s.E[e])
        s.nins += 1
        tok = [None, None, e]
        s.pend[e].append(tok)
        if sig:
            if s.cnt[e] >= SEMLIM:
                s.sems[e].append(s.st.enter_context(s.nc.semaphore("c_%s%d" % (e, len(s.sems[e])))))
                s.cnt[e] = 0
            s.cnt[e] += 1
            sem = s.sems[e][-1]
            ins.then_inc(sem, 1)
            for p in s.pend[e]:
                p[0] = sem; p[1] = s.cnt[e]
            s.pend[e] = []
        for t in W:
            t.w = tok; t.r = {}
        for t in R:
            t.r[e] = tok
        return ins

    def dma(s, q, out, in_, R=(), W=()):
        lo, hi = (0, NDS // 2) if q == "pool" else (NDS // 2, NDS)
        i = s.dnx[q]; s.dnx[q] = lo + (i + 1 - lo) % (hi - lo)
        deps = s._deps(R, W)
        if s.dcnt[i] > 0:
            deps.append([s.dsem[i], s.dcnt[i], "dma"])
        s._wait(q, deps)
        s.dcnt[i] += 16
        s.E[q].dma_start(out=out, in_=in_).then_inc(s.dsem[i], 16)
        s.nins += 1
        tok = [s.dsem[i], s.dcnt[i], "dma"]
        for t in W:
            t.w = tok; t.r = {}
        for t in R:
            t.r[("d", i)] = tok

    def dma_group(s, q, pairs, R=(), W=()):
        lo, hi = (0, NDS // 2) if q == "pool" else (NDS // 2, NDS)
        i = s.dnx[q]; s.dnx[q] = lo + (i + 1 - lo) % (hi - lo)
        deps = s._deps(R, W)
        if s.dcnt[i] > 0:
            deps.append([s.dsem[i], s.dcnt[i], "dma"])
        s._wait(q, deps)
        for out, in_ in pairs:
            s.dcnt[i] += 16
            s.E[q].dma_start(out=out, in_=in_).then_inc(s.dsem[i], 16)
            s.nins += 1
        tok = [s.dsem[i], s.dcnt[i], "dma"]
        for t in W:
            t.w = tok; t.r = {}
        for t in R:
            t.r[("d", i)] = tok

    def barrier(s):
        e = "sp"
        for o in s.E:
            assert not s.pend[o], "barrier with unsignaled instructions on " + o
        for i in range(NDS):
            if s.dcnt[i] > 0 and s.known[e].get(id(s.dsem[i]), 0) < s.dcnt[i]:
                s.E[e].wait_ge(s.dsem[i], s.dcnt[i])
        for o in s.E:
            if o != e and s.cnt[o] > 0 and s.known[e].get(id(s.sems[o][-1]), 0) < s.cnt[o]:
                s.E[e].wait_ge(s.sems[o][-1], s.cnt[o])
        s.cnt[e] += 1
        s.E[e].sem_inc(s.sems[e][-1], 1)
        for o in s.E:
            if o != e:
                s.E[o].wait_ge(s.sems[e][-1], s.cnt[e])
            for x in s.E:
                s.known[o][id(s.sems[x][-1])] = s.cnt[x]
            for i in range(NDS):
                s.known[o][id(s.dsem[i])] = s.dcnt[i]

    def drain(s, e="sp"):
        for i in range(NDS):
            if s.dcnt[i] > 0 and s.known[e].get(id(s.dsem[i]), 0) < s.dcnt[i]:
                s.E[e].wait_ge(s.dsem[i], s.dcnt[i])
        for o in s.E:
            if o != e and s.cnt[o] > 0:
                s.E[e].wait_ge(s.sems[o][-1], s.cnt[o])


VEC_SPEC = [("ada_b", 4 * 72), ("lru_conv_w", 2 * 4 * 8), ("lru_conv_b", 16), ("lru_lambda", 16),
            ("lru_gate_b", 2 * 16), ("rwkv_mu", 2 * 6 * 8), ("rwkv_w0", 16), ("rwkv_a0", 16), ("rwkv_v0", 8),
            ("rwkv_k_k", 16), ("rwkv_k_a", 16), ("rwkv_r_k", 16), ("rwkv_ln_w", 16), ("rwkv_ln_b", 16),
            ("final_gain", 8)]
VOFF = {}
_o = 0
for _n, _r in VEC_SPEC:
    VOFF[_n] = _o; _o += _r
NVEC = _o
NVT = (NVEC + 127) // 128
XO_C8 = NVT * 128
XO_OMMU = XO_C8 + 16
XO_OMKA = XO_OMMU + 96
NVCOL = XO_OMKA + 16

BIG_W = ["ada_w", "ffn_w_in", "ffn_w_out", "lru_w_in", "lru_gate_w", "lru_w_out", "rwkv_w_rkv", "rwkv_w_o",
         "rwkv_w1", "rwkv_w2", "rwkv_a1", "rwkv_a2", "rwkv_v1", "rwkv_v2", "rwkv_g1", "rwkv_g2"]
W_SHAPES = {"ada_w": [4, 1024, 9216], "ffn_w_in": [4, 2, 1024, 5632], "ffn_w_out": [4, 2, 2816, 1024],
            "lru_w_in": [2, 1024, 2048], "lru_gate_w": [2, 4, 256, 512], "lru_w_out": [2, 1024, 1024],
            "rwkv_w_rkv": [2, 3, 1024, 1024], "rwkv_w_o": [2, 1024, 1024], "rwkv_w1": [2, 1024, 64],
            "rwkv_w2": [2, 64, 1024], "rwkv_a1": [2, 1024, 64], "rwkv_a2": [2, 64, 1024],
            "rwkv_v1": [1, 1024, 32], "rwkv_v2": [1, 32, 1024], "rwkv_g1": [2, 1024, 160], "rwkv_g2": [2, 160, 1024]}


def build(TP=2048, DEPTH=4, do_lru=True, do_rwkv=True):
    T = TP + NS * TS
    nc = bass.Bass("TRN2", target_bir_lowering=False)
    A = {}

    def din(name, shape):
        A[name] = nc.dram_tensor(name, list(shape), F32, kind="ExternalInput").ap()

    def dout(name, shape):
        A[name] = nc.dram_tensor(name, list(shape), F32, kind="ExternalOutput").ap()

    din("c_ident", [128, 128]); din("c_mask", [128, 4, 64]); din("c_cm", [128, 640])
    din("xp", [TP, D]); din("xs", [NS * TS, D]); din("cc", [NSEQ, D])
    din("s_lru_h", [2, NS, D]); din("s_lru_conv", [2, NS * 3, D]); din("s_shift", [2, NS, D])
    din("s_wkv", [2, NS, 16, 64, 64])
    for n, r in VEC_SPEC:
        din(n, [r, 128])
    for n in BIG_W:
        din(n, W_SHAPES[n])
    dout("yp", [TP, D]); dout("ys", [NS * TS, D])
    dout("o_p_small", [10, D])
    dout("o_s_small", [10, NS, D])
    dout("o_p_wkv", [2, 16, 64, 64]); dout("o_s_wkv", [2, NS, 16, 64, 64])

    SCR = [{nm: nc.dram_tensor("scr_%d_%s" % (jl, nm), [KC, 128, T], F32).ap() for nm in ("r", "k", "v", "sw", "a", "g", "sv")}
           for jl in range(2)]
    with ExitStack() as st:
        S = Sch(nc, st)

        def sb(name, shape, dt=F32):
            return st.enter_context(nc.sbuf_tensor(name, list(shape), dt))

        x = sb("x", [128, KC, T]); Tx = [[Tl("x%d_%d" % (c, i)) for i in range(8)] for c in range(KC)]
        vecs = sb("vecs", [128, NVCOL]); Tvec = Tl("vecs")
        modTs = [sb("modT0", [128, 72, NSEQ]), sb("modT1", [128, 72, NSEQ])]; Tmods = [Tl("mod0"), Tl("mod1")]
        modT = modTs[0]; Tmod = Tmods[0]
        scT = sb("scT", [128, KC, NSEQ], BF16); TscT = Tl("scT")
        ones_bf = sb("ones_bf", [128, 128], BF16)
        identf = sb("identf", [128, 128]); identb = sb("identb", [128, 128], BF16)
        blk64 = sb("blk64", [128, 128])
        blk1 = sb("blk1", [128, 128])
        ost = sb("ost", [128, KC, 10 * NSEQ]); Tost = Tl("ost")
        Tconst = Tl("const")
        msk = sb("msk", [128, 4, 64]); cmk = sb("cmk", [128, 640])
        ps = [st.enter_context(nc.psum_tensor("ps%d" % i, [128, 512], F32)) for i in range(7)]
        psb = st.enter_context(nc.psum_tensor("psb", [128, 1024], BF16))
        Tps = [Tl("ps%d" % i) for i in range(7)]; Tpsb = Tl("psb")

        tiles = []
        c0 = 0
        while c0 < TP:
            w = min(512, TP - c0); tiles.append((c0, w, "p")); c0 += w
        tiles.append((TP, NS * TS, "s"))
        NT = len(tiles)

        def seqb(ap_n17, kind, w):
            if kind == "p":
                return ap_n17[:, 0:1].to_broadcast([128, w])
            return ap_n17[:, 1:NSEQ].unsqueeze(2).to_broadcast([128, NS, TS])

        def tv(ap2d, kind):
            if kind == "p":
                return ap2d
            return ap2d.rearrange("p (s t) -> p s t", t=TS)

        S.op("dve", lambda e: e.memset(ones_bf[:], 1.0 / 1024.0), W=[Tconst])
        S.op("dve", lambda e: e.memset(blk64[:], 0.0), W=[Tconst])
        S.op("dve", lambda e: e.memset(blk1[:], 0.0), W=[Tconst])
        for p in range(2):
            S.op("dve", lambda e, p=p: e.memset(blk64[p * 64:(p + 1) * 64, p * 64:(p + 1) * 64], 1.0 / 64.0), W=[Tconst])
            S.op("dve", lambda e, p=p: e.memset(blk1[p * 64:(p + 1) * 64, p * 64:(p + 1) * 64], 1.0), W=[Tconst])
        S.dma("sp", identf[:, :], A["c_ident"][:, :], R=[], W=[Tconst])
        S.dma("sp", msk[:, :, :], A["c_mask"][:, :, :], R=[], W=[Tconst])
        S.dma("sp", cmk[:, :], A["c_cm"][:, :], R=[], W=[Tconst])
        S.op("dve", lambda e: e.tensor_copy(identb[:], identf[:]), R=[Tconst], W=[Tconst])
        for i in range(7):
            S.op("dve", lambda e, i=i: e.memset(ps[i][:], 0.0), W=[Tps[i]])
        S.op("dve", lambda e: e.memset(psb[:].bitcast(F32), 0.0), W=[Tpsb])
        S.op("dve", lambda e: e.memset(ost[:], 0.0), W=[Tost])

        with nc.sbuf_tensor("vstage", [128, NVT, 128], F32) as vstage, nc.sbuf_tensor("cst", [NSEQ, D], F32) as cst:
            Tvs = Tl("vstage"); Tcst = Tl("cst")
            S.op("dve", lambda e: e.memset(vstage[:], 0.0), W=[Tvs])
            vpairs = []
            for n, r in VEC_SPEC:
                g0 = VOFF[n]; done = 0
                while done < r:
                    tix = (g0 + done) // 128; p0 = (g0 + done) % 128
                    m = min(r - done, 128 - p0)
                    vpairs.append((vstage[p0:p0 + m, tix, :], A[n][done:done + m, :]))
                    done += m
            S.dma_group("sp", vpairs, W=[Tvs])
            for tix in range(NVT):
                S.op("pe", lambda e, tix=tix: e.transpose(ps[0][:, 0:128], vstage[:, tix, :], identf[:]),
                     R=[Tvs, Tconst], W=[Tps[0]])
                S.op("act", lambda e, tix=tix: e.activation(out=vecs[:, tix * 128:(tix + 1) * 128], in_=ps[0][:, 0:128],
                                                            func=AF.Copy), R=[Tps[0]], W=[Tvec])
            lam = vecs[:, VOFF["lru_lambda"]:VOFF["lru_lambda"] + 16]
            c8 = vecs[:, XO_C8:XO_C8 + 16]
            S.op("act", lambda e: e.activation(out=c8, in_=lam, func=AF.Exp, scale=-1.0), R=[Tvec], W=[Tvec])
            S.op("act", lambda e: e.activation(out=c8, in_=c8, func=AF.Ln, bias=1.0), R=[Tvec], W=[Tvec])
            S.op("dve", lambda e: e.tensor_scalar(c8, c8, -8.0, None, ALU.mult), R=[Tvec], W=[Tvec])
            mu = vecs[:, VOFF["rwkv_mu"]:VOFF["rwkv_mu"] + 96]
            S.op("dve", lambda e: e.tensor_scalar(vecs[:, XO_OMMU:XO_OMMU + 96], mu, -1.0, 1.0, ALU.mult, ALU.add),
                 R=[Tvec], W=[Tvec])
            ka = vecs[:, VOFF["rwkv_k_a"]:VOFF["rwkv_k_a"] + 16]
            S.op("dve", lambda e: e.tensor_scalar(vecs[:, XO_OMKA:XO_OMKA + 16], ka, -1.0, 1.0, ALU.mult, ALU.add),
                 R=[Tvec], W=[Tvec])
            S.dma("sp", cst[:, :], A["cc"][:, :], W=[Tcst])
            S.op("act", lambda e: e.activation(out=cst[:, :], in_=cst[:, :], func=AF.Silu), R=[Tcst], W=[Tcst])
            for c in range(KC):
                S.op("pe", lambda e, c=c: e.transpose(ps[1][:, c * NSEQ:(c + 1) * NSEQ], cst[:, c * 128:(c + 1) * 128],
                                                      identf[0:NSEQ, 0:NSEQ]), R=[Tcst, Tconst], W=[Tps[1]], sig=(c == KC - 1))
            S.op("act", lambda e: e.activation(out=scT[:].rearrange("p c s -> p (c s)"), in_=ps[1][:, 0:KC * NSEQ],
                                               func=AF.Copy), R=[Tps[1]], W=[TscT])

            with nc.sbuf_tensor("xin0", [128, D], F32) as xin0, nc.sbuf_tensor("xin1", [128, D], F32) as xin1:
                xin = [xin0, xin1]; Txin = [Tl("xin0"), Tl("xin1")]
                nblk = T // 128
                for b in range(nblk):
                    t0 = b * 128
                    src = A["xp"][t0:t0 + 128, :] if t0 < TP else A["xs"][:, :]
                    S.dma("sp", xin[b % 2][:, :], src, W=[Txin[b % 2]])
                    ti = min(t0 // 512, NT - 1) if t0 < TP else NT - 1
                    for half in range(2):
                        pb = half
                        for q in range(4):
                            c = half * 4 + q
                            S.op("pe", lambda e, c=c, q=q, b=b, pb=pb: e.transpose(ps[pb][:, q * 128:(q + 1) * 128],
                                 xin[b % 2][:, c * 128:(c + 1) * 128], identf[:]), R=[Txin[b % 2], Tconst], W=[Tps[pb]],
                                 sig=(q == 3))
                        S.op("act" if half == 0 else "dve",
                             (lambda e, half=half, t0=t0, pb=pb: e.activation(out=x[:, half * 4:half * 4 + 4, t0:t0 + 128],
                              in_=ps[pb][:].rearrange("p (q t) -> p q t", t=128), func=AF.Copy)) if half == 0 else
                             (lambda e, half=half, t0=t0, pb=pb: e.tensor_copy(x[:, half * 4:half * 4 + 4, t0:t0 + 128],
                              ps[pb][:].rearrange("p (q t) -> p q t", t=128))),
                             R=[Tps[pb]], W=[Tx[c][ti] for c in range(half * 4, half * 4 + 4)])

        def norm_stats(ti, sq, Tsq, rstd, Trstd):
            c0, w, kind = tiles[ti]
            for c in range(KC):
                S.op("act", lambda e, c=c: e.activation(out=sq[:, c % 2, 0:w], in_=x[:, c, c0:c0 + w], func=AF.Square),
                     R=[Tx[c][ti]], W=[Tsq[c % 2]])
                S.op("pe", lambda e, c=c: e.matmul(ps[6][:, 0:w], ones_bf[:], sq[:, c % 2, 0:w], start=(c == 0), stop=(c == KC - 1)),
                     R=[Tsq[c % 2], Tconst], W=[Tps[6]], sig=True)
            S.op("act", lambda e: e.activation(out=rstd[:, 0:w], in_=ps[6][:, 0:w], func=AF.Ln, bias=1e-6, scale=1.0),
                 R=[Tps[6]], W=[Trstd])
            S.op("act", lambda e: e.activation(out=rstd[:, 0:w], in_=rstd[:, 0:w], func=AF.Exp, scale=-0.5), R=[Trstd], W=[Trstd])

        def modulate(ti, m_shift, m_scale, rstd, Trstd, tmp2, Ttmp2, out_fn, Wout, extra=None):
            c0, w, kind = tiles[ti]
            for c in range(KC):
                tmp = tmp2[:, c % 2, :]; Ttmp = Ttmp2[c % 2]
                S.op("dve", lambda e, c=c, tmp=tmp: e.tensor_tensor(tmp[:, 0:w], x[:, c, c0:c0 + w], rstd[:, 0:w], ALU.mult),
                     R=[Tx[c][ti], Trstd], W=[Ttmp])
                if kind == "p":
                    S.op("act", lambda e, c=c, tmp=tmp: e.activation(out=out_fn(c), in_=tmp[:, 0:w], func=AF.Identity,
                                                            scale=modT[:, m_scale * 8 + c, 0:1], bias=modT[:, m_shift * 8 + c, 0:1]),
                         R=[Ttmp, Tmod], W=Wout(c))
                else:
                    S.op("dve", lambda e, c=c, tmp=tmp: e.tensor_tensor(tv(tmp[:, 0:w], kind), tv(tmp[:, 0:w], kind),
                                                               seqb(modT[:, m_scale * 8 + c, :], kind, w), ALU.mult),
                         R=[Ttmp, Tmod], W=[Ttmp])
                    S.op("dve", lambda e, c=c, tmp=tmp: e.tensor_tensor(out_fn(c) if len(out_fn(c).shape) == 3 else tv(out_fn(c), kind),
                                                               tv(tmp[:, 0:w], kind),
                                                               seqb(modT[:, m_shift * 8 + c, :], kind, w), ALU.add),
                         R=[Ttmp, Tmod], W=Wout(c))
                if extra is not None:
                    extra(c, tmp, Ttmp)

        def norm_mod_all(m_shift, m_scale, sq, Tsq, rstd2, Trstd2, tmp2, Ttmp2, out_fn_t, Wout_t, extra_t=None):
            norm_stats(0, sq, Tsq, rstd2[:, 0, :], Trstd2[0])
            for ti in range(NT):
                if ti + 1 < NT:
                    norm_stats(ti + 1, sq, Tsq, rstd2[:, (ti + 1) % 2, :], Trstd2[(ti + 1) % 2])
                modulate(ti, m_shift, m_scale, rstd2[:, ti % 2, :], Trstd2[ti % 2], tmp2, Ttmp2, out_fn_t(ti), Wout_t(ti),
                         extra=(extra_t(ti) if extra_t is not None else None))

        def resid_update(ti, dc, psum_ap, Tp, m_gate, tmp, Ttmp, sub=None):
            c0, w, kind = tiles[ti] if sub is None else sub
            if kind == "p":
                S.op("dve", lambda e: e.scalar_tensor_tensor(x[:, dc, c0:c0 + w], psum_ap, modT[:, m_gate * 8 + dc, 0:1],
                                                             x[:, dc, c0:c0 + w], ALU.mult, ALU.add),
                     R=[Tp, Tmod, Tx[dc][ti]], W=[Tx[dc][ti]])
            else:
                S.op("dve", lambda e: e.tensor_tensor(tv(tmp[:, 0:w], kind), tv(psum_ap, kind),
                                                      seqb(modT[:, m_gate * 8 + dc, :], kind, w), ALU.mult),
                     R=[Tp, Tmod], W=[Ttmp])
                S.op("dve", lambda e: e.tensor_tensor(x[:, dc, c0:c0 + w], x[:, dc, c0:c0 + w], tmp[:, 0:w], ALU.add),
                     R=[Ttmp, Tx[dc][ti]], W=[Tx[dc][ti]])

        def gen_mod(layer, mw, Tmw, mT, TmT):
            wv = A["ada_w"][layer].rearrange("(k p) n -> p k n", p=128)
            NPC = 9216 // 256
            bank = 6
            S.dma("pool", mw[:, 0, :, :], wv[:, :, 0:256], W=[Tmw[0]])
            yield
            for pc in range(NPC):
                sl = pc % 2
                if pc + 1 < NPC:
                    S.dma("pool", mw[:, 1 - sl, :, :], wv[:, :, (pc + 1) * 256:(pc + 2) * 256], W=[Tmw[1 - sl]])
                for q in range(2):
                    n = pc * 2 + q
                    m = n // 8; nn = n % 8
                    for k in range(KC):
                        S.op("pe", lambda e, k=k, q=q, sl=sl, nn=nn: e.matmul(
                            ps[bank][:, nn * NSEQ:(nn + 1) * NSEQ], mw[:, sl, k, q * 128:(q + 1) * 128], scT[:, k, :],
                            start=(k == 0), stop=(k == KC - 1)), R=[Tmw[sl], TscT], W=[Tps[bank]], sig=(k == KC - 1))
                    if nn == 7:
                        bcol = VOFF["ada_b"] + layer * 72 + m * 8
                        S.op("dve", lambda e, m=m, bcol=bcol: e.tensor_tensor(
                            mT[:, m * 8:(m + 1) * 8, :], ps[bank][:, 0:8 * NSEQ].rearrange("p (n s) -> p n s", s=NSEQ),
                            vecs[:, bcol:bcol + 8].unsqueeze(2).to_broadcast([128, 8, NSEQ]), ALU.add),
                            R=[Tps[bank], Tvec, TmT], W=[TmT])
                        sl8 = mT[:, m * 8:(m + 1) * 8, :]
                        if m in (1, 4, 5, 7):
                            S.op("dve", lambda e, sl8=sl8: e.tensor_scalar(sl8, sl8, 1.0, None, ALU.add), R=[TmT], W=[TmT])
                        elif m in (2, 8):
                            S.op("dve", lambda e, sl8=sl8: e.tensor_scalar(sl8, sl8, 0.5, 0.5, ALU.mult, ALU.add),
                                 R=[TmT], W=[TmT])
                yield

        def phase_mod(layer):
            with nc.sbuf_tensor("mw%d" % layer, [128, 2, KC, 256], BF16) as mw:
                Tmw = [Tl("mw%d" % i) for i in range(3)]
                S.barrier()
                for _ in gen_mod(layer, mw, Tmw, modTs[layer % 2], Tmods[layer % 2]):
                    pass

        def phase_ffn(layer, which, co_layer=None):
            m0 = 0 if which == 0 else 6
            GP = 2
            with ExitStack() as ph:
                def pb_(name, shape, dt=F32):
                    UID[0] += 1; return ph.enter_context(nc.sbuf_tensor("%s_%d" % (name, UID[0]), list(shape), dt))
                h = pb_("f_h", [128, KC, T], BF16); Th = [Tl("h%d" % i) for i in range(NT)]
                act = pb_("f_act", [128, 2 * GP, T], BF16); Tact = [[Tl("a") for _ in range(NT)] for _ in range(2 * GP)]
                sq = pb_("f_sq", [128, 2, 512], BF16); Tsq = [Tl("sq0"), Tl("sq1")]
                rstd2 = pb_("f_rstd", [128, 2, 512]); Trstd2 = [Tl("rstd0"), Tl("rstd1")]
                tmp2 = pb_("f_tmp", [128, 2, 512]); Ttmp2 = [Tl("tmp0"), Tl("tmp1")]
                tmp = tmp2[:, 0, :]; Ttmp = Ttmp2[0]
                sg = pb_("f_sg", [128, 2, 512]); Tsg = [Tl("sg0"), Tl("sg1")]
                win = pb_("f_win", [128, 3, KC, 2, 256], BF16); Twin = [Tl("win%d" % i) for i in range(3)]
                wout = pb_("f_wout", [128, 2 * GP, 2, D], BF16); Twout = [Tl("wout%d" % i) for i in range(2 * GP)]
                co = None
                if co_layer is not None:
                    mw = pb_("f_mw", [128, 2, KC, 256], BF16)
                    co = gen_mod(co_layer, mw, [Tl("mw%d" % i) for i in range(3)], modTs[co_layer % 2], Tmods[co_layer % 2])
                S.barrier()

                co_n = [0]

                def co_step():
                    nonlocal co
                    for _ in range(2):
                        if co is not None:
                            try:
                                next(co)
                            except StopIteration:
                                co = None
                norm_mod_all(m0, m0 + 1, sq, Tsq, rstd2, Trstd2, tmp2, Ttmp2,
                             lambda ti: (lambda c, c0=tiles[ti][0], w=tiles[ti][1]: h[:, c, c0:c0 + w]),
                             lambda ti: (lambda c, ti=ti: [Th[ti]]))
                wi = A["ffn_w_in"][layer, which].rearrange("(k p) n -> p k n", p=128)
                wo = A["ffn_w_out"][layer, which].rearrange("(f p) n -> p f n", p=128)
                NPC = 11
                pcs = list(range(NPC))
                groups = [pcs[i:i + GP] for i in range(0, NPC, GP)]
                evq = 0
                for gi, grp in enumerate(groups):
                    for li, pc in enumerate(grp):
                        sl = pc % 3
                        S.dma_group("pool", [(win[:, sl, :, 0, :], wi[:, :, pc * 256:(pc + 1) * 256]),
                                             (win[:, sl, :, 1, :], wi[:, :, DFF + pc * 256:DFF + (pc + 1) * 256])], W=[Twin[sl]])
                        so = (gi % 2) * GP + li
                        S.dma("pool", wout[:, so, :, :], wo[:, pc * 2:pc * 2 + 2, :], W=[Twout[so]])
                        for q in range(2):
                            fi = li * 2 + q
                            for ti in range(NT):
                                c0, w, kind = tiles[ti]
                                bg = (evq % 2) * 2; evq += 1
                                for gu in range(2):
                                    for k in range(KC):
                                        S.op("pe", lambda e, k=k, gu=gu, sl=sl, q=q, bg=bg, c0=c0, w=w: e.matmul(
                                            ps[bg + gu][:, 0:w], win[:, sl, k, gu, q * 128:(q + 1) * 128], h[:, k, c0:c0 + w],
                                            start=(k == 0), stop=(k == KC - 1)), R=[Twin[sl], Th[ti]], W=[Tps[bg + gu]],
                                            sig=(k == KC - 1))
                                sgi = (bg // 2)
                                S.op("act", lambda e, bg=bg, w=w, sgi=sgi: e.activation(out=sg[:, sgi, 0:w], in_=ps[bg][:, 0:w],
                                                                                      func=AF.Silu), R=[Tps[bg]], W=[Tsg[sgi]])
                                S.op("dve", lambda e, bg=bg, w=w, sgi=sgi, fi=fi, c0=c0: e.tensor_tensor(
                                    act[:, fi, c0:c0 + w], sg[:, sgi, 0:w], ps[bg + 1][:, 0:w], ALU.mult),
                                    R=[Tsg[sgi], Tps[bg + 1]], W=[Tact[fi][ti]])
                            co_step()
                    nf = len(grp) * 2
                    for ti in range(NT):
                        c0, w, kind = tiles[ti]
                        for dc in range(KC):
                            bk = 4 + (dc % 2)
                            for fi in range(nf):
                                so = (gi % 2) * GP + fi // 2
                                S.op("pe", lambda e, fi=fi, so=so, dc=dc, bk=bk, c0=c0, w=w: e.matmul(
                                    ps[bk][:, 0:w], wout[:, so, fi % 2, dc * 128:(dc + 1) * 128], act[:, fi, c0:c0 + w],
                                    start=(fi == 0), stop=(fi == nf - 1)), R=[Twout[so], Tact[fi][ti]], W=[Tps[bk]],
                                    sig=(fi == nf - 1))
                            resid_update(ti, dc, ps[bk][:, 0:w], Tps[bk], m0 + 2, tmp, Ttmp)
                while co is not None:
                    co_step()


        def load_T(src_ap, rows, dst_fn, Wd, stg, Tstg):
            S.dma("sp", stg[0:rows, :], src_ap, W=[Tstg])
            for c in range(KC):
                bk = c % 2
                S.op("pe", lambda e, c=c, bk=bk: e.transpose(ps[bk][:, 0:rows], stg[0:rows, c * 128:(c + 1) * 128],
                                                             identf[0:rows, 0:rows]), R=[Tstg, Tconst], W=[Tps[bk]])
                S.op("act", lambda e, c=c, bk=bk: e.activation(out=dst_fn(c), in_=ps[bk][:, 0:rows], func=AF.Copy),
                     R=[Tps[bk]], W=Wd)

        OSTV = ost[:].rearrange("p c (r s) -> p c r s", s=NSEQ)

        def phase_lru(layer):
            j = layer // 2
            with ExitStack() as ph:
                def pb_(name, shape, dt=F32):
                    UID[0] += 1; return ph.enter_context(nc.sbuf_tensor("%s_%d" % (name, UID[0]), list(shape), dt))
                h = pb_("l_h", [128, KC, T], BF16); Th = [Tl("h%d" % i) for i in range(NT)]
                yin = pb_("l_yin", [128, KC, T], BF16); Tyin = [Tl("yin%d" % i) for i in range(NT)]
                h0s = pb_("l_h0s", [128, KC, NS]); Th0s = Tl("h0s")
                cvh = pb_("l_cvh", [128, KC, NS * 3]); Tcvh = Tl("cvh")
                with ExitStack() as ph1:
                    UID[0] += 1
                    sq = ph1.enter_context(nc.sbuf_tensor("l_sq_%d" % UID[0], [128, 2, 512], BF16)); Tsq = [Tl("sq0"), Tl("sq1")]
                    rstd2 = ph1.enter_context(nc.sbuf_tensor("l_rstd_%d" % UID[0], [128, 2, 512], F32)); Trstd2 = [Tl("r0"), Tl("r1")]
                    stg = ph1.enter_context(nc.sbuf_tensor("l_stg_%d" % UID[0], [NS * 3, D], F32)); Tstg = Tl("stg")
                    tmp2 = ph1.enter_context(nc.sbuf_tensor("l_tmp_%d" % UID[0], [128, 2, 512], F32)); Ttmp2 = [Tl("t0"), Tl("t1")]
                    S.barrier()
                    load_T(A["s_lru_h"][j], NS, lambda c: h0s[:, c, :], [Th0s], stg, Tstg)
                    load_T(A["s_lru_conv"][j], NS * 3, lambda c: cvh[:, c, :], [Tcvh], stg, Tstg)
                    norm_mod_all(3, 4, sq, Tsq, rstd2, Trstd2, tmp2, Ttmp2,
                                 lambda ti: (lambda c, c0=tiles[ti][0], w=tiles[ti][1]: h[:, c, c0:c0 + w]),
                                 lambda ti: (lambda c, ti=ti: [Th[ti]]))
                ph2 = ExitStack()
                def pb2(name, shape, dt=F32):
                    UID[0] += 1; return ph2.enter_context(nc.sbuf_tensor("%s_%d" % (name, UID[0]), list(shape), dt))
                win = pb2("l_win", [128, 2, KC, 2, 256], BF16); Twin = [Tl("w0"), Tl("w1")]
                gw = pb2("l_gw", [128, 2, 2, 512], BF16); Tgw = [Tl("g0"), Tl("g1")]
                rec = pb2("l_rec", [128, 2, 3 + 512]); Trec = Tl("rec")
                recs = pb2("l_recs", [128, 2, NS, 3 + TS]); Trecs = Tl("recs")
                xc = pb2("l_xc", [128, 2, 512]); Txc = Tl("xc")
                xcb = pb2("l_xcb", [128, 2, 512], BF16); Txcb = Tl("xcb")
                ii = pb2("l_ii", [128, 2, 512]); Tii = Tl("ii")
                aa = pb2("l_aa", [128, 2, 512]); Taa = Tl("aa")
                hs = pb2("l_hs", [128, 2, 512]); Ths = Tl("hs")
                gt = pb2("l_gt", [128, 512]); Tgt = Tl("gt")
                carry = pb2("l_carry", [128, 2]); Tcar = Tl("carry")
                t16 = pb2("l_t16", [128, NS]); Tt16 = Tl("t16")
                rr = aa; uu = ii; gb = gt; Tgb = Tgt
                Taa2 = [Tl("aa0"), Tl("aa1")]; Tii2 = [Tl("ii0"), Tl("ii1")]; Ths2 = [Tl("hs0"), Tl("hs1")]; Txc2 = [Tl("xc0"), Tl("xc1")]
                S.barrier()
                wv = A["lru_w_in"][j].rearrange("(k p) n -> p k n", p=128)
                NPT = sum(1 for t_ in tiles if t_[2] == "p")
                for n in range(4):
                    sl = n % 2
                    S.dma_group("pool", [(win[:, sl, :, 0, :], wv[:, :, n * 256:(n + 1) * 256]),
                                         (win[:, sl, :, 1, :], wv[:, :, D + n * 256:D + (n + 1) * 256])], W=[Twin[sl]])
                    S.dma("pool", gw[:, sl, :, :], A["lru_gate_w"][j, n].rearrange("(k p) g -> p k g", p=128), W=[Tgw[sl]])
                    S.op("dve", lambda e: e.memset(carry[:], 0.0), W=[Tcar])
                    S.op("dve", lambda e: e.memset(rec[:, :, 0:3], 0.0), W=[Trec])
                    wprev = 0
                    for ti in range(NT):
                        c0, w, kind = tiles[ti]
                        last_p = (kind == "p" and ti == NPT - 1)
                        if kind == "p" and ti > 0:
                            S.op("dve", lambda e, wprev=wprev: e.tensor_copy(rec[:, :, 0:3], rec[:, :, wprev:wprev + 3]),
                                 R=[Trec], W=[Trec])
                        for ci in range(2):
                            for k in range(KC):
                                S.op("pe", lambda e, k=k, ci=ci: e.matmul(ps[ci][:, 0:w], win[:, sl, k, 1, ci * 128:(ci + 1) * 128],
                                     h[:, k, c0:c0 + w], start=(k == 0), stop=(k == KC - 1)), R=[Twin[sl], Th[ti]], W=[Tps[ci]],
                                     sig=(k == KC - 1))
                            if kind == "p":
                                S.op("act", lambda e, ci=ci: e.activation(out=rec[:, ci, 3:3 + w], in_=ps[ci][:, 0:w], func=AF.Copy),
                                     R=[Tps[ci]], W=[Trec])
                            else:
                                S.op("dve", lambda e, ci=ci: e.tensor_copy(recs[:, ci, :, 0:3],
                                     cvh[:, 2 * n + ci, :].rearrange("p (s r) -> p s r", r=3)), R=[Tcvh], W=[Trecs])
                                S.op("act", lambda e, ci=ci: e.activation(out=recs[:, ci, :, 3:3 + TS],
                                     in_=ps[ci][:, 0:w].rearrange("p (s t) -> p s t", t=TS), func=AF.Copy), R=[Tps[ci]], W=[Trecs])
                        for ci in range(2):
                            bk = 4 + ci
                            for k in range(KC):
                                S.op("pe", lambda e, k=k, ci=ci, bk=bk: e.matmul(ps[bk][:, 0:w], win[:, sl, k, 0, ci * 128:(ci + 1) * 128],
                                     h[:, k, c0:c0 + w], start=(k == 0), stop=(k == KC - 1)), R=[Twin[sl], Th[ti]], W=[Tps[bk]],
                                     sig=(k == KC - 1))
                        for ci in range(2):
                            c = 2 * n + ci
                            cw = VOFF["lru_conv_w"] + j * 32 + c
                            cb = VOFF["lru_conv_b"] + j * 8 + c
                            if kind == "p":
                                XP = lambda jt, ci=ci: rec[:, ci, jt:jt + w]
                                XO = xc[:, ci, 0:w]
                                TR = Trec
                            else:
                                XP = lambda jt, ci=ci: recs[:, ci, :, jt:jt + TS]
                                XO = xc[:, ci, 0:w].rearrange("p (s t) -> p s t", t=TS)
                                TR = Trecs
                            S.op("act", lambda e, XP=XP, XO=XO, cw=cw, cb=cb: e.activation(out=XO, in_=XP(3), func=AF.Identity,
                                 scale=vecs[:, cw + 24:cw + 25], bias=vecs[:, cb:cb + 1]), R=[TR, Tvec], W=[Txc2[ci]])
                            for jt in range(3):
                                S.op("dve", lambda e, XP=XP, XO=XO, cw=cw, jt=jt: e.scalar_tensor_tensor(XO, XP(jt),
                                     vecs[:, cw + jt * 8:cw + jt * 8 + 1], XO, ALU.mult, ALU.add), R=[TR, Tvec, Txc2[ci]], W=[Txc2[ci]])
                            if last_p:
                                S.op("act", lambda e, ci=ci, c=c: e.activation(out=OSTV[:, c, 4 * j + 1:4 * j + 4, 0],
                                     in_=rec[:, ci, w:w + 3], func=AF.Copy), R=[Trec], W=[Tost])
                            if kind == "s":
                                S.op("act", lambda e, ci=ci, c=c: e.activation(out=OSTV[:, c, 4 * j + 1:4 * j + 4, 1:NSEQ],
                                     in_=recs[:, ci, :, TS:TS + 3].rearrange("p s q -> p q s"), func=AF.Copy), R=[Trecs], W=[Tost])
                        S.op("act", lambda e: e.activation(out=xcb[:, :, 0:w], in_=xc[:, :, 0:w], func=AF.Copy), R=Txc2, W=[Txcb])
                        for oc in range(4):
                            bk = 2 + oc % 2
                            for k in range(2):
                                S.op("pe", lambda e, k=k, oc=oc, bk=bk: e.matmul(ps[bk][:, 0:w], gw[:, sl, k, oc * 128:(oc + 1) * 128],
                                     xcb[:, k, 0:w], start=(k == 0), stop=(k == 1)), R=[Tgw[sl], Txcb], W=[Tps[bk]], sig=(k == 1))
                            gbc = VOFF["lru_gate_b"] + j * 16 + n * 4 + oc
                            dst = rr[:, oc, 0:w] if oc < 2 else ii[:, oc - 2, 0:w]
                            S.op("act", lambda e, bk=bk, dst=dst, gbc=gbc: e.activation(out=dst, in_=ps[bk][:, 0:w], func=AF.Sigmoid,
                                 bias=vecs[:, gbc:gbc + 1], scale=1.0), R=[Tps[bk], Tvec], W=[Taa2[oc] if oc < 2 else Tii2[oc - 2]])
                        for ci in range(2):
                            c8c = XO_C8 + j * 8 + 2 * n + ci
                            S.op("act", lambda e, ci=ci, c8c=c8c: e.activation(out=aa[:, ci, 0:w], in_=rr[:, ci, 0:w], func=AF.Exp,
                                 scale=vecs[:, c8c:c8c + 1]), R=[Taa2[ci], Tvec], W=[Taa2[ci]])
                        for ci in range(2):
                            S.op("act", lambda e, ci=ci: e.activation(out=hs[:, ci, 0:w], in_=aa[:, ci, 0:w], func=AF.Square),
                                 R=[Taa2[ci]], W=[Ths2[ci]])
                        for ci in range(2):
                            S.op("act", lambda e, ci=ci: e.activation(out=hs[:, ci, 0:w], in_=hs[:, ci, 0:w], func=AF.Sqrt,
                                 scale=-1.0, bias=1.0), R=[Ths2[ci]], W=[Ths2[ci]])
                        for ci in range(2):
                            S.op("dve", lambda e, ci=ci: e.tensor_tensor(uu[:, ci, 0:w], ii[:, ci, 0:w], hs[:, ci, 0:w], ALU.mult),
                                 R=[Ths2[ci], Tii2[ci]], W=[Tii2[ci]])
                            S.op("dve", lambda e, ci=ci: e.tensor_tensor(uu[:, ci, 0:w], uu[:, ci, 0:w], xc[:, ci, 0:w], ALU.mult),
                                 R=[Tii2[ci], Txc2[ci]], W=[Tii2[ci]])
                        for ci in range(2):
                            c = 2 * n + ci
                            if kind == "p":
                                S.op("dve", lambda e, ci=ci: e.tensor_tensor_scan(hs[:, ci, 0:w], aa[:, ci, 0:w], uu[:, ci, 0:w],
                                     carry[:, ci:ci + 1], ALU.mult, ALU.add), R=[Taa2[ci], Tii2[ci], Tcar], W=[Ths2[ci]])
                                S.op("act", lambda e, ci=ci: e.activation(out=carry[:, ci:ci + 1], in_=hs[:, ci, w - 1:w], func=AF.Copy),
                                     R=[Ths2[ci]], W=[Tcar])
                                if last_p:
                                    S.op("act", lambda e, ci=ci, c=c: e.activation(out=OSTV[:, c, 4 * j, 0:1], in_=hs[:, ci, w - 1:w],
                                         func=AF.Copy), R=[Ths2[ci]], W=[Tost])
                            else:
                                a3 = aa[:, ci, 0:w].rearrange("p (s t) -> p s t", t=TS)
                                u3 = uu[:, ci, 0:w].rearrange("p (s t) -> p s t", t=TS)
                                h3 = hs[:, ci, 0:w].rearrange("p (s t) -> p s t", t=TS)
                                S.op("dve", lambda e, a3=a3, c=c: e.tensor_tensor(t16[:, :], a3[:, :, 0], h0s[:, c, :], ALU.mult),
                                     R=[Taa2[ci], Th0s], W=[Tt16])
                                S.op("dve", lambda e, u3=u3: e.tensor_tensor(u3[:, :, 0], u3[:, :, 0], t16[:, :], ALU.add),
                                     R=[Tt16, Tii2[ci]], W=[Tii2[ci]])
                                S.op("dve", lambda e, a3=a3: e.memset(a3[:, :, 0], 0.0), R=[Tt16], W=[Taa2[ci]])
                                S.op("dve", lambda e, ci=ci: e.tensor_tensor_scan(hs[:, ci, 0:w], aa[:, ci, 0:w], uu[:, ci, 0:w],
                                     0.0, ALU.mult, ALU.add), R=[Taa2[ci], Tii2[ci]], W=[Ths2[ci]])
                                S.op("act", lambda e, h3=h3, c=c: e.activation(out=OSTV[:, c, 4 * j, 1:NSEQ], in_=h3[:, :, TS - 1],
                                     func=AF.Copy), R=[Ths2[ci]], W=[Tost])
                        for ci in range(2):
                            bk = 4 + ci
                            S.op("act", lambda e, bk=bk, ci=ci: e.activation(out=xc[:, ci, 0:w], in_=ps[bk][:, 0:w], func=AF.Square),
                                 R=[Tps[bk]], W=[Txc2[ci]])
                        for ci in range(2):
                            bk = 4 + ci
                            S.op("dve", lambda e, ci=ci: e.tensor_scalar(xc[:, ci, 0:w], xc[:, ci, 0:w], 0.044715, 1.0, ALU.mult, ALU.add),
                                 R=[Txc2[ci]], W=[Txc2[ci]])
                            S.op("dve", lambda e, bk=bk, ci=ci: e.tensor_tensor(xc[:, ci, 0:w], xc[:, ci, 0:w], ps[bk][:, 0:w], ALU.mult),
                                 R=[Txc2[ci], Tps[bk]], W=[Txc2[ci]])
                        for ci in range(2):
                            S.op("act", lambda e, ci=ci: e.activation(out=xc[:, ci, 0:w], in_=xc[:, ci, 0:w], func=AF.Sigmoid,
                                 scale=1.5957691216057308), R=[Txc2[ci]], W=[Txc2[ci]])
                        for ci in range(2):
                            bk = 4 + ci; c = 2 * n + ci
                            S.op("dve", lambda e, bk=bk, ci=ci: e.tensor_tensor(xc[:, ci, 0:w], xc[:, ci, 0:w], ps[bk][:, 0:w], ALU.mult),
                                 R=[Txc2[ci], Tps[bk]], W=[Txc2[ci]])
                            S.op("dve", lambda e, ci=ci, c=c: e.tensor_tensor(yin[:, c, c0:c0 + w], xc[:, ci, 0:w], hs[:, ci, 0:w], ALU.mult),
                                 R=[Txc2[ci], Ths2[ci]], W=[Tyin[ti]])
                        wprev = w
                ph2.close()
                wo = pb_("l_wo", [128, KC, D], BF16); Two = Tl("wo")
                tmp = pb_("l_tmp2", [128, 512]); Ttmp = Tl("tmp2")
                S.barrier()
                S.dma("pool", wo[:, :, :], A["lru_w_out"][j].rearrange("(k p) n -> p k n", p=128), W=[Two])
                for ti in range(NT):
                    c0, w, kind = tiles[ti]
                    for dc in range(KC):
                        bk = dc % 2
                        for k in range(KC):
                            S.op("pe", lambda e, k=k, dc=dc, bk=bk: e.matmul(ps[bk][:, 0:w], wo[:, k, dc * 128:(dc + 1) * 128],
                                 yin[:, k, c0:c0 + w], start=(k == 0), stop=(k == KC - 1)), R=[Two, Tyin[ti]], W=[Tps[bk]],
                                 sig=(k == KC - 1))
                        resid_update(ti, dc, ps[bk][:, 0:w], Tps[bk], 5, tmp, Ttmp)


        def phase_rwkv(layer):
            jl = layer // 2
            scr = SCR[jl]
            HW_ = 1 + TP + NS * (1 + TS)
            NPT = sum(1 for t_ in tiles if t_[2] == "p")
            with ExitStack() as ph:
                def pb_(name, shape, dt=F32):
                    UID[0] += 1; return ph.enter_context(nc.sbuf_tensor("%s_%d" % (name, UID[0]), list(shape), dt))
                h = pb_("r_h", [128, KC, HW_], BF16); Th = Tl("h")
                xm = pb_("r_xm", [128, KC, T], BF16); Txm = Tl("xm")
                shs = pb_("r_shs", [128, KC, NS]); Tshs = Tl("shs")

                def hS(c):
                    return h[:, c, 1 + TP:HW_].rearrange("p (s t) -> p s t", t=1 + TS)
                with ExitStack() as ph1:
                    UID[0] += 1
                    sq = ph1.enter_context(nc.sbuf_tensor("r_sq_%d" % UID[0], [128, 2, 512], BF16)); Tsq = [Tl("sq0"), Tl("sq1")]
                    rstd2 = ph1.enter_context(nc.sbuf_tensor("r_rstd_%d" % UID[0], [128, 2, 512], F32)); Trstd2 = [Tl("r0"), Tl("r1")]
                    tmp2 = ph1.enter_context(nc.sbuf_tensor("r_tmp_%d" % UID[0], [128, 2, 512], F32)); Ttmp2 = [Tl("tmp0"), Tl("tmp1")]
                    stg = ph1.enter_context(nc.sbuf_tensor("r_stg_%d" % UID[0], [NS * 3, D], F32)); Tstg = Tl("stg")
                    S.barrier()
                    load_T(A["s_shift"][jl], NS, lambda c: shs[:, c, :], [Tshs], stg, Tstg)
                    S.op("dve", lambda e: e.memset(h[:, :, 0:1], 0.0), W=[Th])
                    for c in range(KC):
                        S.op("dve", lambda e, c=c: e.tensor_copy(hS(c)[:, :, 0], shs[:, c, :]), R=[Tshs], W=[Th])
                    def outf_t(ti):
                        c0, w, kind = tiles[ti]
                        if kind == "p":
                            return lambda c, c0=c0, w=w: h[:, c, 1 + c0:1 + c0 + w]
                        return lambda c: hS(c)[:, :, 1:1 + TS]

                    def extra_t(ti):
                        c0, w, kind = tiles[ti]

                        def extra(c, tmp, Ttmp):
                            if kind == "p" and ti == NPT - 1:
                                S.op("act", lambda e: e.activation(out=OSTV[:, c, 8 + jl, 0:1], in_=tmp[:, w - 1:w], func=AF.Identity,
                                     scale=modT[:, 4 * 8 + c, 0:1], bias=modT[:, 3 * 8 + c, 0:1]), R=[Ttmp, Tmod], W=[Tost])
                            if kind == "s":
                                S.op("dve", lambda e: e.tensor_tensor(OSTV[:, c, 8 + jl, 1:NSEQ], tv(tmp[:, 0:w], kind)[:, :, TS - 1],
                                     modT[:, 3 * 8 + c, 1:NSEQ], ALU.add), R=[Ttmp, Tmod], W=[Tost])
                        return extra
                    norm_mod_all(3, 4, sq, Tsq, rstd2, Trstd2, tmp2, Ttmp2, outf_t, lambda ti: (lambda c: [Th]), extra_t)
                ph2 = ExitStack()

                def pb2(name, shape, dt=F32):
                    UID[0] += 1; return ph2.enter_context(nc.sbuf_tensor("%s_%d" % (name, UID[0]), list(shape), dt))
                stage = pb2("r_stage", [128, 2, T]); Tstage = [Tl("st0"), Tl("st1")]
                xtmp2 = pb2("r_xtmp", [128, 2, 512]); Txt2 = [Tl("xt0"), Tl("xt1")]; xcnt = [0]
                wpc = pb2("r_wpc", [128, 2, KC, 256], BF16); Twpc = [Tl("wp0"), Tl("wp1")]
                wl1 = pb2("r_wl1", [128, KC, 160], BF16); Twl1 = Tl("wl1")
                wl2 = pb2("r_wl2", [128, 2, D], BF16); Twl2 = Tl("wl2")
                mid = pb2("r_mid", [128, 2, T], BF16); Tmid = Tl("mid")
                S.barrier()
                cnt = {"bank": 0, "st": 0, "pc": 0}

                def project(name, mm_fn, evac_fn):
                    for oc in range(KC):
                        sbi = cnt["st"] % 2; cnt["st"] += 1
                        for ti in range(NT):
                            c0, w, kind = tiles[ti]
                            bk = cnt["bank"] % 4; cnt["bank"] += 1
                            mms = mm_fn(oc, ti)
                            for i, (l_, r_, Rt) in enumerate(mms):
                                S.op("pe", lambda e, l_=l_, r_=r_, i=i, bk=bk, w=w: e.matmul(ps[bk][:, 0:w], l_, r_, start=(i == 0),
                                     stop=(i == len(mms) - 1)), R=Rt, W=[Tps[bk]], sig=(i == len(mms) - 1))
                            eng, fn, Rx = evac_fn(oc, ps[bk][:, 0:w], stage[:, sbi, c0:c0 + w])
                            S.op(eng, fn, R=[Tps[bk]] + Rx, W=[Tstage[sbi]])
                        S.dma("sp", scr[name][oc], stage[:, sbi, :], R=[Tstage[sbi]])

                def copy_evac(oc, src, dst):
                    return ("act", lambda e: e.activation(out=dst, in_=src, func=AF.Copy), [])

                def lora(w1ap, n1, w2ap, midf, outf, bias_col, name):
                    S.dma("pool", wl1[:, :, 0:n1], w1ap.rearrange("(k p) n -> p k n", p=128), W=[Twl1])
                    parts = [(0, min(n1, 128))] + ([(128, n1 - 128)] if n1 > 128 else [])
                    for pi, (r0, rn) in enumerate(parts):
                        S.dma("pool", wl2[0:rn, pi, :], w2ap[r0:r0 + rn, :], W=[Twl2])
                        for ti in range(NT):
                            c0, w, kind = tiles[ti]
                            bk = cnt["bank"] % 4; cnt["bank"] += 1
                            for k in range(KC):
                                S.op("pe", lambda e, k=k, bk=bk, r0=r0, rn=rn, c0=c0, w=w: e.matmul(ps[bk][0:rn, 0:w], wl1[:, k, r0:r0 + rn],
                                     xm[:, k, c0:c0 + w], start=(k == 0), stop=(k == KC - 1)), R=[Twl1, Txm], W=[Tps[bk]], sig=(k == KC - 1))
                            S.op("act", lambda e, bk=bk, rn=rn, pi=pi, c0=c0, w=w: e.activation(out=mid[0:rn, pi, c0:c0 + w],
                                 in_=ps[bk][0:rn, 0:w], func=midf), R=[Tps[bk]], W=[Tmid])

                    def mm_fn(oc, ti):
                        c0, w, kind = tiles[ti]
                        return [(wl2[0:rn, pi, oc * 128:(oc + 1) * 128], mid[0:rn, pi, c0:c0 + w], [Twl2, Tmid])
                                for pi, (r0, rn) in enumerate(parts)]

                    def evac_fn(oc, src, dst):
                        if bias_col is None:
                            return ("act", lambda e: e.activation(out=dst, in_=src, func=outf), [])
                        bc = bias_col + oc
                        return ("act", lambda e: e.activation(out=dst, in_=src, func=outf, bias=vecs[:, bc:bc + 1], scale=1.0), [Tvec])
                    project(name, mm_fn, evac_fn)

                for pj in range(6):
                    for c in range(KC):
                        muc = VOFF["rwkv_mu"] + jl * 48 + pj * 8 + c
                        omc = XO_OMMU + jl * 48 + pj * 8 + c
                        for (pc0, pw, pk) in tiles:
                            if pk != "p":
                                continue
                            xp_ = xcnt[0] % 2; xcnt[0] += 1
                            xtmp = xtmp2[:, xp_, :]; Txt = Txt2[xp_]
                            S.op("act", lambda e, c=c, muc=muc, pc0=pc0, pw=pw, xtmp=xtmp: e.activation(out=xtmp[:, 0:pw], in_=h[:, c, pc0:pc0 + pw],
                                 func=AF.Identity, scale=vecs[:, muc:muc + 1]), R=[Th, Tvec], W=[Txt])
                            S.op("dve", lambda e, c=c, omc=omc, pc0=pc0, pw=pw, xtmp=xtmp: e.scalar_tensor_tensor(xm[:, c, pc0:pc0 + pw],
                                 h[:, c, 1 + pc0:1 + pc0 + pw], vecs[:, omc:omc + 1], xtmp[:, 0:pw], ALU.mult, ALU.add),
                                 R=[Th, Txt, Tvec], W=[Txm])
                        xp_ = xcnt[0] % 2; xcnt[0] += 1
                        xtmp = xtmp2[:, xp_, :]; Txt = Txt2[xp_]
                        xs3 = xtmp[:, 0:NS * TS].rearrange("p (s t) -> p s t", t=TS)
                        S.op("act", lambda e, c=c, muc=muc, xs3=xs3: e.activation(out=xs3, in_=hS(c)[:, :, 0:TS], func=AF.Identity,
                             scale=vecs[:, muc:muc + 1]), R=[Th, Tvec], W=[Txt])
                        S.op("dve", lambda e, c=c, omc=omc, xs3=xs3: e.scalar_tensor_tensor(
                             xm[:, c, TP:T].rearrange("p (s t) -> p s t", t=TS), hS(c)[:, :, 1:1 + TS], vecs[:, omc:omc + 1], xs3,
                             ALU.mult, ALU.add), R=[Th, Txt, Tvec], W=[Txm])
                    if pj < 3:
                        wv = A["rwkv_w_rkv"][jl, pj].rearrange("(k p) n -> p k n", p=128)
                        slots = {}

                        def mm_fn(oc, ti, wv=wv, slots=slots):
                            c0, w, kind = tiles[ti]
                            pc, q = oc // 2, oc % 2
                            if pc not in slots:
                                sl = cnt["pc"] % 2; cnt["pc"] += 1
                                S.dma("pool", wpc[:, sl, :, :], wv[:, :, pc * 256:(pc + 1) * 256], W=[Twpc[sl]])
                                slots[pc] = sl
                            sl = slots[pc]
                            return [(wpc[:, sl, k, q * 128:(q + 1) * 128], xm[:, k, c0:c0 + w], [Twpc[sl], Txm]) for k in range(KC)]
                        project(("r", "k", "v")[pj], mm_fn, copy_evac)
                        if pj == 2 and jl == 1:
                            lora(A["rwkv_v1"][0], 32, A["rwkv_v2"][0], AF.Copy, AF.Sigmoid, VOFF["rwkv_v0"], "sv")
                    elif pj == 3:
                        lora(A["rwkv_w1"][jl], 64, A["rwkv_w2"][jl], AF.Tanh, AF.Sigmoid, VOFF["rwkv_w0"] + jl * 8, "sw")
                    elif pj == 4:
                        lora(A["rwkv_a1"][jl], 64, A["rwkv_a2"][jl], AF.Copy, AF.Sigmoid, VOFF["rwkv_a0"] + jl * 8, "a")
                    else:
                        lora(A["rwkv_g1"][jl], 160, A["rwkv_g2"][jl], AF.Sigmoid, AF.Copy, None, "g")
                ph2.close()
            tilesB = []
            for ti_, (c0_, w_, k_) in enumerate(tiles):
                if k_ == "p":
                    for o_ in range(0, w_, 256):
                        tilesB.append((ti_, c0_ + o_, min(256, w_ - o_), "p", False))
                else:
                    tilesB.append((ti_, c0_, w_, "s", False))
            lp_ = max(i for i, t_ in enumerate(tilesB) if t_[3] == "p")
            tilesB[lp_] = tilesB[lp_][:4] + (True,)
            with ExitStack() as ph:
                def pb_(name, shape, dt=F32):
                    UID[0] += 1; return ph.enter_context(nc.sbuf_tensor("%s_%d" % (name, UID[0]), list(shape), dt))

                def alloc_stream():
                    B = {}
                    B["Lb"] = (pb_("b_L", [128, 6, 256]), Tl("L"))
                    Wk = {}
                    for nm in ("cum", "ecx", "ecp", "ecn", "etc", "kkn", "kmod", "bv", "tq", "tk"):
                        Wk[nm] = (pb_("b_" + nm, [128, 256]), Tl(nm))
                    Wk["Yt"] = Wk["cum"]; Wk["cen"] = Wk["ecx"]; Wk["bon"] = Wk["etc"]
                    B["Wk"] = Wk
                    B["AR"] = (pb_("b_AR", [128, 512], BF16), Tl("AR"))
                    B["BK"] = (pb_("b_BK", [128, 512], BF16), Tl("BK"))
                    B["BKH"] = (pb_("b_BKH", [128, 3, 256], BF16), Tl("BKH"))
                    B["TM"] = (pb_("b_TM", [128, 3072], BF16), Tl("TM"))
                    Gb = {}
                    for nm in ("Lab", "LabT", "Xa", "XTa", "Ta", "Tb"):
                        Gb[nm] = (pb_("b_" + nm, [128, 256]), Tl(nm))
                    for nm in ("Mbr", "Lkb", "Mkr", "Tbf"):
                        Gb[nm] = (pb_("b_" + nm, [128, 256], BF16), Tl(nm))
                    B["Gb"] = Gb
                    B["WTb"] = (pb_("b_WTb", [128, 8, 64], BF16), Tl("WTb"))
                    B["PTb"] = (pb_("b_PTb", [128, 8, 64], BF16), Tl("PTb"))
                    B["Sio"] = (pb_("b_Sio", [128, NS, 64]), Tl("Sio"))
                    B["STs"] = (pb_("b_STs", [128, NS, 64]), Tl("STs"))
                    B["STsb"] = (pb_("b_STsb", [128, NS, 64], BF16), Tl("STsb"))
                    B["eg"] = (pb_("b_eg", [128, 16]), Tl("eg"))
                    B["ybc"] = (pb_("b_ybc", [128, 256], BF16), Tl("ybc"))
                    B["woc"] = (pb_("b_woc", [128, D], BF16), Tl("woc"))
                    B["rtmp"] = (pb_("b_rtmp", [128, 256]), Tl("rtmp"))
                    return B
                BFS = [alloc_stream(), alloc_stream()]
                S.barrier()
                NEG = -float(np.exp(-0.5))

                def stream(c, B, si):
                    Lb, TL = B["Lb"]; Wk = B["Wk"]; AR, TAR = B["AR"]; BK, TBK = B["BK"]; BKH, TBKH = B["BKH"]; TM, TTM = B["TM"]
                    Gb = B["Gb"]; WTb, TWTb = B["WTb"]; PTb, TPTb = B["PTb"]; Sio, TSio = B["Sio"]; STs, TST = B["STs"]
                    STsb, TSTb = B["STsb"]; eg, Teg = B["eg"]; ybc, Tybc = B["ybc"]; woc, Twoc = B["woc"]; rtmp, Trtmp = B["rtmp"]
                    tq3s = Sio; Ttq3s = TSio
                    b0, b1, b2 = (0, 1, 2) if si == 0 else (3, 4, 5)
                    S.dma("pool", woc[:, :], A["rwkv_w_o"][jl, c * 128:(c + 1) * 128, :], W=[Twoc])
                    kkc = VOFF["rwkv_k_k"] + jl * 8 + c; kac = VOFF["rwkv_k_a"] + jl * 8 + c; omka = XO_OMKA + jl * 8 + c
                    rkc = VOFF["rwkv_r_k"] + jl * 8 + c; lnw = VOFF["rwkv_ln_w"] + jl * 8 + c; lnb = VOFF["rwkv_ln_b"] + jl * 8 + c
                    S.op("dve", lambda e: e.memset(STs[:, 0:1, :], 0.0), W=[TST])
                    S.op("dve", lambda e: e.memset(STsb[:, 0:1, :], 0.0), W=[TSTb])
                    for (ti, c0, w, kind, lastp) in tilesB:
                        yield
                        C = 64 if kind == "p" else TS
                        nch = w // C
                        gsz = 4 if kind == "p" else 16
                        gsz = min(gsz, nch)
                        v3 = lambda ap: ap[:, 0:w].rearrange("p (n t) -> p n t", t=C)
                        ARv = AR[:, 0:nch * 2 * C].rearrange("p (n a t) -> p n a t", a=2, t=C)
                        BKv = BK[:, 0:nch * 2 * C].rearrange("p (n a t) -> p n a t", a=2, t=C)
                        TMv = TM[:, 0:nch * 192].rearrange("p (n a k) -> p n a k", a=3, k=64)
                        Gv = lambda nm: Gb[nm][0][:, 0:nch * C].rearrange("p (n t) -> p n t", t=C)
                        cmo = 0 if kind == "p" else 512
                        Lr, Lk, Lv, Lsw, La, Lg = [Lb[:, i, 0:w] for i in range(6)]
                        W_ = lambda nm: Wk[nm][0][:, 0:w]
                        TW = lambda nm: Wk[nm][1]
                        if kind == "s":
                            S.dma("sp", Sio[:, :, :], A["s_wkv"][jl, :, 2 * c:2 * c + 2, :, :].rearrange("s h v k -> (h v) s k"), W=[TSio])
                            for half in range(2):
                                bk = (b0, b1)[half]
                                for l in range(8):
                                    s_ = half * 8 + l
                                    for p in range(2):
                                        kr = slice(p * 64, (p + 1) * 64)
                                        S.op("pe", lambda e, l=l, s_=s_, kr=kr, bk=bk, p=p: e.matmul(ps[bk][kr, l * 64:(l + 1) * 64],
                                             Sio[kr, s_, :], identf[kr, kr], start=True, stop=True), R=[TSio, Tconst], W=[Tps[bk]],
                                             sig=(l == 7 and p == 1))
                                S.op("act", lambda e, half=half, bk=bk: e.activation(out=STs[:, half * 8:half * 8 + 8, :],
                                     in_=ps[bk][:, :].rearrange("p (s v) -> p s v", v=64), func=AF.Copy), R=[Tps[bk]], W=[TST])
                            S.op("act", lambda e: e.activation(out=STsb[:, :, :], in_=STs[:, :, :], func=AF.Copy), R=[TST], W=[TSTb])
                        names = ["r", "k", "v", "sw", "a", "g"]
                        S.dma_group("sp", [(Lb[:, i, 0:w], scr[nm][c, :, c0:c0 + w]) for i, nm in enumerate(names)], W=[TL])
                        if jl == 1:
                            S.dma("sp", W_("kkn"), scr["sv"][c, :, c0:c0 + w], W=[TW("kkn")])
                            S.dma("sp", W_("kmod"), SCR[0]["v"][c, :, c0:c0 + w], W=[TW("kmod")])
                            S.op("dve", lambda e: e.tensor_tensor(W_("tq"), W_("kmod"), Lv, ALU.subtract), R=[TL, TW("kmod")], W=[TW("tq")])
                            S.op("dve", lambda e: e.tensor_tensor(W_("tq"), W_("tq"), W_("kkn"), ALU.mult), R=[TW("kkn"), TW("tq")], W=[TW("tq")])
                            S.op("dve", lambda e: e.tensor_tensor(Lv, Lv, W_("tq"), ALU.add), R=[TL, TW("tq")], W=[TL])
                        yield
                        S.op("dve", lambda e: e.tensor_scalar(Lsw, Lsw, NEG, None, ALU.mult), R=[TL], W=[TL])
                        S.op("dve", lambda e: e.tensor_tensor_scan(W_("cum"), cmk[:, cmo:cmo + w], Lsw, 0.0, ALU.mult, ALU.add),
                             R=[TL, Tconst], W=[TW("cum")])
                        S.op("dve", lambda e: e.tensor_tensor(W_("tq"), W_("cum"), Lsw, ALU.subtract), R=[TL, TW("cum")], W=[TW("tq")])
                        yield
                        S.op("act", lambda e: e.activation(out=W_("ecx"), in_=W_("tq"), func=AF.Exp), R=[TW("tq")], W=[TW("ecx")])
                        S.op("act", lambda e: e.activation(out=W_("ecp"), in_=W_("cum"), func=AF.Exp), R=[TW("cum")], W=[TW("ecp")])
                        S.op("act", lambda e: e.activation(out=W_("ecn"), in_=W_("cum"), func=AF.Exp, scale=-1.0), R=[TW("cum")], W=[TW("ecn")])
                        S.op("dve", lambda e: e.tensor_tensor(v3(Wk["tq"][0]), v3(Wk["cum"][0])[:, :, C - 1:C].to_broadcast([128, nch, C]),
                             v3(Wk["cum"][0]), ALU.subtract), R=[TW("cum"), TW("ecx")], W=[TW("tq")])
                        S.op("act", lambda e: e.activation(out=W_("etc"), in_=W_("tq"), func=AF.Exp), R=[TW("tq")], W=[TW("etc")])
                        S.op("act", lambda e: e.activation(out=eg[:, 0:nch], in_=v3(Wk["cum"][0])[:, :, C - 1], func=AF.Exp),
                             R=[TW("cum")], W=[Teg])
                        yield
                        S.op("dve", lambda e: e.tensor_scalar(W_("kkn"), Lk, vecs[:, kkc:kkc + 1], None, ALU.mult), R=[TL, Tvec], W=[TW("kkn")])
                        S.op("act", lambda e: e.activation(out=W_("tk"), in_=W_("kkn"), func=AF.Square), R=[TW("kkn")], W=[TW("tk")])
                        S.op("pe", lambda e: e.matmul(ps[b0][:, 0:w], blk1[:], W_("tk"), start=True, stop=True), R=[TW("tk"), Tconst], W=[Tps[b0]])
                        yield
                        S.op("dve", lambda e: e.tensor_scalar(W_("tk"), ps[b0][:, 0:w], 1e-24, None, ALU.max), R=[Tps[b0]], W=[TW("tk")])
                        S.op("act", lambda e: e.activation(out=W_("tk"), in_=W_("tk"), func=AF.Ln), R=[TW("tk")], W=[TW("tk")])
                        S.op("act", lambda e: e.activation(out=W_("tk"), in_=W_("tk"), func=AF.Exp, scale=-0.5), R=[TW("tk")], W=[TW("tk")])
                        S.op("dve", lambda e: e.tensor_tensor(W_("kkn"), W_("kkn"), W_("tk"), ALU.mult), R=[TW("tk"), TW("kkn")], W=[TW("kkn")])
                        yield
                        S.op("dve", lambda e: e.tensor_scalar(W_("kmod"), La, vecs[:, kac:kac + 1], vecs[:, omka:omka + 1], ALU.mult, ALU.add),
                             R=[TL, Tvec], W=[TW("kmod")])
                        S.op("dve", lambda e: e.tensor_tensor(W_("kmod"), W_("kmod"), Lk, ALU.mult), R=[TL, TW("kmod")], W=[TW("kmod")])
                        S.op("dve", lambda e: e.tensor_tensor(W_("bv"), W_("kkn"), La, ALU.mult), R=[TL, TW("kkn")], W=[TW("bv")])
                        yield
                        S.op("dve", lambda e: e.scalar_tensor_tensor(ARv[:, :, 0, :], v3(Wk["kkn"][0]), -1.0, v3(Wk["ecx"][0]), ALU.mult, ALU.mult),
                             R=[TW("kkn"), TW("ecx")], W=[TAR])
                        S.op("dve", lambda e: e.tensor_tensor(ARv[:, :, 1, :], v3(Lb[:, 0, :]), v3(Wk["ecp"][0]), ALU.mult), R=[TL, TW("ecp")], W=[TAR])
                        S.op("dve", lambda e: e.tensor_tensor(BKv[:, :, 0, :], v3(Wk["bv"][0]), v3(Wk["ecn"][0]), ALU.mult), R=[TW("bv"), TW("ecn")], W=[TBK])
                        S.op("dve", lambda e: e.tensor_tensor(BKv[:, :, 1, :], v3(Wk["kmod"][0]), v3(Wk["ecn"][0]), ALU.mult), R=[TW("kmod"), TW("ecn")], W=[TBK])
                        S.op("dve", lambda e: e.tensor_tensor(BKH[:, 0, 0:w], W_("bv"), W_("etc"), ALU.mult), R=[TW("bv"), TW("etc")], W=[TBKH])
                        S.op("dve", lambda e: e.tensor_tensor(BKH[:, 1, 0:w], W_("kmod"), W_("etc"), ALU.mult), R=[TW("kmod"), TW("etc")], W=[TBKH])
                        S.op("act", lambda e: e.activation(out=BKH[:, 2, 0:w], in_=Lv, func=AF.Copy), R=[TL], W=[TBKH])
                        yield
                        S.op("dve", lambda e: e.tensor_tensor(W_("bv"), Lr, W_("kmod"), ALU.mult), R=[TL, TW("kmod")], W=[TW("bv")])
                        S.op("dve", lambda e: e.tensor_scalar(W_("bv"), W_("bv"), vecs[:, rkc:rkc + 1], None, ALU.mult), R=[TW("bv"), Tvec], W=[TW("bv")])
                        S.op("pe", lambda e: e.matmul(ps[b1][:, 0:w], blk1[:], W_("bv"), start=True, stop=True), R=[TW("bv"), Tconst], W=[Tps[b1]])
                        S.op("dve", lambda e: e.tensor_tensor(W_("bon"), ps[b1][:, 0:w], Lv, ALU.mult), R=[Tps[b1], TL], W=[TW("bon")])
                        yield
                        for g0 in range(0, nch, 4):
                            gn = min(4, nch - g0)
                            for l in range(gn):
                                ch = g0 + l
                                for p in range(2):
                                    kr = slice(p * 64, (p + 1) * 64); cr = slice(p * 64, p * 64 + C)
                                    for a in range(3):
                                        S.op("pe", lambda e, l=l, ch=ch, kr=kr, cr=cr, a=a: e.transpose(
                                             psb[cr, l * 192 + a * 64:l * 192 + (a + 1) * 64], BKH[kr, a, ch * C:(ch + 1) * C], identb[kr, kr]),
                                             R=[TBKH, Tconst], W=[Tpsb], sig=(l == gn - 1 and p == 1 and a == 2))
                            S.op("act", lambda e, g0=g0, gn=gn: e.activation(out=TMv[:, g0:g0 + gn, :, :].rearrange("p n a k -> p (n a k)"),
                                 in_=psb[:, 0:gn * 192], func=AF.Copy), R=[Tpsb], W=[TTM])
                        yield
                        nlev = {64: 6, 8: 3}[C]
                        Tfin = None
                        for g0 in range(0, nch, gsz):
                            for l in range(gsz):
                                ch = g0 + l
                                for p in range(2):
                                    kr = slice(p * 64, (p + 1) * 64); cr = slice(p * 64, p * 64 + C)
                                    lastm = (l == gsz - 1 and p == 1)
                                    S.op("pe", lambda e, l=l, ch=ch, kr=kr, cr=cr: e.matmul(ps[b0][cr, l * 2 * C:(l + 1) * 2 * C], BKv[kr, ch, 0, :],
                                         ARv[kr, ch, :, :].rearrange("p a t -> p (a t)"), start=True, stop=True), R=[TBK, TAR], W=[Tps[b0]], sig=lastm)
                                    S.op("pe", lambda e, l=l, ch=ch, kr=kr, cr=cr: e.matmul(ps[b1][cr, l * 2 * C:(l + 1) * 2 * C], BKv[kr, ch, 1, :],
                                         ARv[kr, ch, :, :].rearrange("p a t -> p (a t)"), start=True, stop=True), R=[TBK, TAR], W=[Tps[b1]], sig=lastm)
                                    S.op("pe", lambda e, l=l, ch=ch, kr=kr, cr=cr: e.matmul(ps[b2][cr, l * C:(l + 1) * C], ARv[kr, ch, 0, :],
                                         BKv[kr, ch, 0, :], start=True, stop=True), R=[TBK, TAR], W=[Tps[b2]], sig=lastm)
                            yield
                            gs = slice(g0, g0 + gsz)
                            p0v = ps[b0][:, 0:gsz * 2 * C].rearrange("p (n a t) -> p n a t", a=2, t=C)
                            p1v = ps[b1][:, 0:gsz * 2 * C].rearrange("p (n a t) -> p n a t", a=2, t=C)
                            pgv = lambda i: ps[i][:, 0:gsz * C].rearrange("p (n t) -> p n t", t=C)
                            mb = lambda i: msk[:, i, 0:C].unsqueeze(1).to_broadcast([128, gsz, C])
                            S.op("dve", lambda e: e.tensor_tensor(Gv("Lab")[:, gs, :], p0v[:, :, 0, :], mb(0), ALU.mult), R=[Tps[b0], Tconst], W=[Gb["Lab"][1]])
                            S.op("dve", lambda e: e.tensor_tensor(Gv("Mbr")[:, gs, :], p0v[:, :, 1, :], mb(1), ALU.mult), R=[Tps[b0], Tconst], W=[Gb["Mbr"][1]])
                            S.op("dve", lambda e: e.tensor_tensor(Gv("Lkb")[:, gs, :], p1v[:, :, 0, :], mb(0), ALU.mult), R=[Tps[b1], Tconst], W=[Gb["Lkb"][1]])
                            S.op("dve", lambda e: e.tensor_tensor(Gv("Mkr")[:, gs, :], p1v[:, :, 1, :], mb(1), ALU.mult), R=[Tps[b1], Tconst], W=[Gb["Mkr"][1]])
                            S.op("dve", lambda e: e.tensor_tensor(Gv("LabT")[:, gs, :], pgv(b2), mb(2), ALU.mult), R=[Tps[b2], Tconst], W=[Gb["LabT"][1]])
                            yield
                            Xn, XTn, Tc, Tn = "Lab", "LabT", "Ta", "Tb"
                            Xo, XTo = "Xa", "XTa"
                            S.op("dve", lambda e, Xn=Xn, Tc=Tc: e.tensor_tensor(Gv(Tc)[:, gs, :], Gv(Xn)[:, gs, :], mb(3), ALU.add),
                                 R=[Gb[Xn][1], Tconst], W=[Gb[Tc][1]])
                            for lev in range(1, nlev):
                                lastl = (lev == nlev - 1)
                                for l in range(gsz):
                                    ch = g0 + l
                                    for p in range(2):
                                        cr = slice(p * 64, p * 64 + C)
                                        lastm = (l == gsz - 1 and p == 1)
                                        if not lastl:
                                            S.op("pe", lambda e, l=l, ch=ch, cr=cr, Xn=Xn, XTn=XTn: e.matmul(ps[b2][cr, l * C:(l + 1) * C], Gv(XTn)[cr, ch, :],
                                                 Gv(Xn)[cr, ch, :], start=True, stop=True), R=[Gb[Xn][1], Gb[XTn][1]], W=[Tps[b2]], sig=lastm)
                                        S.op("pe", lambda e, l=l, ch=ch, cr=cr, Xn=Xn, XTn=XTn: e.matmul(ps[b0][cr, l * C:(l + 1) * C], Gv(Xn)[cr, ch, :],
                                             Gv(XTn)[cr, ch, :], start=True, stop=True), R=[Gb[Xn][1], Gb[XTn][1]], W=[Tps[b0]], sig=lastm)
                                yield
                                if not lastl:
                                    S.op("act", lambda e, Xo=Xo: e.activation(out=Gv(Xo)[:, gs, :], in_=pgv(b2), func=AF.Copy), R=[Tps[b2]], W=[Gb[Xo][1]])
                                S.op("act", lambda e, XTo=XTo: e.activation(out=Gv(XTo)[:, gs, :], in_=pgv(b0), func=AF.Copy), R=[Tps[b0]], W=[Gb[XTo][1]])
                                yield
                                for l in range(gsz):
                                    ch = g0 + l
                                    for p in range(2):
                                        cr = slice(p * 64, p * 64 + C)
                                        S.op("pe", lambda e, l=l, ch=ch, cr=cr, XTo=XTo, Tc=Tc: e.matmul(ps[b1][cr, l * C:(l + 1) * C], Gv(XTo)[cr, ch, :],
                                             Gv(Tc)[cr, ch, :], start=True, stop=True), R=[Gb[XTo][1], Gb[Tc][1]], W=[Tps[b1]], sig=(l == gsz - 1 and p == 1))
                                S.op("dve", lambda e, Tc=Tc, Tn=Tn: e.tensor_tensor(Gv(Tn)[:, gs, :], pgv(b1), Gv(Tc)[:, gs, :], ALU.add),
                                     R=[Tps[b1], Gb[Tc][1]], W=[Gb[Tn][1]])
                                yield
                                Xn, Xo = Xo, Xn
                                XTn, XTo = XTo, XTn
                                Tc, Tn = Tn, Tc
                            S.op("act", lambda e, Tc=Tc: e.activation(out=Gv("Tbf")[:, gs, :], in_=Gv(Tc)[:, gs, :], func=AF.Copy),
                                 R=[Gb[Tc][1]], W=[Gb["Tbf"][1]])
                            Tfin = "Tbf"
                        yield
                        sets = [[ch] for ch in range(nch)] if kind == "p" else [list(range(0, 8)), list(range(8, 16))]
                        for chs in sets:
                            n = len(chs)
                            for l, ch in enumerate(chs):
                                si = 0 if kind == "p" else ch
                                for p in range(2):
                                    kr = slice(p * 64, (p + 1) * 64); cr = slice(p * 64, p * 64 + C)
                                    S.op("pe", lambda e, l=l, ch=ch, si=si, kr=kr, cr=cr: e.matmul(ps[b2][cr, l * 64:(l + 1) * 64], ARv[kr, ch, 0, :],
                                         STsb[kr, si, :], start=True, stop=False), R=[TAR, TSTb], W=[Tps[b2]], sig=False)
                                    S.op("pe", lambda e, l=l, ch=ch, cr=cr: e.matmul(ps[b2][cr, l * 64:(l + 1) * 64], Gv("Lkb")[cr, ch, :],
                                         TMv[cr, ch, 2, :], start=False, stop=True), R=[Gb["Lkb"][1], TTM], W=[Tps[b2]], sig=(l == n - 1 and p == 1))
                            yield
                            S.op("act", lambda e, n=n: e.activation(out=WTb[:, 0:n, :].rearrange("p n v -> p (n v)"), in_=ps[b2][:, 0:n * 64], func=AF.Copy),
                                 R=[Tps[b2]], W=[TWTb])
                            for l, ch in enumerate(chs):
                                for p in range(2):
                                    cr = slice(p * 64, p * 64 + C)
                                    S.op("pe", lambda e, l=l, ch=ch, cr=cr: e.matmul(ps[b0][cr, l * 64:(l + 1) * 64], Gv(Tfin)[cr, ch, :], WTb[cr, l, :],
                                         start=True, stop=True), R=[Gb[Tfin][1], TWTb], W=[Tps[b0]], sig=(l == n - 1 and p == 1))
                            yield
                            S.op("dve", lambda e, n=n: e.tensor_copy(PTb[:, 0:n, :].rearrange("p n v -> p (n v)"), ps[b0][:, 0:n * 64]), R=[Tps[b0]], W=[TPTb])
                            for l, ch in enumerate(chs):
                                si = 0 if kind == "p" else ch
                                for p in range(2):
                                    kr = slice(p * 64, (p + 1) * 64); cr = slice(p * 64, p * 64 + C)
                                    lastm = (l == n - 1 and p == 1)
                                    S.op("pe", lambda e, l=l, ch=ch, si=si, kr=kr: e.matmul(ps[b1][kr, ch * C:(ch + 1) * C], STsb[kr, si, :], ARv[kr, ch, 1, :],
                                         start=True, stop=False), R=[TSTb, TAR], W=[Tps[b1]], sig=False)
                                    S.op("pe", lambda e, l=l, ch=ch, kr=kr, cr=cr: e.matmul(ps[b1][kr, ch * C:(ch + 1) * C], PTb[cr, l, :], Gv("Mbr")[cr, ch, :],
                                         start=False, stop=False), R=[TPTb, Gb["Mbr"][1]], W=[Tps[b1]], sig=False)
                                    S.op("pe", lambda e, l=l, ch=ch, kr=kr, cr=cr: e.matmul(ps[b1][kr, ch * C:(ch + 1) * C], TMv[cr, ch, 2, :], Gv("Mkr")[cr, ch, :],
                                         start=False, stop=True), R=[TTM, Gb["Mkr"][1]], W=[Tps[b1]], sig=lastm)
                                    S.op("pe", lambda e, l=l, ch=ch, kr=kr, cr=cr: e.matmul(ps[b2][kr, l * 64:(l + 1) * 64], TMv[cr, ch, 0, :], PTb[cr, l, :],
                                         start=True, stop=False), R=[TTM, TPTb], W=[Tps[b2]], sig=False)
                                    S.op("pe", lambda e, l=l, ch=ch, kr=kr, cr=cr: e.matmul(ps[b2][kr, l * 64:(l + 1) * 64], TMv[cr, ch, 1, :], TMv[cr, ch, 2, :],
                                         start=False, stop=True), R=[TTM], W=[Tps[b2]], sig=lastm)
                            yield
                            ch0 = chs[0]
                            if kind == "p":
                                S.op("dve", lambda e, ch0=ch0: e.scalar_tensor_tensor(STs[:, 0, :], STs[:, 0, :], eg[:, ch0:ch0 + 1], ps[b2][:, 0:64],
                                     ALU.mult, ALU.add), R=[TST, Teg, Tps[b2]], W=[TST])
                                S.op("act", lambda e: e.activation(out=STsb[:, 0, :], in_=STs[:, 0, :], func=AF.Copy), R=[TST], W=[TSTb])
                            else:
                                S.op("dve", lambda e, ch0=ch0, n=n: e.tensor_tensor(tq3s[:, 0:n, :], STs[:, ch0:ch0 + n, :],
                                     eg[:, ch0:ch0 + n].unsqueeze(2).to_broadcast([128, n, 64]), ALU.mult), R=[TST, Teg], W=[Ttq3s])
                                S.op("dve", lambda e, ch0=ch0, n=n: e.tensor_tensor(STs[:, ch0:ch0 + n, :], tq3s[:, 0:n, :],
                                     ps[b2][:, 0:n * 64].rearrange("p (n v) -> p n v", v=64), ALU.add), R=[Ttq3s, Tps[b2], TST], W=[TST])
                                S.op("act", lambda e, ch0=ch0, n=n: e.activation(out=STsb[:, ch0:ch0 + n, :], in_=STs[:, ch0:ch0 + n, :], func=AF.Copy),
                                     R=[TST], W=[TSTb])
                        yield
                        S.op("act", lambda e: e.activation(out=W_("Yt"), in_=ps[b1][:, 0:w], func=AF.Copy), R=[Tps[b1]], W=[TW("Yt")])
                        S.op("pe", lambda e: e.matmul(ps[b0][:, 0:w], blk64[:], W_("Yt"), start=True, stop=True), R=[TW("Yt"), Tconst], W=[Tps[b0]])
                        S.op("dve", lambda e: e.tensor_tensor(W_("cen"), W_("Yt"), ps[b0][:, 0:w], ALU.subtract), R=[TW("Yt"), Tps[b0]], W=[TW("cen")])
                        S.op("act", lambda e: e.activation(out=W_("tq"), in_=W_("cen"), func=AF.Square), R=[TW("cen")], W=[TW("tq")])
                        S.op("pe", lambda e: e.matmul(ps[b1][:, 0:w], blk64[:], W_("tq"), start=True, stop=True), R=[TW("tq"), Tconst], W=[Tps[b1]])
                        yield
                        S.op("act", lambda e: e.activation(out=W_("tq"), in_=ps[b1][:, 0:w], func=AF.Ln, bias=64e-5, scale=1.0), R=[Tps[b1]], W=[TW("tq")])
                        S.op("act", lambda e: e.activation(out=W_("tq"), in_=W_("tq"), func=AF.Exp, scale=-0.5), R=[TW("tq")], W=[TW("tq")])
                        S.op("dve", lambda e: e.tensor_tensor(W_("cen"), W_("cen"), W_("tq"), ALU.mult), R=[TW("tq"), TW("cen")], W=[TW("cen")])
                        S.op("dve", lambda e: e.tensor_scalar(W_("cen"), W_("cen"), vecs[:, lnw:lnw + 1], vecs[:, lnb:lnb + 1], ALU.mult, ALU.add),
                             R=[TW("cen"), Tvec], W=[TW("cen")])
                        S.op("dve", lambda e: e.tensor_tensor(W_("cen"), W_("cen"), W_("bon"), ALU.add), R=[TW("cen"), TW("bon")], W=[TW("cen")])
                        S.op("dve", lambda e: e.tensor_tensor(ybc[:, 0:w], W_("cen"), Lg, ALU.mult), R=[TW("cen"), TL], W=[Tybc])
                        yield
                        for dc in range(KC):
                            bk = (b0, b1)[dc % 2]
                            S.op("pe", lambda e, dc=dc, bk=bk: e.matmul(ps[bk][:, 0:w], woc[:, dc * 128:(dc + 1) * 128], ybc[:, 0:w], start=True, stop=True),
                                 R=[Twoc, Tybc], W=[Tps[bk]])
                            resid_update(ti, dc, ps[bk][:, 0:w], Tps[bk], 5, rtmp, Trtmp, sub=(c0, w, kind))
                            if dc % 2 == 1:
                                yield
                        if lastp:
                            for p in range(2):
                                kr = slice(p * 64, (p + 1) * 64)
                                S.op("pe", lambda e, kr=kr: e.matmul(ps[b0][kr, 0:64], STs[kr, 0, :], identf[kr, kr], start=True, stop=True),
                                     R=[TST, Tconst], W=[Tps[b0]], sig=(p == 1))
                            S.op("act", lambda e: e.activation(out=Sio[:, 0, :], in_=ps[b0][:, 0:64], func=AF.Copy), R=[Tps[b0]], W=[TSio])
                            S.dma("sp", A["o_p_wkv"][jl, 2 * c:2 * c + 2, :, :].rearrange("h v k -> (h v) k"), Sio[:, 0, :], R=[TSio])
                        if kind == "s":
                            for half in range(2):
                                bk = (b0, b1)[half]
                                for l in range(8):
                                    s_ = half * 8 + l
                                    for p in range(2):
                                        kr = slice(p * 64, (p + 1) * 64)
                                        S.op("pe", lambda e, l=l, s_=s_, kr=kr, bk=bk: e.matmul(ps[bk][kr, l * 64:(l + 1) * 64], STs[kr, s_, :],
                                             identf[kr, kr], start=True, stop=True), R=[TST, Tconst], W=[Tps[bk]], sig=(l == 7 and p == 1))
                                S.op("act", lambda e, half=half, bk=bk: e.activation(out=Sio[:, half * 8:half * 8 + 8, :],
                                     in_=ps[bk][:, :].rearrange("p (s k) -> p s k", k=64), func=AF.Copy), R=[Tps[bk]], W=[TSio])
                            S.dma("sp", A["o_s_wkv"][jl, :, 2 * c:2 * c + 2, :, :].rearrange("s h v k -> (h v) s k"), Sio[:, :, :], R=[TSio])
                for pair in range(0, KC, 2):
                    gens = [stream(pair, BFS[0], 0), stream(pair + 1, BFS[1], 1)]
                    alive = list(gens)
                    while alive:
                        for g_ in list(alive):
                            try:
                                next(g_)
                            except StopIteration:
                                alive.remove(g_)

        def phase_final():
            with ExitStack() as ph:
                def pb_(name, shape, dt=F32):
                    UID[0] += 1; return ph.enter_context(nc.sbuf_tensor("%s_%d" % (name, UID[0]), list(shape), dt))
                sq = pb_("o_sq", [128, 2, 512], BF16); Tsq = [Tl("sq0"), Tl("sq1")]
                rstd = pb_("o_rstd", [128, 512]); Trstd = Tl("rstd")
                yt = pb_("o_yt", [128, KC, 512]); Tyt = Tl("yt")
                yo = pb_("o_yo", [128, 2, D]); Tyo = [Tl("yo0"), Tl("yo1")]
                S.barrier()
                g0 = VOFF["final_gain"]
                bi = 0
                for ti in range(NT):
                    c0, w, kind = tiles[ti]
                    norm_stats(ti, sq, Tsq, rstd, Trstd)
                    for c in range(KC):
                        S.op("dve", lambda e, c=c: e.scalar_tensor_tensor(yt[:, c, 0:w], x[:, c, c0:c0 + w], vecs[:, g0 + c:g0 + c + 1],
                                                                          rstd[:, 0:w], ALU.mult, ALU.mult),
                             R=[Tx[c][ti], Trstd, Tvec], W=[Tyt])
                    for b in range(w // 128):
                        yb = bi % 2; bi += 1
                        for half in range(2):
                            pbk = half
                            for q in range(4):
                                c = half * 4 + q
                                S.op("pe", lambda e, c=c, q=q, b=b, pbk=pbk: e.transpose(ps[pbk][:, q * 128:(q + 1) * 128],
                                     yt[:, c, b * 128:(b + 1) * 128], identf[:]), R=[Tyt, Tconst], W=[Tps[pbk]], sig=(q == 3))
                            if half == 0:
                                S.op("act", lambda e, yb=yb, pbk=pbk: e.activation(out=yo[:, yb, 0:512], in_=ps[pbk][:, :], func=AF.Copy),
                                     R=[Tps[pbk]], W=[Tyo[yb]])
                            else:
                                S.op("dve", lambda e, yb=yb, pbk=pbk: e.tensor_copy(yo[:, yb, 512:1024], ps[pbk][:, :]),
                                     R=[Tps[pbk]], W=[Tyo[yb]])
                        t0 = c0 + b * 128
                        dst = A["yp"][t0:t0 + 128, :] if kind == "p" else A["ys"][:, :]
                        S.dma("sp", dst, yo[:, yb, :], R=[Tyo[yb]])
                with nc.sbuf_tensor("o_sm", [128, 2, D], F32) as osm:
                    Tosm = Tl("osm")
                    S.barrier()
                    NCOL = 10 * NSEQ
                    for blk, (b0, bn) in enumerate([(0, 128), (128, NCOL - 128)]):
                        for c in range(KC):
                            pbk = c % 2
                            S.op("pe", lambda e, c=c, b0=b0, bn=bn, pbk=pbk: e.transpose(ps[pbk][0:bn, 0:128], ost[:, c, b0:b0 + bn],
                                                                                      identf[:]), R=[Tost, Tconst], W=[Tps[pbk]])
                            S.op("act", lambda e, c=c, bn=bn, blk=blk, pbk=pbk: e.activation(
                                out=osm[0:bn, blk, c * 128:(c + 1) * 128], in_=ps[pbk][0:bn, 0:128], func=AF.Copy),
                                R=[Tps[pbk]], W=[Tosm])
                    opairs = []
                    for r in range(10):
                        col = r * NSEQ
                        blk, prow = (0, col) if col < 128 else (1, col - 128)
                        opairs.append((A["o_p_small"][r:r + 1, :], osm[prow:prow + 1, blk, :]))
                        s0_ = 0
                        while s0_ < NS:
                            col = r * NSEQ + 1 + s0_
                            blk, prow = (0, col) if col < 128 else (1, col - 128)
                            m = min(NS - s0_, (128 - prow) if blk == 0 else NS)
                            opairs.append((A["o_s_small"][r, s0_:s0_ + m, :], osm[prow:prow + m, blk, :]))
                            s0_ += m
                    S.dma_group("sp", opairs, R=[Tosm])

        phase_mod(0)
        for layer in range(DEPTH):
            modT = modTs[layer % 2]; Tmod = Tmods[layer % 2]
            phase_ffn(layer, 0)
            if layer % 2 == 0 and do_lru:
                phase_lru(layer)
            if layer % 2 == 1 and do_rwkv:
                phase_rwkv(layer)
            phase_ffn(layer, 1, co_layer=(layer + 1 if layer + 1 < DEPTH else None))
        phase_final()
        S.drain("sp")
        build.nins = S.nins
    return nc


def make_consts():
    ident = np.eye(128, dtype=np.float32)
    m = np.zeros((128, 4, 64), np.float32)
    sidx = (np.arange(128) % 64)[:, None]; t = np.arange(64)[None, :]
    m[:, 0, :] = (t > sidx)
    m[:, 1, :] = (t >= sidx)
    m[:, 2, :] = (t < sidx)
    m[:, 3, :] = (t == sidx)
    return ident, m


def pack_inputs(inp, core, TP):
    g = lambda k: np.asarray(inp[k], dtype=np.float32)
    ident, m = make_consts()
    s0, s1 = core * NS, (core + 1) * NS
    cm = np.ones((128, 640), np.float32); cm[:, 0:512:64] = 0.0; cm[:, 512:640:8] = 0.0
    d = {"c_ident": ident, "c_mask": m, "c_cm": cm,
         "xp": np.ascontiguousarray(g("x_prompt")[core, :TP]),
         "xs": np.ascontiguousarray(g("x_sample")[s0:s1].reshape(NS * TS, D)),
         "cc": np.ascontiguousarray(np.concatenate([g("c_prompt")[core:core + 1], g("c_sample")[s0:s1]], 0)),
         "s_lru_h": np.ascontiguousarray(g("state_lru_h")[:, s0:s1]),
         "s_lru_conv": np.ascontiguousarray(g("state_lru_conv")[:, s0:s1].reshape(2, NS * 3, D)),
         "s_shift": np.ascontiguousarray(g("state_rwkv_shift")[:, s0:s1]),
         "s_wkv": np.ascontiguousarray(g("state_rwkv_wkv")[:, s0:s1])}
    for n, r in VEC_SPEC:
        d[n] = np.ascontiguousarray(g(n).reshape(r, 128))
    for n in BIG_W:
        d[n] = np.ascontiguousarray(g(n))
    return d


_NC_CACHE = {}


def run_cores(inp, TP, DEPTH, cores, **kw):
    key = (TP, DEPTH, tuple(sorted(kw.items())))
    if key not in _NC_CACHE:
        _NC_CACHE[key] = build(TP=TP, DEPTH=DEPTH, **kw)
    nc = _NC_CACHE[key]
    maps = [pack_inputs(inp, c, TP) for c in cores]
    res = run_bass_kernel_spmd(nc, maps, core_ids=list(range(len(cores))))
    return res.results


def assemble(results, B, TP):
    NB = len(results)
    y_p = np.stack([r["yp"] for r in results], 0)
    y_s = np.concatenate([r["ys"].reshape(NS, TS, D) for r in results], 0)
    psm = np.stack([r["o_p_small"] for r in results], 0)
    ssm = np.concatenate([r["o_s_small"].transpose(1, 0, 2) for r in results], 0)
    def small(a):
        lru_h = np.stack([a[:, 0], a[:, 4]], 0)
        lru_conv = np.stack([a[:, 1:4], a[:, 5:8]], 0)
        shift = np.stack([a[:, 8], a[:, 9]], 0)
        return lru_h, lru_conv, shift
    p_h, p_c, p_sh = small(psm); s_h, s_c, s_sh = small(ssm)
    p_wkv = np.stack([r["o_p_wkv"] for r in results], 1)
    s_wkv = np.concatenate([r["o_s_wkv"] for r in results], 1)
    f = lambda a: np.ascontiguousarray(a, dtype=np.float32)
    return tuple(f(a) for a in (y_p, y_s, p_h, p_c, p_sh, p_wkv, s_h, s_c, s_sh, s_wkv))


def kernel(**inp):
    res = run_cores(inp, 2048, 4, list(range(NCORE)))
    return assemble(res, NCORE, 2048)
```

```python
import numpy as np
from contextlib import ExitStack
import concourse.bass as bass
import concourse.mybir as mybir
from concourse.bass_utils import run_bass_kernel_spmd

F32 = mybir.dt.float32
BF16 = mybir.dt.bfloat16
AF = mybir.ActivationFunctionType
ALU = mybir.AluOpType

D = 1024; KC = 8; DFF = 2816; NS = 16; TS = 8; NSEQ = 17; NCORE = 8
NDS = 40
UID = [0]
SEMLIM = 30000


class Tl:
    __slots__ = ("name", "w", "r")

    def __init__(s, name=""):
        s.name = name; s.w = None; s.r = {}


class Sch:
    def __init__(s, nc, st):
        s.nc = nc; s.st = st
        s.E = {"pe": nc.tensor, "act": nc.scalar, "dve": nc.vector, "pool": nc.gpsimd, "sp": nc.sync}
        s.sems = {e: [st.enter_context(nc.semaphore("c_%s0" % e))] for e in s.E}
        s.cnt = {e: 0 for e in s.E}
        s.pend = {e: [] for e in s.E}
        s.known = {e: {} for e in s.E}
        s.dsem = [st.enter_context(nc.semaphore("dq%d" % i)) for i in range(NDS)]
        s.dcnt = [0] * NDS
        s.dnx = {"pool": 0, "sp": NDS // 2, "act": NDS // 2}
        s.nins = 0

    def _wait(s, e, deps):
        need = {}
        for tok in deps:
            if tok[2] == "pe" and e == "pe":
                continue
            assert tok[1] is not None, "dependency on unsignaled instruction"
            k = id(tok[0])
            if k not in need or need[k][1] < tok[1]:
                need[k] = (tok[0], tok[1])
        for k, (sem, val) in need.items():
            if s.known[e].get(k, 0) >= val:
                continue
            s.E[e].wait_ge(sem, val)
            s.known[e][k] = val

    @staticmethod
    def _deps(R, W):
        deps = []
        for t in R:
            if t.w is not None:
                deps.append(t.w)
        for t in W:
            if t.w is not None:
                deps.append(t.w)
            deps.extend(t.r.values())
        return deps

    def op(s, e, fn, R=(), W=(), sig=True):
        s._wait(e, s._deps(R, W))
        ins = fn(s.E[e])
        s.nins += 1
        tok = [None, None, e]
        s.pend[e].append(tok)
        if sig:
            if s.cnt[e] >= SEMLIM:
                s.sems[e].append(s.st.enter_context(s.nc.semaphore("c_%s%d" % (e, len(s.sems[e])))))
                s.cnt[e] = 0
            s.cnt[e] += 1
            sem = s.sems[e][-1]
            ins.then_inc(sem, 1)
            for p in s.pend[e]:
                p[0] = sem; p[1] = s.cnt[e]
            s.pend[e] = []
        for t in W:
            t.w = tok; t.r = {}
        for t in R:
            t.r[e] = tok
        return ins

    def dma(s, q, out, in_, R=(), W=()):
        lo, hi = (0, NDS // 2) if q == "pool" else (NDS // 2, NDS)
        i = s.dnx[q]; s.dnx[q] = lo + (i + 1 - lo) % (hi - lo)
        deps = s._deps(R, W)
        if s.dcnt[i] > 0:
            deps.append([s.dsem[i], s.dcnt[i], "dma"])
        s._wait(q, deps)
        s.dcnt[i] += 16
        s.E[q].dma_start(out=out, in_=in_).then_inc(s.dsem[i], 16)
        s.nins += 1
        tok = [s.dsem[i], s.dcnt[i], "dma"]
        for t in W:
            t.w = tok; t.r = {}
        for t in R:
            t.r[("d", i)] = tok

    def dma_group(s, q, pairs, R=(), W=()):
        lo, hi = (0, NDS // 2) if q == "pool" else (NDS // 2, NDS)
        i = s.dnx[q]; s.dnx[q] = lo + (i + 1 - lo) % (hi - lo)
        deps = s._deps(R, W)
        if s.dcnt[i] > 0:
            deps.append([s.dsem[i], s.dcnt[i], "dma"])
        s._wait(q, deps)
        for out, in_ in pairs:
            s.dcnt[i] += 16
            s.E[q].dma_start(out=out, in_=in_).then_inc(s.dsem[i], 16)
            s.nins += 1
        tok = [s.dsem[i], s.dcnt[i], "dma"]
        for t in W:
            t.w = tok; t.r = {}
        for t in R:
            t.r[("d", i)] = tok

    def barrier(s):
        e = "sp"
        for o in s.E:
            assert not s.pend[o], "barrier with unsignaled instructions on " + o
        for i in range(NDS):
            if s.dcnt[i] > 0 and s.known[e].get(id(s.dsem[i]), 0) < s.dcnt[i]:
                s.E[e].wait_ge(s.dsem[i], s.dcnt[i])
        for o in s.E:
            if o != e and s.cnt[o] > 0 and s.known[e].get(id(s.sems[o][-1]), 0) < s.cnt[o]:
                s.E[e].wait_ge(s.sems[o][-1], s.cnt[o])
        s.cnt[e] += 1
        s.E[e].sem_inc(s.sems[e][-1], 1)
        for o in s.E:
            if o != e:
                s.E[o].wait_ge(s.sems[e][-1], s.cnt[e])
            for x in s.E:
                s.known[o][id(s.sems[x][-1])] = s.cnt[x]
            for i in range(NDS):
                s.known[o][id(s.dsem[i])] = s.dcnt[i]

    def drain(s, e="sp"):
        for i in range(NDS):
            if s.dcnt[i] > 0 and s.known[e].get(id(s.dsem[i]), 0) < s.dcnt[i]:
                s.E[e].wait_ge(s.dsem[i], s.dcnt[i])
        for o in s.E:
            if o != e and s.cnt[o] > 0:
                s.E[e].wait_ge(s.sems[o][-1], s.cnt[o])


VEC_SPEC = [("ada_b", 4 * 72), ("lru_conv_w", 2 * 4 * 8), ("lru_conv_b", 16), ("lru_lambda", 16),
            ("lru_gate_b", 2 * 16), ("rwkv_mu", 2 * 6 * 8), ("rwkv_w0", 16), ("rwkv_a0", 16), ("rwkv_v0", 8),
            ("rwkv_k_k", 16), ("rwkv_k_a", 16), ("rwkv_r_k", 16), ("rwkv_ln_w", 16), ("rwkv_ln_b", 16),
            ("final_gain", 8)]
VOFF = {}
_o = 0
for _n, _r in VEC_SPEC:
    VOFF[_n] = _o; _o += _r
NVEC = _o
NVT = (NVEC + 127) // 128
XO_C8 = NVT * 128
XO_OMMU = XO_C8 + 16
XO_OMKA = XO_OMMU + 96
NVCOL = XO_OMKA + 16

BIG_W = ["ada_w", "ffn_w_in", "ffn_w_out", "lru_w_in", "lru_gate_w", "lru_w_out", "rwkv_w_rkv", "rwkv_w_o",
         "rwkv_w1", "rwkv_w2", "rwkv_a1", "rwkv_a2", "rwkv_v1", "rwkv_v2", "rwkv_g1", "rwkv_g2"]
W_SHAPES = {"ada_w": [4, 1024, 9216], "ffn_w_in": [4, 2, 1024, 5632], "ffn_w_out": [4, 2, 2816, 1024],
            "lru_w_in": [2, 1024, 2048], "lru_gate_w": [2, 4, 256, 512], "lru_w_out": [2, 1024, 1024],
            "rwkv_w_rkv": [2, 3, 1024, 1024], "rwkv_w_o": [2, 1024, 1024], "rwkv_w1": [2, 1024, 64],
            "rwkv_w2": [2, 64, 1024], "rwkv_a1": [2, 1024, 64], "rwkv_a2": [2, 64, 1024],
            "rwkv_v1": [1, 1024, 32], "rwkv_v2": [1, 32, 1024], "rwkv_g1": [2, 1024, 160], "rwkv_g2": [2, 160, 1024]}


def build(TP=2048, DEPTH=4, do_lru=True, do_rwkv=True):
    T = TP + NS * TS
    nc = bass.Bass("TRN2", target_bir_lowering=False)
    A = {}

    def din(name, shape):
        A[name] = nc.dram_tensor(name, list(shape), F32, kind="ExternalInput").ap()

    def dout(name, shape):
        A[name] = nc.dram_tensor(name, list(shape), F32, kind="ExternalOutput").ap()

    din("c_ident", [128, 128]); din("c_mask", [128, 4, 64]); din("c_cm", [128, 640])
    din("xp", [TP, D]); din("xs", [NS * TS, D]); din("cc", [NSEQ, D])
    din("s_lru_h", [2, NS, D]); din("s_lru_conv", [2, NS * 3, D]); din("s_shift", [2, NS, D])
    din("s_wkv", [2, NS, 16, 64, 64])
    for n, r in VEC_SPEC:
        din(n, [r, 128])
    for n in BIG_W:
        din(n, W_SHAPES[n])
    dout("yp", [TP, D]); dout("ys", [NS * TS, D])
    dout("o_p_small", [10, D])
    dout("o_s_small", [10, NS, D])
    dout("o_p_wkv", [2, 16, 64, 64]); dout("o_s_wkv", [2, NS, 16, 64, 64])

    SCR = [{nm: nc.dram_tensor("scr_%d_%s" % (jl, nm), [KC, 128, T], F32).ap() for nm in ("r", "k", "v", "sw", "a", "g", "sv")}
           for jl in range(2)]
    with ExitStack() as st:
        S = Sch(nc, st)

        def sb(name, shape, dt=F32):
            return st.enter_context(nc.sbuf_tensor(name, list(shape), dt))

        x = sb("x", [128, KC, T]); Tx = [[Tl("x%d_%d" % (c, i)) for i in range(8)] for c in range(KC)]
        vecs = sb("vecs", [128, NVCOL]); Tvec = Tl("vecs")
        modTs = [sb("modT0", [128, 72, NSEQ]), sb("modT1", [128, 72, NSEQ])]; Tmods = [Tl("mod0"), Tl("mod1")]
        modT = modTs[0]; Tmod = Tmods[0]
        scT = sb("scT", [128, KC, NSEQ], BF16); TscT = Tl("scT")
        ones_bf = sb("ones_bf", [128, 128], BF16)
        identf = sb("identf", [128, 128]); identb = sb("identb", [128, 128], BF16)
        blk64 = sb("blk64", [128, 128])
        blk1 = sb("blk1", [128, 128])
        ost = sb("ost", [128, KC, 10 * NSEQ]); Tost = Tl("ost")
        Tconst = Tl("const")
        msk = sb("msk", [128, 4, 64]); cmk = sb("cmk", [128, 640])
        ps = [st.enter_context(nc.psum_tensor("ps%d" % i, [128, 512], F32)) for i in range(7)]
        psb = st.enter_context(nc.psum_tensor("psb", [128, 1024], BF16))
        Tps = [Tl("ps%d" % i) for i in range(7)]; Tpsb = Tl("psb")

        tiles = []
        c0 = 0
        while c0 < TP:
            w = min(512, TP - c0); tiles.append((c0, w, "p")); c0 += w
        tiles.append((TP, NS * TS, "s"))
        NT = len(tiles)

        def seqb(ap_n17, kind, w):
            if kind == "p":
                return ap_n17[:, 0:1].to_broadcast([128, w])
            return ap_n17[:, 1:NSEQ].unsqueeze(2).to_broadcast([128, NS, TS])

        def tv(ap2d, kind):
            if kind == "p":
                return ap2d
            return ap2d.rearrange("p (s t) -> p s t", t=TS)

        S.op("dve", lambda e: e.memset(ones_bf[:], 1.0 / 1024.0), W=[Tconst])
        S.op("dve", lambda e: e.memset(blk64[:], 0.0), W=[Tconst])
        S.op("dve", lambda e: e.memset(blk1[:], 0.0), W=[Tconst])
        for p in range(2):
            S.op("dve", lambda e, p=p: e.memset(blk64[p * 64:(p + 1) * 64, p * 64:(p + 1) * 64], 1.0 / 64.0), W=[Tconst])
            S.op("dve", lambda e, p=p: e.memset(blk1[p * 64:(p + 1) * 64, p * 64:(p + 1) * 64], 1.0), W=[Tconst])
        S.dma("sp", identf[:, :], A["c_ident"][:, :], R=[], W=[Tconst])
        S.dma("sp", msk[:, :, :], A["c_mask"][:, :, :], R=[], W=[Tconst])
        S.dma("sp", cmk[:, :], A["c_cm"][:, :], R=[], W=[Tconst])
        S.op("dve", lambda e: e.tensor_copy(identb[:], identf[:]), R=[Tconst], W=[Tconst])
        for i in range(7):
            S.op("dve", lambda e, i=i: e.memset(ps[i][:], 0.0), W=[Tps[i]])
        S.op("dve", lambda e: e.memset(psb[:].bitcast(F32), 0.0), W=[Tpsb])
        S.op("dve", lambda e: e.memset(ost[:], 0.0), W=[Tost])

        with nc.sbuf_tensor("vstage", [128, NVT, 128], F32) as vstage, nc.sbuf_tensor("cst", [NSEQ, D], F32) as cst:
            Tvs = Tl("vstage"); Tcst = Tl("cst")
            S.op("dve", lambda e: e.memset(vstage[:], 0.0), W=[Tvs])
            vpairs = []
            for n, r in VEC_SPEC:
                g0 = VOFF[n]; done = 0
                while done < r:
                    tix = (g0 + done) // 128; p0 = (g0 + done) % 128
                    m = min(r - done, 128 - p0)
                    vpairs.append((vstage[p0:p0 + m, tix, :], A[n][done:done + m, :]))
                    done += m
            S.dma_group("sp", vpairs, W=[Tvs])
            for tix in range(NVT):
                S.op("pe", lambda e, tix=tix: e.transpose(ps[0][:, 0:128], vstage[:, tix, :], identf[:]),
                     R=[Tvs, Tconst], W=[Tps[0]])
                S.op("act", lambda e, tix=tix: e.activation(out=vecs[:, tix * 128:(tix + 1) * 128], in_=ps[0][:, 0:128],
                                                            func=AF.Copy), R=[Tps[0]], W=[Tvec])
            lam = vecs[:, VOFF["lru_lambda"]:VOFF["lru_lambda"] + 16]
            c8 = vecs[:, XO_C8:XO_C8 + 16]
            S.op("act", lambda e: e.activation(out=c8, in_=lam, func=AF.Exp, scale=-1.0), R=[Tvec], W=[Tvec])
            S.op("act", lambda e: e.activation(out=c8, in_=c8, func=AF.Ln, bias=1.0), R=[Tvec], W=[Tvec])
            S.op("dve", lambda e: e.tensor_scalar(c8, c8, -8.0, None, ALU.mult), R=[Tvec], W=[Tvec])
            mu = vecs[:, VOFF["rwkv_mu"]:VOFF["rwkv_mu"] + 96]
            S.op("dve", lambda e: e.tensor_scalar(vecs[:, XO_OMMU:XO_OMMU + 96], mu, -1.0, 1.0, ALU.mult, ALU.add),
                 R=[Tvec], W=[Tvec])
            ka = vecs[:, VOFF["rwkv_k_a"]:VOFF["rwkv_k_a"] + 16]
            S.op("dve", lambda e: e.tensor_scalar(vecs[:, XO_OMKA:XO_OMKA + 16], ka, -1.0, 1.0, ALU.mult, ALU.add),
                 R=[Tvec], W=[Tvec])
            S.dma("sp", cst[:, :], A["cc"][:, :], W=[Tcst])
            S.op("act", lambda e: e.activation(out=cst[:, :], in_=cst[:, :], func=AF.Silu), R=[Tcst], W=[Tcst])
            for c in range(KC):
                S.op("pe", lambda e, c=c: e.transpose(ps[1][:, c * NSEQ:(c + 1) * NSEQ], cst[:, c * 128:(c + 1) * 128],
                                                      identf[0:NSEQ, 0:NSEQ]), R=[Tcst, Tconst], W=[Tps[1]], sig=(c == KC - 1))
            S.op("act", lambda e: e.activation(out=scT[:].rearrange("p c s -> p (c s)"), in_=ps[1][:, 0:KC * NSEQ],
                                               func=AF.Copy), R=[Tps[1]], W=[TscT])

            with nc.sbuf_tensor("xin0", [128, D], F32) as xin0, nc.sbuf_tensor("xin1", [128, D], F32) as xin1:
                xin = [xin0, xin1]; Txin = [Tl("xin0"), Tl("xin1")]
                nblk = T // 128
                for b in range(nblk):
                    t0 = b * 128
                    src = A["xp"][t0:t0 + 128, :] if t0 < TP else A["xs"][:, :]
                    S.dma("sp", xin[b % 2][:, :], src, W=[Txin[b % 2]])
                    ti = min(t0 // 512, NT - 1) if t0 < TP else NT - 1
                    for half in range(2):
                        pb = half
                        for q in range(4):
                            c = half * 4 + q
                            S.op("pe", lambda e, c=c, q=q, b=b, pb=pb: e.transpose(ps[pb][:, q * 128:(q + 1) * 128],
                                 xin[b % 2][:, c * 128:(c + 1) * 128], identf[:]), R=[Txin[b % 2], Tconst], W=[Tps[pb]],
                                 sig=(q == 3))
                        S.op("act" if half == 0 else "dve",
                             (lambda e, half=half, t0=t0, pb=pb: e.activation(out=x[:, half * 4:half * 4 + 4, t0:t0 + 128],
                              in_=ps[pb][:].rearrange("p (q t) -> p q t", t=128), func=AF.Copy)) if half == 0 else
                             (lambda e, half=half, t0=t0, pb=pb: e.tensor_copy(x[:, half * 4:half * 4 + 4, t0:t0 + 128],
                              ps[pb][:].rearrange("p (q t) -> p q t", t=128))),
                             R=[Tps[pb]], W=[Tx[c][ti] for c in range(half * 4, half * 4 + 4)])

        def norm_stats(ti, sq, Tsq, rstd, Trstd):
            c0, w, kind = tiles[ti]
            for c in range(KC):
                S.op("act", lambda e, c=c: e.activation(out=sq[:, c % 2, 0:w], in_=x[:, c, c0:c0 + w], func=AF.Square),
                     R=[Tx[c][ti]], W=[Tsq[c % 2]])
                S.op("pe", lambda e, c=c: e.matmul(ps[6][:, 0:w], ones_bf[:], sq[:, c % 2, 0:w], start=(c == 0), stop=(c == KC - 1)),
                     R=[Tsq[c % 2], Tconst], W=[Tps[6]], sig=True)
            S.op("act", lambda e: e.activation(out=rstd[:, 0:w], in_=ps[6][:, 0:w], func=AF.Ln, bias=1e-6, scale=1.0),
                 R=[Tps[6]], W=[Trstd])
            S.op("act", lambda e: e.activation(out=rstd[:, 0:w], in_=rstd[:, 0:w], func=AF.Exp, scale=-0.5), R=[Trstd], W=[Trstd])

        def modulate(ti, m_shift, m_scale, rstd, Trstd, tmp2, Ttmp2, out_fn, Wout, extra=None):
            c0, w, kind = tiles[ti]
            for c in range(KC):
                tmp = tmp2[:, c % 2, :]; Ttmp = Ttmp2[c % 2]
                S.op("dve", lambda e, c=c, tmp=tmp: e.tensor_tensor(tmp[:, 0:w], x[:, c, c0:c0 + w], rstd[:, 0:w], ALU.mult),
                     R=[Tx[c][ti], Trstd], W=[Ttmp])
                if kind == "p":
                    S.op("act", lambda e, c=c, tmp=tmp: e.activation(out=out_fn(c), in_=tmp[:, 0:w], func=AF.Identity,
                                                            scale=modT[:, m_scale * 8 + c, 0:1], bias=modT[:, m_shift * 8 + c, 0:1]),
                         R=[Ttmp, Tmod], W=Wout(c))
                else:
                    S.op("dve", lambda e, c=c, tmp=tmp: e.tensor_tensor(tv(tmp[:, 0:w], kind), tv(tmp[:, 0:w], kind),
                                                               seqb(modT[:, m_scale * 8 + c, :], kind, w), ALU.mult),
                         R=[Ttmp, Tmod], W=[Ttmp])
                    S.op("dve", lambda e, c=c, tmp=tmp: e.tensor_tensor(out_fn(c) if len(out_fn(c).shape) == 3 else tv(out_fn(c), kind),
                                                               tv(tmp[:, 0:w], kind),
                                                               seqb(modT[:, m_shift * 8 + c, :], kind, w), ALU.add),
                         R=[Ttmp, Tmod], W=Wout(c))
                if extra is not None:
                    extra(c, tmp, Ttmp)

        def norm_mod_all(m_shift, m_scale, sq, Tsq, rstd2, Trstd2, tmp2, Ttmp2, out_fn_t, Wout_t, extra_t=None):
            norm_stats(0, sq, Tsq, rstd2[:, 0, :], Trstd2[0])
            for ti in range(NT):
                if ti + 1 < NT:
                    norm_stats(ti + 1, sq, Tsq, rstd2[:, (ti + 1) % 2, :], Trstd2[(ti + 1) % 2])
                modulate(ti, m_shift, m_scale, rstd2[:, ti % 2, :], Trstd2[ti % 2], tmp2, Ttmp2, out_fn_t(ti), Wout_t(ti),
                         extra=(extra_t(ti) if extra_t is not None else None))

        def resid_update(ti, dc, psum_ap, Tp, m_gate, tmp, Ttmp, sub=None):
            c0, w, kind = tiles[ti] if sub is None else sub
            if kind == "p":
                S.op("dve", lambda e: e.scalar_tensor_tensor(x[:, dc, c0:c0 + w], psum_ap, modT[:, m_gate * 8 + dc, 0:1],
                                                             x[:, dc, c0:c0 + w], ALU.mult, ALU.add),
                     R=[Tp, Tmod, Tx[dc][ti]], W=[Tx[dc][ti]])
            else:
                S.op("dve", lambda e: e.tensor_tensor(tv(tmp[:, 0:w], kind), tv(psum_ap, kind),
                                                      seqb(modT[:, m_gate * 8 + dc, :], kind, w), ALU.mult),
                     R=[Tp, Tmod], W=[Ttmp])
                S.op("dve", lambda e: e.tensor_tensor(x[:, dc, c0:c0 + w], x[:, dc, c0:c0 + w], tmp[:, 0:w], ALU.add),
                     R=[Ttmp, Tx[dc][ti]], W=[Tx[dc][ti]])

        def gen_mod(layer, mw, Tmw, mT, TmT):
            wv = A["ada_w"][layer].rearrange("(k p) n -> p k n", p=128)
            NPC = 9216 // 256
            bank = 6
            S.dma("pool", mw[:, 0, :, :], wv[:, :, 0:256], W=[Tmw[0]])
            yield
            for pc in range(NPC):
                sl = pc % 2
                if pc + 1 < NPC:
                    S.dma("pool", mw[:, 1 - sl, :, :], wv[:, :, (pc + 1) * 256:(pc + 2) * 256], W=[Tmw[1 - sl]])
                for q in range(2):
                    n = pc * 2 + q
                    m = n // 8; nn = n % 8
                    for k in range(KC):
                        S.op("pe", lambda e, k=k, q=q, sl=sl, nn=nn: e.matmul(
                            ps[bank][:, nn * NSEQ:(nn + 1) * NSEQ], mw[:, sl, k, q * 128:(q + 1) * 128], scT[:, k, :],
                            start=(k == 0), stop=(k == KC - 1)), R=[Tmw[sl], TscT], W=[Tps[bank]], sig=(k == KC - 1))
                    if nn == 7:
                        bcol = VOFF["ada_b"] + layer * 72 + m * 8
                        S.op("dve", lambda e, m=m, bcol=bcol: e.tensor_tensor(
                            mT[:, m * 8:(m + 1) * 8, :], ps[bank][:, 0:8 * NSEQ].rearrange("p (n s) -> p n s", s=NSEQ),
                            vecs[:, bcol:bcol + 8].unsqueeze(2).to_broadcast([128, 8, NSEQ]), ALU.add),
                            R=[Tps[bank], Tvec, TmT], W=[TmT])
                        sl8 = mT[:, m * 8:(m + 1) * 8, :]
                        if m in (1, 4, 5, 7):
                            S.op("dve", lambda e, sl8=sl8: e.tensor_scalar(sl8, sl8, 1.0, None, ALU.add), R=[TmT], W=[TmT])
                        elif m in (2, 8):
                            S.op("dve", lambda e, sl8=sl8: e.tensor_scalar(sl8, sl8, 0.5, 0.5, ALU.mult, ALU.add),
                                 R=[TmT], W=[TmT])
                yield

        def phase_mod(layer):
            with nc.sbuf_tensor("mw%d" % layer, [128, 2, KC, 256], BF16) as mw:
                Tmw = [Tl("mw%d" % i) for i in range(3)]
                S.barrier()
                for _ in gen_mod(layer, mw, Tmw, modTs[layer % 2], Tmods[layer % 2]):
                    pass

        def phase_ffn(layer, which, co_layer=None):
            m0 = 0 if which == 0 else 6
            GP = 2
            with ExitStack() as ph:
                def pb_(name, shape, dt=F32):
                    UID[0] += 1; return ph.enter_context(nc.sbuf_tensor("%s_%d" % (name, UID[0]), list(shape), dt))
                h = pb_("f_h", [128, KC, T], BF16); Th = [Tl("h%d" % i) for i in range(NT)]
                act = pb_("f_act", [128, 2 * GP, T], BF16); Tact = [[Tl("a") for _ in range(NT)] for _ in range(2 * GP)]
                sq = pb_("f_sq", [128, 2, 512], BF16); Tsq = [Tl("sq0"), Tl("sq1")]
                rstd2 = pb_("f_rstd", [128, 2, 512]); Trstd2 = [Tl("rstd0"), Tl("rstd1")]
                tmp2 = pb_("f_tmp", [128, 2, 512]); Ttmp2 = [Tl("tmp0"), Tl("tmp1")]
                tmp = tmp2[:, 0, :]; Ttmp = Ttmp2[0]
                sg = pb_("f_sg", [128, 2, 512]); Tsg = [Tl("sg0"), Tl("sg1")]
                win = pb_("f_win", [128, 3, KC, 2, 256], BF16); Twin = [Tl("win%d" % i) for i in range(3)]
                wout = pb_("f_wout", [128, 2 * GP, 2, D], BF16); Twout = [Tl("wout%d" % i) for i in range(2 * GP)]
                co = None
                if co_layer is not None:
                    mw = pb_("f_mw", [128, 2, KC, 256], BF16)
                    co = gen_mod(co_layer, mw, [Tl("mw%d" % i) for i in range(3)], modTs[co_layer % 2], Tmods[co_layer % 2])
                S.barrier()

                co_n = [0]

                def co_step():
                    nonlocal co
                    for _ in range(2):
                        if co is not None:
                            try:
                                next(co)
                            except StopIteration:
                                co = None
                norm_mod_all(m0, m0 + 1, sq, Tsq, rstd2, Trstd2, tmp2, Ttmp2,
                             lambda ti: (lambda c, c0=tiles[ti][0], w=tiles[ti][1]: h[:, c, c0:c0 + w]),
                             lambda ti: (lambda c, ti=ti: [Th[ti]]))
                wi = A["ffn_w_in"][layer, which].rearrange("(k p) n -> p k n", p=128)
                wo = A["ffn_w_out"][layer, which].rearrange("(f p) n -> p f n", p=128)
                NPC = 11
                pcs = list(range(NPC))
                groups = [pcs[i:i + GP] for i in range(0, NPC, GP)]
                evq = 0
                for gi, grp in enumerate(groups):
                    for li, pc in enumerate(grp):
                        sl = pc % 3
                        S.dma_group("pool", [(win[:, sl, :, 0, :], wi[:, :, pc * 256:(pc + 1) * 256]),
                                             (win[:, sl, :, 1, :], wi[:, :, DFF + pc * 256:DFF + (pc + 1) * 256])], W=[Twin[sl]])
                        so = (gi % 2) * GP + li
                        S.dma("pool", wout[:, so, :, :], wo[:, pc * 2:pc * 2 + 2, :], W=[Twout[so]])
                        for q in range(2):
                            fi = li * 2 + q
                            for ti in range(NT):
                                c0, w, kind = tiles[ti]
                                bg = (evq % 2) * 2; evq += 1
                                for gu in range(2):
                                    for k in range(KC):
                                        S.op("pe", lambda e, k=k, gu=gu, sl=sl, q=q, bg=bg, c0=c0, w=w: e.matmul(
                                            ps[bg + gu][:, 0:w], win[:, sl, k, gu, q * 128:(q + 1) * 128], h[:, k, c0:c0 + w],
                                            start=(k == 0), stop=(k == KC - 1)), R=[Twin[sl], Th[ti]], W=[Tps[bg + gu]],
                                            sig=(k == KC - 1))
                                sgi = (bg // 2)
                                S.op("act", lambda e, bg=bg, w=w, sgi=sgi: e.activation(out=sg[:, sgi, 0:w], in_=ps[bg][:, 0:w],
                                                                                      func=AF.Silu), R=[Tps[bg]], W=[Tsg[sgi]])
                                S.op("dve", lambda e, bg=bg, w=w, sgi=sgi, fi=fi, c0=c0: e.tensor_tensor(
                                    act[:, fi, c0:c0 + w], sg[:, sgi, 0:w], ps[bg + 1][:, 0:w], ALU.mult),
                                    R=[Tsg[sgi], Tps[bg + 1]], W=[Tact[fi][ti]])
                            co_step()
                    nf = len(grp) * 2
                    for ti in range(NT):
                        c0, w, kind = tiles[ti]
                        for dc in range(KC):
                            bk = 4 + (dc % 2)
                            for fi in range(nf):
                                so = (gi % 2) * GP + fi // 2
                                S.op("pe", lambda e, fi=fi, so=so, dc=dc, bk=bk, c0=c0, w=w: e.matmul(
                                    ps[bk][:, 0:w], wout[:, so, fi % 2, dc * 128:(dc + 1) * 128], act[:, fi, c0:c0 + w],
                                    start=(fi == 0), stop=(fi == nf - 1)), R=[Twout[so], Tact[fi][ti]], W=[Tps[bk]],
                                    sig=(fi == nf - 1))
                            resid_update(ti, dc, ps[bk][:, 0:w], Tps[bk], m0 + 2, tmp, Ttmp)
                while co is not None:
                    co_step()


        def load_T(src_ap, rows, dst_fn, Wd, stg, Tstg):
            S.dma("sp", stg[0:rows, :], src_ap, W=[Tstg])
            for c in range(KC):
                bk = c % 2
                S.op("pe", lambda e, c=c, bk=bk: e.transpose(ps[bk][:, 0:rows], stg[0:rows, c * 128:(c + 1) * 128],
                                                             identf[0:rows, 0:rows]), R=[Tstg, Tconst], W=[Tps[bk]])
                S.op("act", lambda e, c=c, bk=bk: e.activation(out=dst_fn(c), in_=ps[bk][:, 0:rows], func=AF.Copy),
                     R=[Tps[bk]], W=Wd)

        OSTV = ost[:].rearrange("p c (r s) -> p c r s", s=NSEQ)

        def phase_lru(layer):
            j = layer // 2
            with ExitStack() as ph:
                def pb_(name, shape, dt=F32):
                    UID[0] += 1; return ph.enter_context(nc.sbuf_tensor("%s_%d" % (name, UID[0]), list(shape), dt))
                h = pb_("l_h", [128, KC, T], BF16); Th = [Tl("h%d" % i) for i in range(NT)]
                yin = pb_("l_yin", [128, KC, T], BF16); Tyin = [Tl("yin%d" % i) for i in range(NT)]
                h0s = pb_("l_h0s", [128, KC, NS]); Th0s = Tl("h0s")
                cvh = pb_("l_cvh", [128, KC, NS * 3]); Tcvh = Tl("cvh")
                with ExitStack() as ph1:
                    UID[0] += 1
                    sq = ph1.enter_context(nc.sbuf_tensor("l_sq_%d" % UID[0], [128, 2, 512], BF16)); Tsq = [Tl("sq0"), Tl("sq1")]
                    rstd2 = ph1.enter_context(nc.sbuf_tensor("l_rstd_%d" % UID[0], [128, 2, 512], F32)); Trstd2 = [Tl("r0"), Tl("r1")]
                    stg = ph1.enter_context(nc.sbuf_tensor("l_stg_%d" % UID[0], [NS * 3, D], F32)); Tstg = Tl("stg")
                    tmp2 = ph1.enter_context(nc.sbuf_tensor("l_tmp_%d" % UID[0], [128, 2, 512], F32)); Ttmp2 = [Tl("t0"), Tl("t1")]
                    S.barrier()
                    load_T(A["s_lru_h"][j], NS, lambda c: h0s[:, c, :], [Th0s], stg, Tstg)
                    load_T(A["s_lru_conv"][j], NS * 3, lambda c: cvh[:, c, :], [Tcvh], stg, Tstg)
                    norm_mod_all(3, 4, sq, Tsq, rstd2, Trstd2, tmp2, Ttmp2,
                                 lambda ti: (lambda c, c0=tiles[ti][0], w=tiles[ti][1]: h[:, c, c0:c0 + w]),
                                 lambda ti: (lambda c, ti=ti: [Th[ti]]))
                ph2 = ExitStack()
                def pb2(name, shape, dt=F32):
                    UID[0] += 1; return ph2.enter_context(nc.sbuf_tensor("%s_%d" % (name, UID[0]), list(shape), dt))
                win = pb2("l_win", [128, 2, KC, 2, 256], BF16); Twin = [Tl("w0"), Tl("w1")]
                gw = pb2("l_gw", [128, 2, 2, 512], BF16); Tgw = [Tl("g0"), Tl("g1")]
                rec = pb2("l_rec", [128, 2, 3 + 512]); Trec = Tl("rec")
                recs = pb2("l_recs", [128, 2, NS, 3 + TS]); Trecs = Tl("recs")
                xc = pb2("l_xc", [128, 2, 512]); Txc = Tl("xc")
                xcb = pb2("l_xcb", [128, 2, 512], BF16); Txcb = Tl("xcb")
                ii = pb2("l_ii", [128, 2, 512]); Tii = Tl("ii")
                aa = pb2("l_aa", [128, 2, 512]); Taa = Tl("aa")
                hs = pb2("l_hs", [128, 2, 512]); Ths = Tl("hs")
                gt = pb2("l_gt", [128, 512]); Tgt = Tl("gt")
                carry = pb2("l_carry", [128, 2]); Tcar = Tl("carry")
                t16 = pb2("l_t16", [128, NS]); Tt16 = Tl("t16")
                rr = aa; uu = ii; gb = gt; Tgb = Tgt
                Taa2 = [Tl("aa0"), Tl("aa1")]; Tii2 = [Tl("ii0"), Tl("ii1")]; Ths2 = [Tl("hs0"), Tl("hs1")]; Txc2 = [Tl("xc0"), Tl("xc1")]
                S.barrier()
                wv = A["lru_w_in"][j].rearrange("(k p) n -> p k n", p=128)
                NPT = sum(1 for t_ in tiles if t_[2] == "p")
                for n in range(4):
                    sl = n % 2
                    S.dma_group("pool", [(win[:, sl, :, 0, :], wv[:, :, n * 256:(n + 1) * 256]),
                                         (win[:, sl, :, 1, :], wv[:, :, D + n * 256:D + (n + 1) * 256])], W=[Twin[sl]])
                    S.dma("pool", gw[:, sl, :, :], A["lru_gate_w"][j, n].rearrange("(k p) g -> p k g", p=128), W=[Tgw[sl]])
                    S.op("dve", lambda e: e.memset(carry[:], 0.0), W=[Tcar])
                    S.op("dve", lambda e: e.memset(rec[:, :, 0:3], 0.0), W=[Trec])
                    wprev = 0
                    for ti in range(NT):
                        c0, w, kind = tiles[ti]
                        last_p = (kind == "p" and ti == NPT - 1)
                        if kind == "p" and ti > 0:
                            S.op("dve", lambda e, wprev=wprev: e.tensor_copy(rec[:, :, 0:3], rec[:, :, wprev:wprev + 3]),
                                 R=[Trec], W=[Trec])
                        for ci in range(2):
                            for k in range(KC):
                                S.op("pe", lambda e, k=k, ci=ci: e.matmul(ps[ci][:, 0:w], win[:, sl, k, 1, ci * 128:(ci + 1) * 128],
                                     h[:, k, c0:c0 + w], start=(k == 0), stop=(k == KC - 1)), R=[Twin[sl], Th[ti]], W=[Tps[ci]],
                                     sig=(k == KC - 1))
                            if kind == "p":
                                S.op("act", lambda e, ci=ci: e.activation(out=rec[:, ci, 3:3 + w], in_=ps[ci][:, 0:w], func=AF.Copy),
                                     R=[Tps[ci]], W=[Trec])
                            else:
                                S.op("dve", lambda e, ci=ci: e.tensor_copy(recs[:, ci, :, 0:3],
                                     cvh[:, 2 * n + ci, :].rearrange("p (s r) -> p s r", r=3)), R=[Tcvh], W=[Trecs])
                                S.op("act", lambda e, ci=ci: e.activation(out=recs[:, ci, :, 3:3 + TS],
                                     in_=ps[ci][:, 0:w].rearrange("p (s t) -> p s t", t=TS), func=AF.Copy), R=[Tps[ci]], W=[Trecs])
                        for ci in range(2):
                            bk = 4 + ci
                            for k in range(KC):
                                S.op("pe", lambda e, k=k, ci=ci, bk=bk: e.matmul(ps[bk][:, 0:w], win[:, sl, k, 0, ci * 128:(ci + 1) * 128],
                                     h[:, k, c0:c0 + w], start=(k == 0), stop=(k == KC - 1)), R=[Twin[sl], Th[ti]], W=[Tps[bk]],
                                     sig=(k == KC - 1))
                        for ci in range(2):
                            c = 2 * n + ci
                            cw = VOFF["lru_conv_w"] + j * 32 + c
                            cb = VOFF["lru_conv_b"] + j * 8 + c
                            if kind == "p":
                                XP = lambda jt, ci=ci: rec[:, ci, jt:jt + w]
                                XO = xc[:, ci, 0:w]
                                TR = Trec
                            else:
                                XP = lambda jt, ci=ci: recs[:, ci, :, jt:jt + TS]
                                XO = xc[:, ci, 0:w].rearrange("p (s t) -> p s t", t=TS)
                                TR = Trecs
                            S.op("act", lambda e, XP=XP, XO=XO, cw=cw, cb=cb: e.activation(out=XO, in_=XP(3), func=AF.Identity,
                                 scale=vecs[:, cw + 24:cw + 25], bias=vecs[:, cb:cb + 1]), R=[TR, Tvec], W=[Txc2[ci]])
                            for jt in range(3):
                                S.op("dve", lambda e, XP=XP, XO=XO, cw=cw, jt=jt: e.scalar_tensor_tensor(XO, XP(jt),
                                     vecs[:, cw + jt * 8:cw + jt * 8 + 1], XO, ALU.mult, ALU.add), R=[TR, Tvec, Txc2[ci]], W=[Txc2[ci]])
                            if last_p:
                                S.op("act", lambda e, ci=ci, c=c: e.activation(out=OSTV[:, c, 4 * j + 1:4 * j + 4, 0],
                                     in_=rec[:, ci, w:w + 3], func=AF.Copy), R=[Trec], W=[Tost])
                            if kind == "s":
                                S.op("act", lambda e, ci=ci, c=c: e.activation(out=OSTV[:, c, 4 * j + 1:4 * j + 4, 1:NSEQ],
                                     in_=recs[:, ci, :, TS:TS + 3].rearrange("p s q -> p q s"), func=AF.Copy), R=[Trecs], W=[Tost])
                        S.op("act", lambda e: e.activation(out=xcb[:, :, 0:w], in_=xc[:, :, 0:w], func=AF.Copy), R=Txc2, W=[Txcb])
                        for oc in range(4):
                            bk = 2 + oc % 2
                            for k in range(2):
                                S.op("pe", lambda e, k=k, oc=oc, bk=bk: e.matmul(ps[bk][:, 0:w], gw[:, sl, k, oc * 128:(oc + 1) * 128],
                                     xcb[:, k, 0:w], start=(k == 0), stop=(k == 1)), R=[Tgw[sl], Txcb], W=[Tps[bk]], sig=(k == 1))
                            gbc = VOFF["lru_gate_b"] + j * 16 + n * 4 + oc
                            dst = rr[:, oc, 0:w] if oc < 2 else ii[:, oc - 2, 0:w]
                            S.op("act", lambda e, bk=bk, dst=dst, gbc=gbc: e.activation(out=dst, in_=ps[bk][:, 0:w], func=AF.Sigmoid,
                                 bias=vecs[:, gbc:gbc + 1], scale=1.0), R=[Tps[bk], Tvec], W=[Taa2[oc] if oc < 2 else Tii2[oc - 2]])
                        for ci in range(2):
                            c8c = XO_C8 + j * 8 + 2 * n + ci
                            S.op("act", lambda e, ci=ci, c8c=c8c: e.activation(out=aa[:, ci, 0:w], in_=rr[:, ci, 0:w], func=AF.Exp,
                                 scale=vecs[:, c8c:c8c + 1]), R=[Taa2[ci], Tvec], W=[Taa2[ci]])
                        for ci in range(2):
                            S.op("act", lambda e, ci=ci: e.activation(out=hs[:, ci, 0:w], in_=aa[:, ci, 0:w], func=AF.Square),
                                 R=[Taa2[ci]], W=[Ths2[ci]])
                        for ci in range(2):
                            S.op("act", lambda e, ci=ci: e.activation(out=hs[:, ci, 0:w], in_=hs[:, ci, 0:w], func=AF.Sqrt,
                                 scale=-1.0, bias=1.0), R=[Ths2[ci]], W=[Ths2[ci]])
                        for ci in range(2):
                            S.op("dve", lambda e, ci=ci: e.tensor_tensor(uu[:, ci, 0:w], ii[:, ci, 0:w], hs[:, ci, 0:w], ALU.mult),
                                 R=[Ths2[ci], Tii2[ci]], W=[Tii2[ci]])
                            S.op("dve", lambda e, ci=ci: e.tensor_tensor(uu[:, ci, 0:w], uu[:, ci, 0:w], xc[:, ci, 0:w], ALU.mult),
                                 R=[Tii2[ci], Txc2[ci]], W=[Tii2[ci]])
                        for ci in range(2):
                            c = 2 * n + ci
                            if kind == "p":
                                S.op("dve", lambda e, ci=ci: e.tensor_tensor_scan(hs[:, ci, 0:w], aa[:, ci, 0:w], uu[:, ci, 0:w],
                                     carry[:, ci:ci + 1], ALU.mult, ALU.add), R=[Taa2[ci], Tii2[ci], Tcar], W=[Ths2[ci]])
                                S.op("act", lambda e, ci=ci: e.activation(out=carry[:, ci:ci + 1], in_=hs[:, ci, w - 1:w], func=AF.Copy),
                                     R=[Ths2[ci]], W=[Tcar])
                                if last_p:
                                    S.op("act", lambda e, ci=ci, c=c: e.activation(out=OSTV[:, c, 4 * j, 0:1], in_=hs[:, ci, w - 1:w],
                                         func=AF.Copy), R=[Ths2[ci]], W=[Tost])
                            else:
                                a3 = aa[:, ci, 0:w].rearrange("p (s t) -> p s t", t=TS)
                                u3 = uu[:, ci, 0:w].rearrange("p (s t) -> p s t", t=TS)
                                h3 = hs[:, ci, 0:w].rearrange("p (s t) -> p s t", t=TS)
                                S.op("dve", lambda e, a3=a3, c=c: e.tensor_tensor(t16[:, :], a3[:, :, 0], h0s[:, c, :], ALU.mult),
                                     R=[Taa2[ci], Th0s], W=[Tt16])
                                S.op("dve", lambda e, u3=u3: e.tensor_tensor(u3[:, :, 0], u3[:, :, 0], t16[:, :], ALU.add),
                                     R=[Tt16, Tii2[ci]], W=[Tii2[ci]])
                                S.op("dve", lambda e, a3=a3: e.memset(a3[:, :, 0], 0.0), R=[Tt16], W=[Taa2[ci]])
                                S.op("dve", lambda e, ci=ci: e.tensor_tensor_scan(hs[:, ci, 0:w], aa[:, ci, 0:w], uu[:, ci, 0:w],
                                     0.0, ALU.mult, ALU.add), R=[Taa2[ci], Tii2[ci]], W=[Ths2[ci]])
                                S.op("act", lambda e, h3=h3, c=c: e.activation(out=OSTV[:, c, 4 * j, 1:NSEQ], in_=h3[:, :, TS - 1],
                                     func=AF.Copy), R=[Ths2[ci]], W=[Tost])
                        for ci in range(2):
                            bk = 4 + ci
                            S.op("act", lambda e, bk=bk, ci=ci: e.activation(out=xc[:, ci, 0:w], in_=ps[bk][:, 0:w], func=AF.Square),
                                 R=[Tps[bk]], W=[Txc2[ci]])
                        for ci in range(2):
                            bk = 4 + ci
                            S.op("dve", lambda e, ci=ci: e.tensor_scalar(xc[:, ci, 0:w], xc[:, ci, 0:w], 0.044715, 1.0, ALU.mult, ALU.add),
                                 R=[Txc2[ci]], W=[Txc2[ci]])
                            S.op("dve", lambda e, bk=bk, ci=ci: e.tensor_tensor(xc[:, ci, 0:w], xc[:, ci, 0:w], ps[bk][:, 0:w], ALU.mult),
                                 R=[Txc2[ci], Tps[bk]], W=[Txc2[ci]])
                        for ci in range(2):
                            S.op("act", lambda e, ci=ci: e.activation(out=xc[:, ci, 0:w], in_=xc[:, ci, 0:w], func=AF.Sigmoid,
                                 scale=1.5957691216057308), R=[Txc2[ci]], W=[Txc2[ci]])
                        for ci in range(2):
                            bk = 4 + ci; c = 2 * n + ci
                            S.op("dve", lambda e, bk=bk, ci=ci: e.tensor_tensor(xc[:, ci, 0:w], xc[:, ci, 0:w], ps[bk][:, 0:w], ALU.mult),
                                 R=[Txc2[ci], Tps[bk]], W=[Txc2[ci]])
                            S.op("dve", lambda e, ci=ci, c=c: e.tensor_tensor(yin[:, c, c0:c0 + w], xc[:, ci, 0:w], hs[:, ci, 0:w], ALU.mult),
                                 R=[Txc2[ci], Ths2[ci]], W=[Tyin[ti]])
                        wprev = w
                ph2.close()
                wo = pb_("l_wo", [128, KC, D], BF16); Two = Tl("wo")
                tmp = pb_("l_tmp2", [128, 512]); Ttmp = Tl("tmp2")
                S.barrier()
                S.dma("pool", wo[:, :, :], A["lru_w_out"][j].rearrange("(k p) n -> p k n", p=128), W=[Two])
                for ti in range(NT):
                    c0, w, kind = tiles[ti]
                    for dc in range(KC):
                        bk = dc % 2
                        for k in range(KC):
                            S.op("pe", lambda e, k=k, dc=dc, bk=bk: e.matmul(ps[bk][:, 0:w], wo[:, k, dc * 128:(dc + 1) * 128],
                                 yin[:, k, c0:c0 + w], start=(k == 0), stop=(k == KC - 1)), R=[Two, Tyin[ti]], W=[Tps[bk]],
                                 sig=(k == KC - 1))
                        resid_update(ti, dc, ps[bk][:, 0:w], Tps[bk], 5, tmp, Ttmp)


        def phase_rwkv(layer):
            jl = layer // 2
            scr = SCR[jl]
            HW_ = 1 + TP + NS * (1 + TS)
            NPT = sum(1 for t_ in tiles if t_[2] == "p")
            with ExitStack() as ph:
                def pb_(name, shape, dt=F32):
                    UID[0] += 1; return ph.enter_context(nc.sbuf_tensor("%s_%d" % (name, UID[0]), list(shape), dt))
                h = pb_("r_h", [128, KC, HW_], BF16); Th = Tl("h")
                xm = pb_("r_xm", [128, KC, T], BF16); Txm = Tl("xm")
                shs = pb_("r_shs", [128, KC, NS]); Tshs = Tl("shs")

                def hS(c):
                    return h[:, c, 1 + TP:HW_].rearrange("p (s t) -> p s t", t=1 + TS)
                with ExitStack() as ph1:
                    UID[0] += 1
                    sq = ph1.enter_context(nc.sbuf_tensor("r_sq_%d" % UID[0], [128, 2, 512], BF16)); Tsq = [Tl("sq0"), Tl("sq1")]
                    rstd2 = ph1.enter_context(nc.sbuf_tensor("r_rstd_%d" % UID[0], [128, 2, 512], F32)); Trstd2 = [Tl("r0"), Tl("r1")]
                    tmp2 = ph1.enter_context(nc.sbuf_tensor("r_tmp_%d" % UID[0], [128, 2, 512], F32)); Ttmp2 = [Tl("tmp0"), Tl("tmp1")]
                    stg = ph1.enter_context(nc.sbuf_tensor("r_stg_%d" % UID[0], [NS * 3, D], F32)); Tstg = Tl("stg")
                    S.barrier()
                    load_T(A["s_shift"][jl], NS, lambda c: shs[:, c, :], [Tshs], stg, Tstg)
                    S.op("dve", lambda e: e.memset(h[:, :, 0:1], 0.0), W=[Th])
                    for c in range(KC):
                        S.op("dve", lambda e, c=c: e.tensor_copy(hS(c)[:, :, 0], shs[:, c, :]), R=[Tshs], W=[Th])
                    def outf_t(ti):
                        c0, w, kind = tiles[ti]
                        if kind == "p":
                            return lambda c, c0=c0, w=w: h[:, c, 1 + c0:1 + c0 + w]
                        return lambda c: hS(c)[:, :, 1:1 + TS]

                    def extra_t(ti):
                        c0, w, kind = tiles[ti]

                        def extra(c, tmp, Ttmp):
                            if kind == "p" and ti == NPT - 1:
                                S.op("act", lambda e: e.activation(out=OSTV[:, c, 8 + jl, 0:1], in_=tmp[:, w - 1:w], func=AF.Identity,
                                     scale=modT[:, 4 * 8 + c, 0:1], bias=modT[:, 3 * 8 + c, 0:1]), R=[Ttmp, Tmod], W=[Tost])
                            if kind == "s":
                                S.op("dve", lambda e: e.tensor_tensor(OSTV[:, c, 8 + jl, 1:NSEQ], tv(tmp[:, 0:w], kind)[:, :, TS - 1],
                                     modT[:, 3 * 8 + c, 1:NSEQ], ALU.add), R=[Ttmp, Tmod], W=[Tost])
                        return extra
                    norm_mod_all(3, 4, sq, Tsq, rstd2, Trstd2, tmp2, Ttmp2, outf_t, lambda ti: (lambda c: [Th]), extra_t)
                ph2 = ExitStack()

                def pb2(name, shape, dt=F32):
                    UID[0] += 1; return ph2.enter_context(nc.sbuf_tensor("%s_%d" % (name, UID[0]), list(shape), dt))
                stage = pb2("r_stage", [128, 2, T]); Tstage = [Tl("st0"), Tl("st1")]
                xtmp2 = pb2("r_xtmp", [128, 2, 512]); Txt2 = [Tl("xt0"), Tl("xt1")]; xcnt = [0]
                wpc = pb2("r_wpc", [128, 2, KC, 256], BF16); Twpc = [Tl("wp0"), Tl("wp1")]
                wl1 = pb2("r_wl1", [128, KC, 160], BF16); Twl1 = Tl("wl1")
                wl2 = pb2("r_wl2", [128, 2, D], BF16); Twl2 = Tl("wl2")
                mid = pb2("r_mid", [128, 2, T], BF16); Tmid = Tl("mid")
                S.barrier()
                cnt = {"bank": 0, "st": 0, "pc": 0}

                def project(name, mm_fn, evac_fn):
                    for oc in range(KC):
                        sbi = cnt["st"] % 2; cnt["st"] += 1
                        for ti in range(NT):
                            c0, w, kind = tiles[ti]
                            bk = cnt["bank"] % 4; cnt["bank"] += 1
                            mms = mm_fn(oc, ti)
                            for i, (l_, r_, Rt) in enumerate(mms):
                                S.op("pe", lambda e, l_=l_, r_=r_, i=i, bk=bk, w=w: e.matmul(ps[bk][:, 0:w], l_, r_, start=(i == 0),
                                     stop=(i == len(mms) - 1)), R=Rt, W=[Tps[bk]], sig=(i == len(mms) - 1))
                            eng, fn, Rx = evac_fn(oc, ps[bk][:, 0:w], stage[:, sbi, c0:c0 + w])
                            S.op(eng, fn, R=[Tps[bk]] + Rx, W=[Tstage[sbi]])
                        S.dma("sp", scr[name][oc], stage[:, sbi, :], R=[Tstage[sbi]])

                def copy_evac(oc, src, dst):
                    return ("act", lambda e: e.activation(out=dst, in_=src, func=AF.Copy), [])

                def lora(w1ap, n1, w2ap, midf, outf, bias_col, name):
                    S.dma("pool", wl1[:, :, 0:n1], w1ap.rearrange("(k p) n -> p k n", p=128), W=[Twl1])
                    parts = [(0, min(n1, 128))] + ([(128, n1 - 128)] if n1 > 128 else [])
                    for pi, (r0, rn) in enumerate(parts):
                        S.dma("pool", wl2[0:rn, pi, :], w2ap[r0:r0 + rn, :], W=[Twl2])
                        for ti in range(NT):
                            c0, w, kind = tiles[ti]
                            bk = cnt["bank"] % 4; cnt["bank"] += 1
                            for k in range(KC):
                                S.op("pe", lambda e, k=k, bk=bk, r0=r0, rn=rn, c0=c0, w=w: e.matmul(ps[bk][0:rn, 0:w], wl1[:, k, r0:r0 + rn],
                                     xm[:, k, c0:c0 + w], start=(k == 0), stop=(k == KC - 1)), R=[Twl1, Txm], W=[Tps[bk]], sig=(k == KC - 1))
                            S.op("act", lambda e, bk=bk, rn=rn, pi=pi, c0=c0, w=w: e.activation(out=mid[0:rn, pi, c0:c0 + w],
                                 in_=ps[bk][0:rn, 0:w], func=midf), R=[Tps[bk]], W=[Tmid])

                    def mm_fn(oc, ti):
                        c0, w, kind = tiles[ti]
                        return [(wl2[0:rn, pi, oc * 128:(oc + 1) * 128], mid[0:rn, pi, c0:c0 + w], [Twl2, Tmid])
                                for pi, (r0, rn) in enumerate(parts)]

                    def evac_fn(oc, src, dst):
                        if bias_col is None:
                            return ("act", lambda e: e.activation(out=dst, in_=src, func=outf), [])
                        bc = bias_col + oc
                        return ("act", lambda e: e.activation(out=dst, in_=src, func=outf, bias=vecs[:, bc:bc + 1], scale=1.0), [Tvec])
                    project(name, mm_fn, evac_fn)

                for pj in range(6):
                    for c in range(KC):
                        muc = VOFF["rwkv_mu"] + jl * 48 + pj * 8 + c
                        omc = XO_OMMU + jl * 48 + pj * 8 + c
                        for (pc0, pw, pk) in tiles:
                            if pk != "p":
                                continue
                            xp_ = xcnt[0] % 2; xcnt[0] += 1
                            xtmp = xtmp2[:, xp_, :]; Txt = Txt2[xp_]
                            S.op("act", lambda e, c=c, muc=muc, pc0=pc0, pw=pw, xtmp=xtmp: e.activation(out=xtmp[:, 0:pw], in_=h[:, c, pc0:pc0 + pw],
                                 func=AF.Identity, scale=vecs[:, muc:muc + 1]), R=[Th, Tvec], W=[Txt])
                            S.op("dve", lambda e, c=c, omc=omc, pc0=pc0, pw=pw, xtmp=xtmp: e.scalar_tensor_tensor(xm[:, c, pc0:pc0 + pw],
                                 h[:, c, 1 + pc0:1 + pc0 + pw], vecs[:, omc:omc + 1], xtmp[:, 0:pw], ALU.mult, ALU.add),
                                 R=[Th, Txt, Tvec], W=[Txm])
                        xp_ = xcnt[0] % 2; xcnt[0] += 1
                        xtmp = xtmp2[:, xp_, :]; Txt = Txt2[xp_]
                        xs3 = xtmp[:, 0:NS * TS].rearrange("p (s t) -> p s t", t=TS)
                        S.op("act", lambda e, c=c, muc=muc, xs3=xs3: e.activation(out=xs3, in_=hS(c)[:, :, 0:TS], func=AF.Identity,
                             scale=vecs[:, muc:muc + 1]), R=[Th, Tvec], W=[Txt])
                        S.op("dve", lambda e, c=c, omc=omc, xs3=xs3: e.scalar_tensor_tensor(
                             xm[:, c, TP:T].rearrange("p (s t) -> p s t", t=TS), hS(c)[:, :, 1:1 + TS], vecs[:, omc:omc + 1], xs3,
                             ALU.mult, ALU.add), R=[Th, Txt, Tvec], W=[Txm])
                    if pj < 3:
                        wv = A["rwkv_w_rkv"][jl, pj].rearrange("(k p) n -> p k n", p=128)
                        slots = {}

                        def mm_fn(oc, ti, wv=wv, slots=slots):
                            c0, w, kind = tiles[ti]
                            pc, q = oc // 2, oc % 2
                            if pc not in slots:
                                sl = cnt["pc"] % 2; cnt["pc"] += 1
                                S.dma("pool", wpc[:, sl, :, :], wv[:, :, pc * 256:(pc + 1) * 256], W=[Twpc[sl]])
                                slots[pc] = sl
                            sl = slots[pc]
                            return [(wpc[:, sl, k, q * 128:(q + 1) * 128], xm[:, k, c0:c0 + w], [Twpc[sl], Txm]) for k in range(KC)]
                        project(("r", "k", "v")[pj], mm_fn, copy_evac)
                        if pj == 2 and jl == 1:
                            lora(A["rwkv_v1"][0], 32, A["rwkv_v2"][0], AF.Copy, AF.Sigmoid, VOFF["rwkv_v0"], "sv")
                    elif pj == 3:
                        lora(A["rwkv_w1"][jl], 64, A["rwkv_w2"][jl], AF.Tanh, AF.Sigmoid, VOFF["rwkv_w0"] + jl * 8, "sw")
                    elif pj == 4:
                        lora(A["rwkv_a1"][jl], 64, A["rwkv_a2"][jl], AF.Copy, AF.Sigmoid, VOFF["rwkv_a0"] + jl * 8, "a")
                    else:
                        lora(A["rwkv_g1"][jl], 160, A["rwkv_g2"][jl], AF.Sigmoid, AF.Copy, None, "g")
                ph2.close()
            tilesB = []
            for ti_, (c0_, w_, k_) in enumerate(tiles):
                if k_ == "p":
                    for o_ in range(0, w_, 256):
                        tilesB.append((ti_, c0_ + o_, min(256, w_ - o_), "p", False))
                else:
                    tilesB.append((ti_, c0_, w_, "s", False))
            lp_ = max(i for i, t_ in enumerate(tilesB) if t_[3] == "p")
            tilesB[lp_] = tilesB[lp_][:4] + (True,)
            with ExitStack() as ph:
                def pb_(name, shape, dt=F32):
                    UID[0] += 1; return ph.enter_context(nc.sbuf_tensor("%s_%d" % (name, UID[0]), list(shape), dt))

                def alloc_stream():
                    B = {}
                    B["Lb"] = (pb_("b_L", [128, 2, 6, 256]), [Tl("L0"), Tl("L1")])
                    Wk = {}
                    for nm in ("cum", "ecx", "ecp", "ecn", "etc", "kkn", "kmod", "bv", "tq", "tk"):
                        Wk[nm] = (pb_("b_" + nm, [128, 256]), Tl(nm))
                    Wk["Yt"] = Wk["cum"]; Wk["cen"] = Wk["ecx"]; Wk["bon"] = Wk["etc"]
                    B["Wk"] = Wk
                    B["AR"] = (pb_("b_AR", [128, 512], BF16), Tl("AR"))
                    B["BK"] = (pb_("b_BK", [128, 512], BF16), Tl("BK"))
                    B["BKH"] = (pb_("b_BKH", [128, 3, 256], BF16), Tl("BKH"))
                    B["TM"] = (pb_("b_TM", [128, 3072], BF16), Tl("TM"))
                    Gb = {}
                    for nm in ("Lab", "LabT", "Xa", "XTa", "Ta", "Tb"):
                        Gb[nm] = (pb_("b_" + nm, [128, 256]), Tl(nm))
                    for nm in ("Mbr", "Lkb", "Mkr", "Tbf"):
                        Gb[nm] = (pb_("b_" + nm, [128, 256], BF16), Tl(nm))
                    B["Gb"] = Gb
                    B["WTb"] = (pb_("b_WTb", [128, 8, 64], BF16), Tl("WTb"))
                    B["PTb"] = (pb_("b_PTb", [128, 8, 64], BF16), Tl("PTb"))
                    B["Sio"] = (pb_("b_Sio", [128, NS, 64]), Tl("Sio"))
                    B["STs"] = (pb_("b_STs", [128, NS, 64]), Tl("STs"))
                    B["STsb"] = (pb_("b_STsb", [128, NS, 64], BF16), Tl("STsb"))
                    B["eg"] = (pb_("b_eg", [128, 16]), Tl("eg"))
                    B["ybc"] = (pb_("b_ybc", [128, 256], BF16), Tl("ybc"))
                    B["woc"] = (pb_("b_woc", [128, D], BF16), Tl("woc"))
                    B["rtmp"] = (pb_("b_rtmp", [128, 256]), Tl("rtmp"))
                    return B
                BFS = [alloc_stream(), alloc_stream()]
                S.barrier()
                NEG = -float(np.exp(-0.5))

                def stream(c, B, si):
                    Lb2, TL2 = B["Lb"]; Wk = B["Wk"]; AR, TAR = B["AR"]; BK, TBK = B["BK"]; BKH, TBKH = B["BKH"]; TM, TTM = B["TM"]
                    Gb = B["Gb"]; WTb, TWTb = B["WTb"]; PTb, TPTb = B["PTb"]; Sio, TSio = B["Sio"]; STs, TST = B["STs"]
                    STsb, TSTb = B["STsb"]; eg, Teg = B["eg"]; ybc, Tybc = B["ybc"]; woc, Twoc = B["woc"]; rtmp, Trtmp = B["rtmp"]
                    tq3s = Sio; Ttq3s = TSio
                    b0, b1, b2 = (0, 1, 2) if si == 0 else (3, 4, 5)
                    S.dma("pool", woc[:, :], A["rwkv_w_o"][jl, c * 128:(c + 1) * 128, :], W=[Twoc])
                    kkc = VOFF["rwkv_k_k"] + jl * 8 + c; kac = VOFF["rwkv_k_a"] + jl * 8 + c; omka = XO_OMKA + jl * 8 + c
                    rkc = VOFF["rwkv_r_k"] + jl * 8 + c; lnw = VOFF["rwkv_ln_w"] + jl * 8 + c; lnb = VOFF["rwkv_ln_b"] + jl * 8 + c
                    S.op("dve", lambda e: e.memset(STs[:, 0:1, :], 0.0), W=[TST])
                    S.op("dve", lambda e: e.memset(STsb[:, 0:1, :], 0.0), W=[TSTb])
                    for sub_i, (ti, c0, w, kind, lastp) in enumerate(tilesB):
                        yield
                        Lb = Lb2[:, sub_i % 2, :, :]; TL = TL2[sub_i % 2]
                        C = 64 if kind == "p" else TS
                        nch = w // C
                        gsz = 4 if kind == "p" else 16
                        gsz = min(gsz, nch)
                        v3 = lambda ap: ap[:, 0:w].rearrange("p (n t) -> p n t", t=C)
                        ARv = AR[:, 0:nch * 2 * C].rearrange("p (n a t) -> p n a t", a=2, t=C)
                        BKv = BK[:, 0:nch * 2 * C].rearrange("p (n a t) -> p n a t", a=2, t=C)
                        TMv = TM[:, 0:nch * 192].rearrange("p (n a k) -> p n a k", a=3, k=64)
                        Gv = lambda nm: Gb[nm][0][:, 0:nch * C].rearrange("p (n t) -> p n t", t=C)
                        cmo = 0 if kind == "p" else 512
                        Lr, Lk, Lv, Lsw, La, Lg = [Lb[:, i, 0:w] for i in range(6)]
                        W_ = lambda nm: Wk[nm][0][:, 0:w]
                        TW = lambda nm: Wk[nm][1]
                        if kind == "s":
                            S.dma("sp", Sio[:, :, :], A["s_wkv"][jl, :, 2 * c:2 * c + 2, :, :].rearrange("s h v k -> (h v) s k"), W=[TSio])
                            for half in range(2):
                                bk = (b0, b1)[half]
                                for l in range(8):
                                    s_ = half * 8 + l
                                    for p in range(2):
                                        kr = slice(p * 64, (p + 1) * 64)
                                        S.op("pe", lambda e, l=l, s_=s_, kr=kr, bk=bk, p=p: e.matmul(ps[bk][kr, l * 64:(l + 1) * 64],
                                             Sio[kr, s_, :], identf[kr, kr], start=True, stop=True), R=[TSio, Tconst], W=[Tps[bk]],
                                             sig=(l == 7 and p == 1))
                                S.op("act", lambda e, half=half, bk=bk: e.activation(out=STs[:, half * 8:half * 8 + 8, :],
                                     in_=ps[bk][:, :].rearrange("p (s v) -> p s v", v=64), func=AF.Copy), R=[Tps[bk]], W=[TST])
                            S.op("act", lambda e: e.activation(out=STsb[:, :, :], in_=STs[:, :, :], func=AF.Copy), R=[TST], W=[TSTb])
                        names = ["r", "k", "v", "sw", "a", "g"]
                        S.dma_group("sp", [(Lb[:, i, 0:w], scr[nm][c, :, c0:c0 + w]) for i, nm in enumerate(names)], W=[TL])
                        if jl == 1:
                            S.dma("sp", W_("kkn"), scr["sv"][c, :, c0:c0 + w], W=[TW("kkn")])
                            S.dma("sp", W_("kmod"), SCR[0]["v"][c, :, c0:c0 + w], W=[TW("kmod")])
                            S.op("dve", lambda e: e.tensor_tensor(W_("tq"), W_("kmod"), Lv, ALU.subtract), R=[TL, TW("kmod")], W=[TW("tq")])
                            S.op("dve", lambda e: e.tensor_tensor(W_("tq"), W_("tq"), W_("kkn"), ALU.mult), R=[TW("kkn"), TW("tq")], W=[TW("tq")])
                            S.op("dve", lambda e: e.tensor_tensor(Lv, Lv, W_("tq"), ALU.add), R=[TL, TW("tq")], W=[TL])
                        yield
                        S.op("dve", lambda e: e.tensor_scalar(Lsw, Lsw, NEG, None, ALU.mult), R=[TL], W=[TL])
                        S.op("dve", lambda e: e.tensor_tensor_scan(W_("cum"), cmk[:, cmo:cmo + w], Lsw, 0.0, ALU.mult, ALU.add),
                             R=[TL, Tconst], W=[TW("cum")])
                        S.op("dve", lambda e: e.tensor_tensor(W_("tq"), W_("cum"), Lsw, ALU.subtract), R=[TL, TW("cum")], W=[TW("tq")])
                        yield
                        S.op("act", lambda e: e.activation(out=W_("ecx"), in_=W_("tq"), func=AF.Exp), R=[TW("tq")], W=[TW("ecx")])
                        S.op("act", lambda e: e.activation(out=W_("ecp"), in_=W_("cum"), func=AF.Exp), R=[TW("cum")], W=[TW("ecp")])
                        S.op("act", lambda e: e.activation(out=W_("ecn"), in_=W_("cum"), func=AF.Exp, scale=-1.0), R=[TW("cum")], W=[TW("ecn")])
                        S.op("dve", lambda e: e.tensor_tensor(v3(Wk["tq"][0]), v3(Wk["cum"][0])[:, :, C - 1:C].to_broadcast([128, nch, C]),
                             v3(Wk["cum"][0]), ALU.subtract), R=[TW("cum"), TW("ecx")], W=[TW("tq")])
                        S.op("act", lambda e: e.activation(out=W_("etc"), in_=W_("tq"), func=AF.Exp), R=[TW("tq")], W=[TW("etc")])
                        S.op("act", lambda e: e.activation(out=eg[:, 0:nch], in_=v3(Wk["cum"][0])[:, :, C - 1], func=AF.Exp),
                             R=[TW("cum")], W=[Teg])
                        yield
                        S.op("dve", lambda e: e.tensor_scalar(W_("kkn"), Lk, vecs[:, kkc:kkc + 1], None, ALU.mult), R=[TL, Tvec], W=[TW("kkn")])
                        S.op("act", lambda e: e.activation(out=W_("tk"), in_=W_("kkn"), func=AF.Square), R=[TW("kkn")], W=[TW("tk")])
                        S.op("pe", lambda e: e.matmul(ps[b0][:, 0:w], blk1[:], W_("tk"), start=True, stop=True), R=[TW("tk"), Tconst], W=[Tps[b0]])
                        yield
                        S.op("dve", lambda e: e.tensor_scalar(W_("tk"), ps[b0][:, 0:w], 1e-24, None, ALU.max), R=[Tps[b0]], W=[TW("tk")])
                        S.op("act", lambda e: e.activation(out=W_("tk"), in_=W_("tk"), func=AF.Ln), R=[TW("tk")], W=[TW("tk")])
                        S.op("act", lambda e: e.activation(out=W_("tk"), in_=W_("tk"), func=AF.Exp, scale=-0.5), R=[TW("tk")], W=[TW("tk")])
                        S.op("dve", lambda e: e.tensor_tensor(W_("kkn"), W_("kkn"), W_("tk"), ALU.mult), R=[TW("tk"), TW("kkn")], W=[TW("kkn")])
                        yield
                        S.op("dve", lambda e: e.tensor_scalar(W_("kmod"), La, vecs[:, kac:kac + 1], vecs[:, omka:omka + 1], ALU.mult, ALU.add),
                             R=[TL, Tvec], W=[TW("kmod")])
                        S.op("dve", lambda e: e.tensor_tensor(W_("kmod"), W_("kmod"), Lk, ALU.mult), R=[TL, TW("kmod")], W=[TW("kmod")])
                        S.op("dve", lambda e: e.tensor_tensor(W_("bv"), W_("kkn"), La, ALU.mult), R=[TL, TW("kkn")], W=[TW("bv")])
                        yield
                        S.op("dve", lambda e: e.scalar_tensor_tensor(ARv[:, :, 0, :], v3(Wk["kkn"][0]), -1.0, v3(Wk["ecx"][0]), ALU.mult, ALU.mult),
                             R=[TW("kkn"), TW("ecx")], W=[TAR])
                        S.op("dve", lambda e: e.tensor_tensor(ARv[:, :, 1, :], v3(Lb[:, 0, :]), v3(Wk["ecp"][0]), ALU.mult), R=[TL, TW("ecp")], W=[TAR])
                        S.op("dve", lambda e: e.tensor_tensor(BKv[:, :, 0, :], v3(Wk["bv"][0]), v3(Wk["ecn"][0]), ALU.mult), R=[TW("bv"), TW("ecn")], W=[TBK])
                        S.op("dve", lambda e: e.tensor_tensor(BKv[:, :, 1, :], v3(Wk["kmod"][0]), v3(Wk["ecn"][0]), ALU.mult), R=[TW("kmod"), TW("ecn")], W=[TBK])
                        S.op("dve", lambda e: e.tensor_tensor(BKH[:, 0, 0:w], W_("bv"), W_("etc"), ALU.mult), R=[TW("bv"), TW("etc")], W=[TBKH])
                        S.op("dve", lambda e: e.tensor_tensor(BKH[:, 1, 0:w], W_("kmod"), W_("etc"), ALU.mult), R=[TW("kmod"), TW("etc")], W=[TBKH])
                        S.op("act", lambda e: e.activation(out=BKH[:, 2, 0:w], in_=Lv, func=AF.Copy), R=[TL], W=[TBKH])
                        yield
                        S.op("dve", lambda e: e.tensor_tensor(W_("bv"), Lr, W_("kmod"), ALU.mult), R=[TL, TW("kmod")], W=[TW("bv")])
                        S.op("dve", lambda e: e.tensor_scalar(W_("bv"), W_("bv"), vecs[:, rkc:rkc + 1], None, ALU.mult), R=[TW("bv"), Tvec], W=[TW("bv")])
                        S.op("pe", lambda e: e.matmul(ps[b1][:, 0:w], blk1[:], W_("bv"), start=True, stop=True), R=[TW("bv"), Tconst], W=[Tps[b1]])
                        S.op("dve", lambda e: e.tensor_tensor(W_("bon"), ps[b1][:, 0:w], Lv, ALU.mult), R=[Tps[b1], TL], W=[TW("bon")])
                        yield
                        for g0 in range(0, nch, 4):
                            gn = min(4, nch - g0)
                            for l in range(gn):
                                ch = g0 + l
                                for p in range(2):
                                    kr = slice(p * 64, (p + 1) * 64); cr = slice(p * 64, p * 64 + C)
                                    for a in range(3):
                                        S.op("pe", lambda e, l=l, ch=ch, kr=kr, cr=cr, a=a: e.transpose(
                                             psb[cr, l * 192 + a * 64:l * 192 + (a + 1) * 64], BKH[kr, a, ch * C:(ch + 1) * C], identb[kr, kr]),
                                             R=[TBKH, Tconst], W=[Tpsb], sig=(l == gn - 1 and p == 1 and a == 2))
                            S.op("act", lambda e, g0=g0, gn=gn: e.activation(out=TMv[:, g0:g0 + gn, :, :].rearrange("p n a k -> p (n a k)"),
                                 in_=psb[:, 0:gn * 192], func=AF.Copy), R=[Tpsb], W=[TTM])
                        yield
                        nlev = {64: 6, 8: 3}[C]
                        Tfin = None
                        for g0 in range(0, nch, gsz):
                            for l in range(gsz):
                                ch = g0 + l
                                for p in range(2):
                                    kr = slice(p * 64, (p + 1) * 64); cr = slice(p * 64, p * 64 + C)
                                    lastm = (l == gsz - 1 and p == 1)
                                    S.op("pe", lambda e, l=l, ch=ch, kr=kr, cr=cr: e.matmul(ps[b0][cr, l * 2 * C:(l + 1) * 2 * C], BKv[kr, ch, 0, :],
                                         ARv[kr, ch, :, :].rearrange("p a t -> p (a t)"), start=True, stop=True), R=[TBK, TAR], W=[Tps[b0]], sig=lastm)
                                    S.op("pe", lambda e, l=l, ch=ch, kr=kr, cr=cr: e.matmul(ps[b1][cr, l * 2 * C:(l + 1) * 2 * C], BKv[kr, ch, 1, :],
                                         ARv[kr, ch, :, :].rearrange("p a t -> p (a t)"), start=True, stop=True), R=[TBK, TAR], W=[Tps[b1]], sig=lastm)
                                    S.op("pe", lambda e, l=l, ch=ch, kr=kr, cr=cr: e.matmul(ps[b2][cr, l * C:(l + 1) * C], ARv[kr, ch, 0, :],
                                         BKv[kr, ch, 0, :], start=True, stop=True), R=[TBK, TAR], W=[Tps[b2]], sig=lastm)
                            yield
                            gs = slice(g0, g0 + gsz)
                            p0v = ps[b0][:, 0:gsz * 2 * C].rearrange("p (n a t) -> p n a t", a=2, t=C)
                            p1v = ps[b1][:, 0:gsz * 2 * C].rearrange("p (n a t) -> p n a t", a=2, t=C)
                            pgv = lambda i: ps[i][:, 0:gsz * C].rearrange("p (n t) -> p n t", t=C)
                            mb = lambda i: msk[:, i, 0:C].unsqueeze(1).to_broadcast([128, gsz, C])
                            S.op("dve", lambda e: e.tensor_tensor(Gv("Lab")[:, gs, :], p0v[:, :, 0, :], mb(0), ALU.mult), R=[Tps[b0], Tconst], W=[Gb["Lab"][1]])
                            S.op("dve", lambda e: e.tensor_tensor(Gv("Mbr")[:, gs, :], p0v[:, :, 1, :], mb(1), ALU.mult), R=[Tps[b0], Tconst], W=[Gb["Mbr"][1]])
                            S.op("dve", lambda e: e.tensor_tensor(Gv("Lkb")[:, gs, :], p1v[:, :, 0, :], mb(0), ALU.mult), R=[Tps[b1], Tconst], W=[Gb["Lkb"][1]])
                            S.op("dve", lambda e: e.tensor_tensor(Gv("Mkr")[:, gs, :], p1v[:, :, 1, :], mb(1), ALU.mult), R=[Tps[b1], Tconst], W=[Gb["Mkr"][1]])
                            S.op("dve", lambda e: e.tensor_tensor(Gv("LabT")[:, gs, :], pgv(b2), mb(2), ALU.mult), R=[Tps[b2], Tconst], W=[Gb["LabT"][1]])
                            yield
                            Xn, XTn, Tc, Tn = "Lab", "LabT", "Ta", "Tb"
                            Xo, XTo = "Xa", "XTa"
                            S.op("dve", lambda e, Xn=Xn, Tc=Tc: e.tensor_tensor(Gv(Tc)[:, gs, :], Gv(Xn)[:, gs, :], mb(3), ALU.add),
                                 R=[Gb[Xn][1], Tconst], W=[Gb[Tc][1]])
                            for lev in range(1, nlev):
                                lastl = (lev == nlev - 1)
                                for l in range(gsz):
                                    ch = g0 + l
                                    for p in range(2):
                                        cr = slice(p * 64, p * 64 + C)
                                        lastm = (l == gsz - 1 and p == 1)
                                        if not lastl:
                                            S.op("pe", lambda e, l=l, ch=ch, cr=cr, Xn=Xn, XTn=XTn: e.matmul(ps[b2][cr, l * C:(l + 1) * C], Gv(XTn)[cr, ch, :],
                                                 Gv(Xn)[cr, ch, :], start=True, stop=True), R=[Gb[Xn][1], Gb[XTn][1]], W=[Tps[b2]], sig=lastm)
                                        S.op("pe", lambda e, l=l, ch=ch, cr=cr, Xn=Xn, XTn=XTn: e.matmul(ps[b0][cr, l * C:(l + 1) * C], Gv(Xn)[cr, ch, :],
                                             Gv(XTn)[cr, ch, :], start=True, stop=True), R=[Gb[Xn][1], Gb[XTn][1]], W=[Tps[b0]], sig=lastm)
                                yield
                                if not lastl:
                                    S.op("act", lambda e, Xo=Xo: e.activation(out=Gv(Xo)[:, gs, :], in_=pgv(b2), func=AF.Copy), R=[Tps[b2]], W=[Gb[Xo][1]])
                                S.op("act", lambda e, XTo=XTo: e.activation(out=Gv(XTo)[:, gs, :], in_=pgv(b0), func=AF.Copy), R=[Tps[b0]], W=[Gb[XTo][1]])
                                yield
                                for l in range(gsz):
                                    ch = g0 + l
                                    for p in range(2):
                                        cr = slice(p * 64, p * 64 + C)
                                        S.op("pe", lambda e, l=l, ch=ch, cr=cr, XTo=XTo, Tc=Tc: e.matmul(ps[b1][cr, l * C:(l + 1) * C], Gv(XTo)[cr, ch, :],
                                             Gv(Tc)[cr, ch, :], start=True, stop=True), R=[Gb[XTo][1], Gb[Tc][1]], W=[Tps[b1]], sig=(l == gsz - 1 and p == 1))
                                S.op("dve", lambda e, Tc=Tc, Tn=Tn: e.tensor_tensor(Gv(Tn)[:, gs, :], pgv(b1), Gv(Tc)[:, gs, :], ALU.add),
                                     R=[Tps[b1], Gb[Tc][1]], W=[Gb[Tn][1]])
                                yield
                                Xn, Xo = Xo, Xn
                                XTn, XTo = XTo, XTn
                                Tc, Tn = Tn, Tc
                            S.op("act", lambda e, Tc=Tc: e.activation(out=Gv("Tbf")[:, gs, :], in_=Gv(Tc)[:, gs, :], func=AF.Copy),
                                 R=[Gb[Tc][1]], W=[Gb["Tbf"][1]])
                            Tfin = "Tbf"
                        yield
                        sets = [[ch] for ch in range(nch)] if kind == "p" else [list(range(0, 8)), list(range(8, 16))]
                        for chs in sets:
                            n = len(chs)
                            for l, ch in enumerate(chs):
                                si = 0 if kind == "p" else ch
                                for p in range(2):
                                    kr = slice(p * 64, (p + 1) * 64); cr = slice(p * 64, p * 64 + C)
                                    S.op("pe", lambda e, l=l, ch=ch, si=si, kr=kr, cr=cr: e.matmul(ps[b2][cr, l * 64:(l + 1) * 64], ARv[kr, ch, 0, :],
                                         STsb[kr, si, :], start=True, stop=False), R=[TAR, TSTb], W=[Tps[b2]], sig=False)
                                    S.op("pe", lambda e, l=l, ch=ch, cr=cr: e.matmul(ps[b2][cr, l * 64:(l + 1) * 64], Gv("Lkb")[cr, ch, :],
                                         TMv[cr, ch, 2, :], start=False, stop=True), R=[Gb["Lkb"][1], TTM], W=[Tps[b2]], sig=(l == n - 1 and p == 1))
                            yield
                            S.op("act", lambda e, n=n: e.activation(out=WTb[:, 0:n, :].rearrange("p n v -> p (n v)"), in_=ps[b2][:, 0:n * 64], func=AF.Copy),
                                 R=[Tps[b2]], W=[TWTb])
                            for l, ch in enumerate(chs):
                                for p in range(2):
                                    cr = slice(p * 64, p * 64 + C)
                                    S.op("pe", lambda e, l=l, ch=ch, cr=cr: e.matmul(ps[b0][cr, l * 64:(l + 1) * 64], Gv(Tfin)[cr, ch, :], WTb[cr, l, :],
                                         start=True, stop=True), R=[Gb[Tfin][1], TWTb], W=[Tps[b0]], sig=(l == n - 1 and p == 1))
                            yield
                            S.op("dve", lambda e, n=n: e.tensor_copy(PTb[:, 0:n, :].rearrange("p n v -> p (n v)"), ps[b0][:, 0:n * 64]), R=[Tps[b0]], W=[TPTb])
                            for l, ch in enumerate(chs):
                                si = 0 if kind == "p" else ch
                                for p in range(2):
                                    kr = slice(p * 64, (p + 1) * 64); cr = slice(p * 64, p * 64 + C)
                                    lastm = (l == n - 1 and p == 1)
                                    S.op("pe", lambda e, l=l, ch=ch, si=si, kr=kr: e.matmul(ps[b1][kr, ch * C:(ch + 1) * C], STsb[kr, si, :], ARv[kr, ch, 1, :],
                                         start=True, stop=False), R=[TSTb, TAR], W=[Tps[b1]], sig=False)
                                    S.op("pe", lambda e, l=l, ch=ch, kr=kr, cr=cr: e.matmul(ps[b1][kr, ch * C:(ch + 1) * C], PTb[cr, l, :], Gv("Mbr")[cr, ch, :],
                                         start=False, stop=False), R=[TPTb, Gb["Mbr"][1]], W=[Tps[b1]], sig=False)
                                    S.op("pe", lambda e, l=l, ch=ch, kr=kr, cr=cr: e.matmul(ps[b1][kr, ch * C:(ch + 1) * C], TMv[cr, ch, 2, :], Gv("Mkr")[cr, ch, :],
                                         start=False, stop=True), R=[TTM, Gb["Mkr"][1]], W=[Tps[b1]], sig=lastm)
                                    S.op("pe", lambda e, l=l, ch=ch, kr=kr, cr=cr: e.matmul(ps[b2][kr, l * 64:(l + 1) * 64], TMv[cr, ch, 0, :], PTb[cr, l, :],
                                         start=True, stop=False), R=[TTM, TPTb], W=[Tps[b2]], sig=False)
                                    S.op("pe", lambda e, l=l, ch=ch, kr=kr, cr=cr: e.matmul(ps[b2][kr, l * 64:(l + 1) * 64], TMv[cr, ch, 1, :], TMv[cr, ch, 2, :],
                                         start=False, stop=True), R=[TTM], W=[Tps[b2]], sig=lastm)
                            yield
                            ch0 = chs[0]
                            if kind == "p":
                                S.op("dve", lambda e, ch0=ch0: e.scalar_tensor_tensor(STs[:, 0, :], STs[:, 0, :], eg[:, ch0:ch0 + 1], ps[b2][:, 0:64],
                                     ALU.mult, ALU.add), R=[TST, Teg, Tps[b2]], W=[TST])
                                S.op("act", lambda e: e.activation(out=STsb[:, 0, :], in_=STs[:, 0, :], func=AF.Copy), R=[TST], W=[TSTb])
                            else:
                                S.op("dve", lambda e, ch0=ch0, n=n: e.tensor_tensor(tq3s[:, 0:n, :], STs[:, ch0:ch0 + n, :],
                                     eg[:, ch0:ch0 + n].unsqueeze(2).to_broadcast([128, n, 64]), ALU.mult), R=[TST, Teg], W=[Ttq3s])
                                S.op("dve", lambda e, ch0=ch0, n=n: e.tensor_tensor(STs[:, ch0:ch0 + n, :], tq3s[:, 0:n, :],
                                     ps[b2][:, 0:n * 64].rearrange("p (n v) -> p n v", v=64), ALU.add), R=[Ttq3s, Tps[b2], TST], W=[TST])
                                S.op("act", lambda e, ch0=ch0, n=n: e.activation(out=STsb[:, ch0:ch0 + n, :], in_=STs[:, ch0:ch0 + n, :], func=AF.Copy),
                                     R=[TST], W=[TSTb])
                        yield
                        S.op("act", lambda e: e.activation(out=W_("Yt"), in_=ps[b1][:, 0:w], func=AF.Copy), R=[Tps[b1]], W=[TW("Yt")])
                        S.op("pe", lambda e: e.matmul(ps[b0][:, 0:w], blk64[:], W_("Yt"), start=True, stop=True), R=[TW("Yt"), Tconst], W=[Tps[b0]])
                        S.op("dve", lambda e: e.tensor_tensor(W_("cen"), W_("Yt"), ps[b0][:, 0:w], ALU.subtract), R=[TW("Yt"), Tps[b0]], W=[TW("cen")])
                        S.op("act", lambda e: e.activation(out=W_("tq"), in_=W_("cen"), func=AF.Square), R=[TW("cen")], W=[TW("tq")])
                        S.op("pe", lambda e: e.matmul(ps[b1][:, 0:w], blk64[:], W_("tq"), start=True, stop=True), R=[TW("tq"), Tconst], W=[Tps[b1]])
                        yield
                        S.op("act", lambda e: e.activation(out=W_("tq"), in_=ps[b1][:, 0:w], func=AF.Ln, bias=64e-5, scale=1.0), R=[Tps[b1]], W=[TW("tq")])
                        S.op("act", lambda e: e.activation(out=W_("tq"), in_=W_("tq"), func=AF.Exp, scale=-0.5), R=[TW("tq")], W=[TW("tq")])
                        S.op("dve", lambda e: e.tensor_tensor(W_("cen"), W_("cen"), W_("tq"), ALU.mult), R=[TW("tq"), TW("cen")], W=[TW("cen")])
                        S.op("dve", lambda e: e.tensor_scalar(W_("cen"), W_("cen"), vecs[:, lnw:lnw + 1], vecs[:, lnb:lnb + 1], ALU.mult, ALU.add),
                             R=[TW("cen"), Tvec], W=[TW("cen")])
                        S.op("dve", lambda e: e.tensor_tensor(W_("cen"), W_("cen"), W_("bon"), ALU.add), R=[TW("cen"), TW("bon")], W=[TW("cen")])
                        S.op("dve", lambda e: e.tensor_tensor(ybc[:, 0:w], W_("cen"), Lg, ALU.mult), R=[TW("cen"), TL], W=[Tybc])
                        yield
                        for dc in range(KC):
                            bk = (b0, b1)[dc % 2]
                            S.op("pe", lambda e, dc=dc, bk=bk: e.matmul(ps[bk][:, 0:w], woc[:, dc * 128:(dc + 1) * 128], ybc[:, 0:w], start=True, stop=True),
                                 R=[Twoc, Tybc], W=[Tps[bk]])
                            resid_update(ti, dc, ps[bk][:, 0:w], Tps[bk], 5, rtmp, Trtmp, sub=(c0, w, kind))
                            if dc % 2 == 1:
                                yield
                        if lastp:
                            for p in range(2):
                                kr = slice(p * 64, (p + 1) * 64)
                                S.op("pe", lambda e, kr=kr: e.matmul(ps[b0][kr, 0:64], STs[kr, 0, :], identf[kr, kr], start=True, stop=True),
                                     R=[TST, Tconst], W=[Tps[b0]], sig=(p == 1))
                            S.op("act", lambda e: e.activation(out=Sio[:, 0, :], in_=ps[b0][:, 0:64], func=AF.Copy), R=[Tps[b0]], W=[TSio])
                            S.dma("sp", A["o_p_wkv"][jl, 2 * c:2 * c + 2, :, :].rearrange("h v k -> (h v) k"), Sio[:, 0, :], R=[TSio])
                        if kind == "s":
                            for half in range(2):
                                bk = (b0, b1)[half]
                                for l in range(8):
                                    s_ = half * 8 + l
                                    for p in range(2):
                                        kr = slice(p * 64, (p + 1) * 64)
                                        S.op("pe", lambda e, l=l, s_=s_, kr=kr, bk=bk: e.matmul(ps[bk][kr, l * 64:(l + 1) * 64], STs[kr, s_, :],
                                             identf[kr, kr], start=True, stop=True), R=[TST, Tconst], W=[Tps[bk]], sig=(l == 7 and p == 1))
                                S.op("act", lambda e, half=half, bk=bk: e.activation(out=Sio[:, half * 8:half * 8 + 8, :],
                                     in_=ps[bk][:, :].rearrange("p (s k) -> p s k", k=64), func=AF.Copy), R=[Tps[bk]], W=[TSio])
                            S.dma("sp", A["o_s_wkv"][jl, :, 2 * c:2 * c + 2, :, :].rearrange("s h v k -> (h v) s k"), Sio[:, :, :], R=[TSio])
                for pair in range(0, KC, 2):
                    gens = [stream(pair, BFS[0], 0), stream(pair + 1, BFS[1], 1)]
                    alive = list(gens)
                    while alive:
                        for g_ in list(alive):
                            try:
                                next(g_)
                            except StopIteration:
                                alive.remove(g_)

        def phase_final():
            with ExitStack() as ph:
                def pb_(name, shape, dt=F32):
                    UID[0] += 1; return ph.enter_context(nc.sbuf_tensor("%s_%d" % (name, UID[0]), list(shape), dt))
                sq = pb_("o_sq", [128, 2, 512], BF16); Tsq = [Tl("sq0"), Tl("sq1")]
                rstd = pb_("o_rstd", [128, 512]); Trstd = Tl("rstd")
                yt = pb_("o_yt", [128, KC, 512]); Tyt = Tl("yt")
                yo = pb_("o_yo", [128, 2, D]); Tyo = [Tl("yo0"), Tl("yo1")]
                S.barrier()
                g0 = VOFF["final_gain"]
                bi = 0
                for ti in range(NT):
                    c0, w, kind = tiles[ti]
                    norm_stats(ti, sq, Tsq, rstd, Trstd)
                    for c in range(KC):
                        S.op("dve", lambda e, c=c: e.scalar_tensor_tensor(yt[:, c, 0:w], x[:, c, c0:c0 + w], vecs[:, g0 + c:g0 + c + 1],
                                                                          rstd[:, 0:w], ALU.mult, ALU.mult),
                             R=[Tx[c][ti], Trstd, Tvec], W=[Tyt])
                    for b in range(w // 128):
                        yb = bi % 2; bi += 1
                        for half in range(2):
                            pbk = half
                            for q in range(4):
                                c = half * 4 + q
                                S.op("pe", lambda e, c=c, q=q, b=b, pbk=pbk: e.transpose(ps[pbk][:, q * 128:(q + 1) * 128],
                                     yt[:, c, b * 128:(b + 1) * 128], identf[:]), R=[Tyt, Tconst], W=[Tps[pbk]], sig=(q == 3))
                            if half == 0:
                                S.op("act", lambda e, yb=yb, pbk=pbk: e.activation(out=yo[:, yb, 0:512], in_=ps[pbk][:, :], func=AF.Copy),
                                     R=[Tps[pbk]], W=[Tyo[yb]])
                            else:
                                S.op("dve", lambda e, yb=yb, pbk=pbk: e.tensor_copy(yo[:, yb, 512:1024], ps[pbk][:, :]),
                                     R=[Tps[pbk]], W=[Tyo[yb]])
                        t0 = c0 + b * 128
                        dst = A["yp"][t0:t0 + 128, :] if kind == "p" else A["ys"][:, :]
                        S.dma("sp", dst, yo[:, yb, :], R=[Tyo[yb]])
                with nc.sbuf_tensor("o_sm", [128, 2, D], F32) as osm:
                    Tosm = Tl("osm")
                    S.barrier()
                    NCOL = 10 * NSEQ
                    for blk, (b0, bn) in enumerate([(0, 128), (128, NCOL - 128)]):
                        for c in range(KC):
                            pbk = c % 2
                            S.op("pe", lambda e, c=c, b0=b0, bn=bn, pbk=pbk: e.transpose(ps[pbk][0:bn, 0:128], ost[:, c, b0:b0 + bn],
                                                                                      identf[:]), R=[Tost, Tconst], W=[Tps[pbk]])
                            S.op("act", lambda e, c=c, bn=bn, blk=blk, pbk=pbk: e.activation(
                                out=osm[0:bn, blk, c * 128:(c + 1) * 128], in_=ps[pbk][0:bn, 0:128], func=AF.Copy),
                                R=[Tps[pbk]], W=[Tosm])
                    opairs = []
                    for r in range(10):
                        col = r * NSEQ
                        blk, prow = (0, col) if col < 128 else (1, col - 128)
                        opairs.append((A["o_p_small"][r:r + 1, :], osm[prow:prow + 1, blk, :]))
                        s0_ = 0
                        while s0_ < NS:
                            col = r * NSEQ + 1 + s0_
                            blk, prow = (0, col) if col < 128 else (1, col - 128)
                            m = min(NS - s0_, (128 - prow) if blk == 0 else NS)
                            opairs.append((A["o_s_small"][r, s0_:s0_ + m, :], osm[prow:prow + m, blk, :]))
                            s0_ += m
                    S.dma_group("sp", opairs, R=[Tosm])

        phase_mod(0)
        for layer in range(DEPTH):
            modT = modTs[layer % 2]; Tmod = Tmods[layer % 2]
            phase_ffn(layer, 0)
            if layer % 2 == 0 and do_lru:
                phase_lru(layer)
            if layer % 2 == 1 and do_rwkv:
                phase_rwkv(layer)
            phase_ffn(layer, 1, co_layer=(layer + 1 if layer + 1 < DEPTH else None))
        phase_final()
        S.drain("sp")
        build.nins = S.nins
    return nc


def make_consts():
    ident = np.eye(128, dtype=np.float32)
    m = np.zeros((128, 4, 64), np.float32)
    sidx = (np.arange(128) % 64)[:, None]; t = np.arange(64)[None, :]
    m[:, 0, :] = (t > sidx)
    m[:, 1, :] = (t >= sidx)
    m[:, 2, :] = (t < sidx)
    m[:, 3, :] = (t == sidx)
    return ident, m


def pack_inputs(inp, core, TP):
    g = lambda k: np.asarray(inp[k], dtype=np.float32)
    ident, m = make_consts()
    s0, s1 = core * NS, (core + 1) * NS
    cm = np.ones((128, 640), np.float32); cm[:, 0:512:64] = 0.0; cm[:, 512:640:8] = 0.0
    d = {"c_ident": ident, "c_mask": m, "c_cm": cm,
         "xp": np.ascontiguousarray(g("x_prompt")[core, :TP]),
         "xs": np.ascontiguousarray(g("x_sample")[s0:s1].reshape(NS * TS, D)),
         "cc": np.ascontiguousarray(np.concatenate([g("c_prompt")[core:core + 1], g("c_sample")[s0:s1]], 0)),
         "s_lru_h": np.ascontiguousarray(g("state_lru_h")[:, s0:s1]),
         "s_lru_conv": np.ascontiguousarray(g("state_lru_conv")[:, s0:s1].reshape(2, NS * 3, D)),
         "s_shift": np.ascontiguousarray(g("state_rwkv_shift")[:, s0:s1]),
         "s_wkv": np.ascontiguousarray(g("state_rwkv_wkv")[:, s0:s1])}
    for n, r in VEC_SPEC:
        d[n] = np.ascontiguousarray(g(n).reshape(r, 128))
    for n in BIG_W:
        d[n] = np.ascontiguousarray(g(n))
    return d


_NC_CACHE = {}


def run_cores(inp, TP, DEPTH, cores, **kw):
    key = (TP, DEPTH, tuple(sorted(kw.items())))
    if key not in _NC_CACHE:
        _NC_CACHE[key] = build(TP=TP, DEPTH=DEPTH, **kw)
    nc = _NC_CACHE[key]
    maps = [pack_inputs(inp, c, TP) for c in cores]
    res = run_bass_kernel_spmd(nc, maps, core_ids=list(range(len(cores))))
    return res.results


def assemble(results, B, TP):
    NB = len(results)
    y_p = np.stack([r["yp"] for r in results], 0)
    y_s = np.concatenate([r["ys"].reshape(NS, TS, D) for r in results], 0)
    psm = np.stack([r["o_p_small"] for r in results], 0)
    ssm = np.concatenate([r["o_s_small"].transpose(1, 0, 2) for r in results], 0)
    def small(a):
        lru_h = np.stack([a[:, 0], a[:, 4]], 0)
        lru_conv = np.stack([a[:, 1:4], a[:, 5:8]], 0)
        shift = np.stack([a[:, 8], a[:, 9]], 0)
        return lru_h, lru_conv, shift
    p_h, p_c, p_sh = small(psm); s_h, s_c, s_sh = small(ssm)
    p_wkv = np.stack([r["o_p_wkv"] for r in results], 1)
    s_wkv = np.concatenate([r["o_s_wkv"] for r in results], 1)
    f = lambda a: np.ascontiguousarray(a, dtype=np.float32)
    return tuple(f(a) for a in (y_p, y_s, p_h, p_c, p_sh, p_wkv, s_h, s_c, s_sh, s_wkv))


def kernel(**inp):
    res = run_cores(inp, 2048, 4, list(range(NCORE)))
    return assemble(res, NCORE, 2048)
```

```python
import numpy as np
from contextlib import ExitStack
import concourse.bass as bass
import concourse.mybir as mybir
from concourse.bass_utils import run_bass_kernel_spmd

F32 = mybir.dt.float32
BF16 = mybir.dt.bfloat16
AF = mybir.ActivationFunctionType
ALU = mybir.AluOpType

D = 1024; KC = 8; DFF = 2816; NS = 16; TS = 8; NSEQ = 17; NCORE = 8
NDS = 40
MOD0_SPLIT = 12
UID = [0]
SEMLIM = 30000


class Tl:
    __slots__ = ("name", "w", "r")

    def __init__(s, name=""):
        s.name = name; s.w = None; s.r = {}


class Sch:
    def __init__(s, nc, st):
        s.nc = nc; s.st = st
        s.E = {"pe": nc.tensor, "act": nc.scalar, "dve": nc.vector, "pool": nc.gpsimd, "sp": nc.sync}
        s.sems = {e: [st.enter_context(nc.semaphore("c_%s0" % e))] for e in s.E}
        s.cnt = {e: 0 for e in s.E}
        s.pend = {e: [] for e in s.E}
        s.known = {e: {} for e in s.E}
        s.dsem = [st.enter_context(nc.semaphore("dq%d" % i)) for i in range(NDS)]
        s.dcnt = [0] * NDS
        s.dnx = {"pool": 0, "sp": NDS // 2, "act": NDS // 2}
        s.nins = 0

    def _wait(s, e, deps):
        need = {}
        for tok in deps:
            if tok[2] == "pe" and e == "pe":
                continue
            assert tok[1] is not None, "dependency on unsignaled instruction"
            k = id(tok[0])
            if k not in need or need[k][1] < tok[1]:
                need[k] = (tok[0], tok[1])
        for k, (sem, val) in need.items():
            if s.known[e].get(k, 0) >= val:
                continue
            s.E[e].wait_ge(sem, val)
            s.known[e][k] = val

    @staticmethod
    def _deps(R, W):
        deps = []
        for t in R:
            if t.w is not None:
                deps.append(t.w)
        for t in W:
            if t.w is not None:
                deps.append(t.w)
            deps.extend(t.r.values())
        return deps

    def op(s, e, fn, R=(), W=(), sig=True):
        s._wait(e, s._deps(R, W))
        ins = fn(s.E[e])
        s.nins += 1
        tok = [None, None, e]
        s.pend[e].append(tok)
        if sig:
            if s.cnt[e] >= SEMLIM:
                s.sems[e].append(s.st.enter_context(s.nc.semaphore("c_%s%d" % (e, len(s.sems[e])))))
                s.cnt[e] = 0
            s.cnt[e] += 1
            sem = s.sems[e][-1]
            ins.then_inc(sem, 1)
            for p in s.pend[e]:
                p[0] = sem; p[1] = s.cnt[e]
            s.pend[e] = []
        for t in W:
            t.w = tok; t.r = {}
        for t in R:
            t.r[e] = tok
        return ins

    def dma(s, q, out, in_, R=(), W=()):
        lo, hi = (0, NDS // 2) if q == "pool" else (NDS // 2, NDS)
        i = s.dnx[q]; s.dnx[q] = lo + (i + 1 - lo) % (hi - lo)
        deps = s._deps(R, W)
        if s.dcnt[i] > 0:
            deps.append([s.dsem[i], s.dcnt[i], "dma"])
        s._wait(q, deps)
        s.dcnt[i] += 16
        s.E[q].dma_start(out=out, in_=in_).then_inc(s.dsem[i], 16)
        s.nins += 1
        tok = [s.dsem[i], s.dcnt[i], "dma"]
        for t in W:
            t.w = tok; t.r = {}
        for t in R:
            t.r[("d", i)] = tok

    def dma_group(s, q, pairs, R=(), W=()):
        lo, hi = (0, NDS // 2) if q == "pool" else (NDS // 2, NDS)
        i = s.dnx[q]; s.dnx[q] = lo + (i + 1 - lo) % (hi - lo)
        deps = s._deps(R, W)
        if s.dcnt[i] > 0:
            deps.append([s.dsem[i], s.dcnt[i], "dma"])
        s._wait(q, deps)
        for out, in_ in pairs:
            s.dcnt[i] += 16
            s.E[q].dma_start(out=out, in_=in_).then_inc(s.dsem[i], 16)
            s.nins += 1
        tok = [s.dsem[i], s.dcnt[i], "dma"]
        for t in W:
            t.w = tok; t.r = {}
        for t in R:
            t.r[("d", i)] = tok

    def barrier(s):
        e = "sp"
        for o in s.E:
            assert not s.pend[o], "barrier with unsignaled instructions on " + o
        for i in range(NDS):
            if s.dcnt[i] > 0 and s.known[e].get(id(s.dsem[i]), 0) < s.dcnt[i]:
                s.E[e].wait_ge(s.dsem[i], s.dcnt[i])
        for o in s.E:
            if o != e and s.cnt[o] > 0 and s.known[e].get(id(s.sems[o][-1]), 0) < s.cnt[o]:
                s.E[e].wait_ge(s.sems[o][-1], s.cnt[o])
        s.cnt[e] += 1
        s.E[e].sem_inc(s.sems[e][-1], 1)
        for o in s.E:
            if o != e:
                s.E[o].wait_ge(s.sems[e][-1], s.cnt[e])
            for x in s.E:
                s.known[o][id(s.sems[x][-1])] = s.cnt[x]
            for i in range(NDS):
                s.known[o][id(s.dsem[i])] = s.dcnt[i]

    def drain(s, e="sp"):
        for i in range(NDS):
            if s.dcnt[i] > 0 and s.known[e].get(id(s.dsem[i]), 0) < s.dcnt[i]:
                s.E[e].wait_ge(s.dsem[i], s.dcnt[i])
        for o in s.E:
            if o != e and s.cnt[o] > 0:
                s.E[e].wait_ge(s.sems[o][-1], s.cnt[o])


VEC_SPEC = [("ada_b", 4 * 72), ("lru_conv_w", 2 * 4 * 8), ("lru_conv_b", 16), ("lru_lambda", 16),
            ("lru_gate_b", 2 * 16), ("rwkv_mu", 2 * 6 * 8), ("rwkv_w0", 16), ("rwkv_a0", 16), ("rwkv_v0", 8),
            ("rwkv_k_k", 16), ("rwkv_k_a", 16), ("rwkv_r_k", 16), ("rwkv_ln_w", 16), ("rwkv_ln_b", 16),
            ("final_gain", 8)]
VOFF = {}
_o = 0
for _n, _r in VEC_SPEC:
    VOFF[_n] = _o; _o += _r
NVEC = _o
NVT = (NVEC + 127) // 128
XO_C8 = NVT * 128
XO_OMMU = XO_C8 + 16
XO_OMKA = XO_OMMU + 96
NVCOL = XO_OMKA + 16

BIG_W = ["ada_w", "ffn_w_in", "ffn_w_out", "lru_w_in", "lru_gate_w", "lru_w_out", "rwkv_w_rkv", "rwkv_w_o",
         "rwkv_w1", "rwkv_w2", "rwkv_a1", "rwkv_a2", "rwkv_v1", "rwkv_v2", "rwkv_g1", "rwkv_g2"]
W_SHAPES = {"ada_w": [4, 1024, 9216], "ffn_w_in": [4, 2, 1024, 5632], "ffn_w_out": [4, 2, 2816, 1024],
            "lru_w_in": [2, 1024, 2048], "lru_gate_w": [2, 4, 256, 512], "lru_w_out": [2, 1024, 1024],
            "rwkv_w_rkv": [2, 3, 1024, 1024], "rwkv_w_o": [2, 1024, 1024], "rwkv_w1": [2, 1024, 64],
            "rwkv_w2": [2, 64, 1024], "rwkv_a1": [2, 1024, 64], "rwkv_a2": [2, 64, 1024],
            "rwkv_v1": [1, 1024, 32], "rwkv_v2": [1, 32, 1024], "rwkv_g1": [2, 1024, 160], "rwkv_g2": [2, 160, 1024]}


def build(TP=2048, DEPTH=4, do_lru=True, do_rwkv=True):
    T = TP + NS * TS
    nc = bass.Bass("TRN2", target_bir_lowering=False)
    A = {}

    def din(name, shape):
        A[name] = nc.dram_tensor(name, list(shape), F32, kind="ExternalInput").ap()

    def dout(name, shape):
        A[name] = nc.dram_tensor(name, list(shape), F32, kind="ExternalOutput").ap()

    din("c_ident", [128, 128]); din("c_mask", [128, 4, 64]); din("c_cm", [128, 640])
    din("xp", [TP, D]); din("xs", [NS * TS, D]); din("cc", [NSEQ, D])
    din("s_lru_h", [2, NS, D]); din("s_lru_conv", [2, NS * 3, D]); din("s_shift", [2, NS, D])
    din("s_wkv", [2, NS, 16, 64, 64])
    for n, r in VEC_SPEC:
        din(n, [r, 128])
    for n in BIG_W:
        din(n, W_SHAPES[n])
    dout("yp", [TP, D]); dout("ys", [NS * TS, D])
    dout("o_p_small", [10, D])
    dout("o_s_small", [10, NS, D])
    dout("o_p_wkv", [2, 16, 64, 64]); dout("o_s_wkv", [2, NS, 16, 64, 64])

    SCR = [{nm: nc.dram_tensor("scr_%d_%s" % (jl, nm), [KC, 128, T], F32).ap() for nm in ("r", "k", "v", "sw", "a", "g", "sv")}
           for jl in range(2)]
    with ExitStack() as st:
        S = Sch(nc, st)

        def sb(name, shape, dt=F32):
            return st.enter_context(nc.sbuf_tensor(name, list(shape), dt))

        x = sb("x", [128, KC, T]); Tx = [[Tl("x%d_%d" % (c, i)) for i in range(8)] for c in range(KC)]
        vecs = sb("vecs", [128, NVCOL]); Tvec = Tl("vecs")
        modTs = [sb("modT0", [128, 72, NSEQ]), sb("modT1", [128, 72, NSEQ])]; Tmods = [Tl("mod0"), Tl("mod1")]
        modT = modTs[0]; Tmod = Tmods[0]
        scT = sb("scT", [128, KC, NSEQ], BF16); TscT = Tl("scT")
        ones_bf = sb("ones_bf", [128, 128], BF16)
        identf = sb("identf", [128, 128]); identb = sb("identb", [128, 128], BF16)
        blk64 = sb("blk64", [128, 128])
        blk1 = sb("blk1", [128, 128])
        ost = sb("ost", [128, KC, 10 * NSEQ]); Tost = Tl("ost")
        Tconst = Tl("const")
        msk = sb("msk", [128, 4, 64]); cmk = sb("cmk", [128, 640])
        ps = [st.enter_context(nc.psum_tensor("ps%d" % i, [128, 512], F32)) for i in range(7)]
        psb = st.enter_context(nc.psum_tensor("psb", [128, 1024], BF16))
        Tps = [Tl("ps%d" % i) for i in range(7)]; Tpsb = Tl("psb")

        tiles = []
        c0 = 0
        while c0 < TP:
            w = min(512, TP - c0); tiles.append((c0, w, "p")); c0 += w
        tiles.append((TP, NS * TS, "s"))
        NT = len(tiles)

        def seqb(ap_n17, kind, w):
            if kind == "p":
                return ap_n17[:, 0:1].to_broadcast([128, w])
            return ap_n17[:, 1:NSEQ].unsqueeze(2).to_broadcast([128, NS, TS])

        def tv(ap2d, kind):
            if kind == "p":
                return ap2d
            return ap2d.rearrange("p (s t) -> p s t", t=TS)

        S.op("dve", lambda e: e.memset(ones_bf[:], 1.0 / 1024.0), W=[Tconst])
        S.op("dve", lambda e: e.memset(blk64[:], 0.0), W=[Tconst])
        S.op("dve", lambda e: e.memset(blk1[:], 0.0), W=[Tconst])
        for p in range(2):
            S.op("dve", lambda e, p=p: e.memset(blk64[p * 64:(p + 1) * 64, p * 64:(p + 1) * 64], 1.0 / 64.0), W=[Tconst])
            S.op("dve", lambda e, p=p: e.memset(blk1[p * 64:(p + 1) * 64, p * 64:(p + 1) * 64], 1.0), W=[Tconst])
        S.dma("sp", identf[:, :], A["c_ident"][:, :], R=[], W=[Tconst])
        S.dma("sp", msk[:, :, :], A["c_mask"][:, :, :], R=[], W=[Tconst])
        S.dma("sp", cmk[:, :], A["c_cm"][:, :], R=[], W=[Tconst])
        S.op("dve", lambda e: e.tensor_copy(identb[:], identf[:]), R=[Tconst], W=[Tconst])
        for i in range(7):
            S.op("dve", lambda e, i=i: e.memset(ps[i][:], 0.0), W=[Tps[i]])
        S.op("dve", lambda e: e.memset(psb[:].bitcast(F32), 0.0), W=[Tpsb])
        S.op("dve", lambda e: e.memset(ost[:], 0.0), W=[Tost])

        with nc.sbuf_tensor("vstage", [128, NVT, 128], F32) as vstage, nc.sbuf_tensor("cst", [NSEQ, D], F32) as cst:
            Tvs = Tl("vstage"); Tcst = Tl("cst")
            S.op("dve", lambda e: e.memset(vstage[:], 0.0), W=[Tvs])
            vpairs = []
            for n, r in VEC_SPEC:
                g0 = VOFF[n]; done = 0
                while done < r:
                    tix = (g0 + done) // 128; p0 = (g0 + done) % 128
                    m = min(r - done, 128 - p0)
                    vpairs.append((vstage[p0:p0 + m, tix, :], A[n][done:done + m, :]))
                    done += m
            S.dma_group("sp", vpairs, W=[Tvs])
            for tix in range(NVT):
                S.op("pe", lambda e, tix=tix: e.transpose(ps[0][:, 0:128], vstage[:, tix, :], identf[:]),
                     R=[Tvs, Tconst], W=[Tps[0]])
                S.op("act", lambda e, tix=tix: e.activation(out=vecs[:, tix * 128:(tix + 1) * 128], in_=ps[0][:, 0:128],
                                                            func=AF.Copy), R=[Tps[0]], W=[Tvec])
            lam = vecs[:, VOFF["lru_lambda"]:VOFF["lru_lambda"] + 16]
            c8 = vecs[:, XO_C8:XO_C8 + 16]
            S.op("act", lambda e: e.activation(out=c8, in_=lam, func=AF.Exp, scale=-1.0), R=[Tvec], W=[Tvec])
            S.op("act", lambda e: e.activation(out=c8, in_=c8, func=AF.Ln, bias=1.0), R=[Tvec], W=[Tvec])
            S.op("dve", lambda e: e.tensor_scalar(c8, c8, -8.0, None, ALU.mult), R=[Tvec], W=[Tvec])
            mu = vecs[:, VOFF["rwkv_mu"]:VOFF["rwkv_mu"] + 96]
            S.op("dve", lambda e: e.tensor_scalar(vecs[:, XO_OMMU:XO_OMMU + 96], mu, -1.0, 1.0, ALU.mult, ALU.add),
                 R=[Tvec], W=[Tvec])
            ka = vecs[:, VOFF["rwkv_k_a"]:VOFF["rwkv_k_a"] + 16]
            S.op("dve", lambda e: e.tensor_scalar(vecs[:, XO_OMKA:XO_OMKA + 16], ka, -1.0, 1.0, ALU.mult, ALU.add),
                 R=[Tvec], W=[Tvec])
            S.dma("sp", cst[:, :], A["cc"][:, :], W=[Tcst])
            S.op("act", lambda e: e.activation(out=cst[:, :], in_=cst[:, :], func=AF.Silu), R=[Tcst], W=[Tcst])
            for c in range(KC):
                S.op("pe", lambda e, c=c: e.transpose(ps[1][:, c * NSEQ:(c + 1) * NSEQ], cst[:, c * 128:(c + 1) * 128],
                                                      identf[0:NSEQ, 0:NSEQ]), R=[Tcst, Tconst], W=[Tps[1]], sig=(c == KC - 1))
            S.op("act", lambda e: e.activation(out=scT[:].rearrange("p c s -> p (c s)"), in_=ps[1][:, 0:KC * NSEQ],
                                               func=AF.Copy), R=[Tps[1]], W=[TscT])

            with nc.sbuf_tensor("xin0", [128, D], F32) as xin0, nc.sbuf_tensor("xin1", [128, D], F32) as xin1:
                xin = [xin0, xin1]; Txin = [Tl("xin0"), Tl("xin1")]
                nblk = T // 128
                for b in range(nblk):
                    t0 = b * 128
                    src = A["xp"][t0:t0 + 128, :] if t0 < TP else A["xs"][:, :]
                    S.dma("sp", xin[b % 2][:, :], src, W=[Txin[b % 2]])
                    ti = min(t0 // 512, NT - 1) if t0 < TP else NT - 1
                    for half in range(2):
                        pb = half
                        for q in range(4):
                            c = half * 4 + q
                            S.op("pe", lambda e, c=c, q=q, b=b, pb=pb: e.transpose(ps[pb][:, q * 128:(q + 1) * 128],
                                 xin[b % 2][:, c * 128:(c + 1) * 128], identf[:]), R=[Txin[b % 2], Tconst], W=[Tps[pb]],
                                 sig=(q == 3))
                        S.op("act" if half == 0 else "dve",
                             (lambda e, half=half, t0=t0, pb=pb: e.activation(out=x[:, half * 4:half * 4 + 4, t0:t0 + 128],
                              in_=ps[pb][:].rearrange("p (q t) -> p q t", t=128), func=AF.Copy)) if half == 0 else
                             (lambda e, half=half, t0=t0, pb=pb: e.tensor_copy(x[:, half * 4:half * 4 + 4, t0:t0 + 128],
                              ps[pb][:].rearrange("p (q t) -> p q t", t=128))),
                             R=[Tps[pb]], W=[Tx[c][ti] for c in range(half * 4, half * 4 + 4)])

        def norm_stats(ti, sq, Tsq, rstd, Trstd):
            c0, w, kind = tiles[ti]
            for c in range(KC):
                S.op("act", lambda e, c=c: e.activation(out=sq[:, c % 2, 0:w], in_=x[:, c, c0:c0 + w], func=AF.Square),
                     R=[Tx[c][ti]], W=[Tsq[c % 2]])
                S.op("pe", lambda e, c=c: e.matmul(ps[6][:, 0:w], ones_bf[:], sq[:, c % 2, 0:w], start=(c == 0), stop=(c == KC - 1)),
                     R=[Tsq[c % 2], Tconst], W=[Tps[6]], sig=True)
            S.op("act", lambda e: e.activation(out=rstd[:, 0:w], in_=ps[6][:, 0:w], func=AF.Ln, bias=1e-6, scale=1.0),
                 R=[Tps[6]], W=[Trstd])
            S.op("act", lambda e: e.activation(out=rstd[:, 0:w], in_=rstd[:, 0:w], func=AF.Exp, scale=-0.5), R=[Trstd], W=[Trstd])

        def modulate(ti, m_shift, m_scale, rstd, Trstd, tmp2, Ttmp2, out_fn, Wout, extra=None):
            c0, w, kind = tiles[ti]
            for c in range(KC):
                tmp = tmp2[:, c % 2, :]; Ttmp = Ttmp2[c % 2]
                S.op("dve", lambda e, c=c, tmp=tmp: e.tensor_tensor(tmp[:, 0:w], x[:, c, c0:c0 + w], rstd[:, 0:w], ALU.mult),
                     R=[Tx[c][ti], Trstd], W=[Ttmp])
                if kind == "p":
                    S.op("act", lambda e, c=c, tmp=tmp: e.activation(out=out_fn(c), in_=tmp[:, 0:w], func=AF.Identity,
                                                            scale=modT[:, m_scale * 8 + c, 0:1], bias=modT[:, m_shift * 8 + c, 0:1]),
                         R=[Ttmp, Tmod], W=Wout(c))
                else:
                    S.op("dve", lambda e, c=c, tmp=tmp: e.tensor_tensor(tv(tmp[:, 0:w], kind), tv(tmp[:, 0:w], kind),
                                                               seqb(modT[:, m_scale * 8 + c, :], kind, w), ALU.mult),
                         R=[Ttmp, Tmod], W=[Ttmp])
                    S.op("dve", lambda e, c=c, tmp=tmp: e.tensor_tensor(out_fn(c) if len(out_fn(c).shape) == 3 else tv(out_fn(c), kind),
                                                               tv(tmp[:, 0:w], kind),
                                                               seqb(modT[:, m_shift * 8 + c, :], kind, w), ALU.add),
                         R=[Ttmp, Tmod], W=Wout(c))
                if extra is not None:
                    extra(c, tmp, Ttmp)

        def norm_mod_all(m_shift, m_scale, sq, Tsq, rstd2, Trstd2, tmp2, Ttmp2, out_fn_t, Wout_t, extra_t=None):
            norm_stats(0, sq, Tsq, rstd2[:, 0, :], Trstd2[0])
            for ti in range(NT):
                if ti + 1 < NT:
                    norm_stats(ti + 1, sq, Tsq, rstd2[:, (ti + 1) % 2, :], Trstd2[(ti + 1) % 2])
                modulate(ti, m_shift, m_scale, rstd2[:, ti % 2, :], Trstd2[ti % 2], tmp2, Ttmp2, out_fn_t(ti), Wout_t(ti),
                         extra=(extra_t(ti) if extra_t is not None else None))

        def resid_update(ti, dc, psum_ap, Tp, m_gate, tmp, Ttmp, sub=None):
            c0, w, kind = tiles[ti] if sub is None else sub
            if kind == "p":
                S.op("dve", lambda e: e.scalar_tensor_tensor(x[:, dc, c0:c0 + w], psum_ap, modT[:, m_gate * 8 + dc, 0:1],
                                                             x[:, dc, c0:c0 + w], ALU.mult, ALU.add),
                     R=[Tp, Tmod, Tx[dc][ti]], W=[Tx[dc][ti]])
            else:
                S.op("dve", lambda e: e.tensor_tensor(tv(tmp[:, 0:w], kind), tv(psum_ap, kind),
                                                      seqb(modT[:, m_gate * 8 + dc, :], kind, w), ALU.mult),
                     R=[Tp, Tmod], W=[Ttmp])
                S.op("dve", lambda e: e.tensor_tensor(x[:, dc, c0:c0 + w], x[:, dc, c0:c0 + w], tmp[:, 0:w], ALU.add),
                     R=[Ttmp, Tx[dc][ti]], W=[Tx[dc][ti]])

        def gen_mod(layer, mw, Tmw, mT, TmT, pc_lo=0, pc_hi=36):
            wv = A["ada_w"][layer].rearrange("(k p) n -> p k n", p=128)
            NPC = 9216 // 256
            bank = 6
            S.dma("pool", mw[:, pc_lo % 2, :, :], wv[:, :, pc_lo * 256:(pc_lo + 1) * 256], W=[Tmw[pc_lo % 2]])
            yield
            for pc in range(pc_lo, pc_hi):
                sl = pc % 2
                if pc + 1 < pc_hi:
                    S.dma("pool", mw[:, 1 - sl, :, :], wv[:, :, (pc + 1) * 256:(pc + 2) * 256], W=[Tmw[1 - sl]])
                for q in range(2):
                    n = pc * 2 + q
                    m = n // 8; nn = n % 8
                    for k in range(KC):
                        S.op("pe", lambda e, k=k, q=q, sl=sl, nn=nn: e.matmul(
                            ps[bank][:, nn * NSEQ:(nn + 1) * NSEQ], mw[:, sl, k, q * 128:(q + 1) * 128], scT[:, k, :],
                            start=(k == 0), stop=(k == KC - 1)), R=[Tmw[sl], TscT], W=[Tps[bank]], sig=(k == KC - 1))
                    if nn == 7:
                        bcol = VOFF["ada_b"] + layer * 72 + m * 8
                        S.op("dve", lambda e, m=m, bcol=bcol: e.tensor_tensor(
                            mT[:, m * 8:(m + 1) * 8, :], ps[bank][:, 0:8 * NSEQ].rearrange("p (n s) -> p n s", s=NSEQ),
                            vecs[:, bcol:bcol + 8].unsqueeze(2).to_broadcast([128, 8, NSEQ]), ALU.add),
                            R=[Tps[bank], Tvec, TmT], W=[TmT])
                        sl8 = mT[:, m * 8:(m + 1) * 8, :]
                        if m in (1, 4, 5, 7):
                            S.op("dve", lambda e, sl8=sl8: e.tensor_scalar(sl8, sl8, 1.0, None, ALU.add), R=[TmT], W=[TmT])
                        elif m in (2, 8):
                            S.op("dve", lambda e, sl8=sl8: e.tensor_scalar(sl8, sl8, 0.5, 0.5, ALU.mult, ALU.add),
                                 R=[TmT], W=[TmT])
                yield

        def phase_mod(layer):
            with nc.sbuf_tensor("mw%d" % layer, [128, 2, KC, 256], BF16) as mw:
                Tmw = [Tl("mw%d" % i) for i in range(3)]
                S.barrier()
                for _ in gen_mod(layer, mw, Tmw, modTs[layer % 2], Tmods[layer % 2], 0, MOD0_SPLIT if layer == 0 else 36):
                    pass

        def phase_ffn(layer, which, co_layer=None, co_range=(0, 36)):
            m0 = 0 if which == 0 else 6
            GP = 2
            with ExitStack() as ph:
                def pb_(name, shape, dt=F32):
                    UID[0] += 1; return ph.enter_context(nc.sbuf_tensor("%s_%d" % (name, UID[0]), list(shape), dt))
                h = pb_("f_h", [128, KC, T], BF16); Th = [Tl("h%d" % i) for i in range(NT)]
                act = pb_("f_act", [128, 2 * GP, T], BF16); Tact = [[Tl("a") for _ in range(NT)] for _ in range(2 * GP)]
                sq = pb_("f_sq", [128, 2, 512], BF16); Tsq = [Tl("sq0"), Tl("sq1")]
                rstd2 = pb_("f_rstd", [128, 2, 512]); Trstd2 = [Tl("rstd0"), Tl("rstd1")]
                tmp2 = pb_("f_tmp", [128, 2, 512]); Ttmp2 = [Tl("tmp0"), Tl("tmp1")]
                tmp = tmp2[:, 0, :]; Ttmp = Ttmp2[0]
                sg = pb_("f_sg", [128, 2, 512]); Tsg = [Tl("sg0"), Tl("sg1")]
                win = pb_("f_win", [128, 3, KC, 2, 256], BF16); Twin = [Tl("win%d" % i) for i in range(3)]
                wout = pb_("f_wout", [128, 2 * GP, 2, D], BF16); Twout = [Tl("wout%d" % i) for i in range(2 * GP)]
                co = None
                if co_layer is not None:
                    mw = pb_("f_mw", [128, 2, KC, 256], BF16)
                    co = gen_mod(co_layer, mw, [Tl("mw%d" % i) for i in range(3)], modTs[co_layer % 2], Tmods[co_layer % 2],
                                 co_range[0], co_range[1])
                S.barrier()

                co_n = [0]

                def co_step():
                    nonlocal co
                    for _ in range(2):
                        if co is not None:
                            try:
                                next(co)
                            except StopIteration:
                                co = None
                norm_mod_all(m0, m0 + 1, sq, Tsq, rstd2, Trstd2, tmp2, Ttmp2,
                             lambda ti: (lambda c, c0=tiles[ti][0], w=tiles[ti][1]: h[:, c, c0:c0 + w]),
                             lambda ti: (lambda c, ti=ti: [Th[ti]]))
                wi = A["ffn_w_in"][layer, which].rearrange("(k p) n -> p k n", p=128)
                wo = A["ffn_w_out"][layer, which].rearrange("(f p) n -> p f n", p=128)
                NPC = 11
                pcs = list(range(NPC))
                groups = [pcs[i:i + GP] for i in range(0, NPC, GP)]
                evq = 0
                for gi, grp in enumerate(groups):
                    for li, pc in enumerate(grp):
                        sl = pc % 3
                        S.dma_group("pool", [(win[:, sl, :, 0, :], wi[:, :, pc * 256:(pc + 1) * 256]),
                                             (win[:, sl, :, 1, :], wi[:, :, DFF + pc * 256:DFF + (pc + 1) * 256])], W=[Twin[sl]])
                        so = (gi % 2) * GP + li
                        S.dma("pool", wout[:, so, :, :], wo[:, pc * 2:pc * 2 + 2, :], W=[Twout[so]])
                        for q in range(2):
                            fi = li * 2 + q
                            for ti in range(NT):
                                c0, w, kind = tiles[ti]
                                bg = (evq % 2) * 2; evq += 1
                                for gu in range(2):
                                    for k in range(KC):
                                        S.op("pe", lambda e, k=k, gu=gu, sl=sl, q=q, bg=bg, c0=c0, w=w: e.matmul(
                                            ps[bg + gu][:, 0:w], win[:, sl, k, gu, q * 128:(q + 1) * 128], h[:, k, c0:c0 + w],
                                            start=(k == 0), stop=(k == KC - 1)), R=[Twin[sl], Th[ti]], W=[Tps[bg + gu]],
                                            sig=(k == KC - 1))
                                sgi = (bg // 2)
                                S.op("act", lambda e, bg=bg, w=w, sgi=sgi: e.activation(out=sg[:, sgi, 0:w], in_=ps[bg][:, 0:w],
                                                                                      func=AF.Silu), R=[Tps[bg]], W=[Tsg[sgi]])
                                S.op("dve", lambda e, bg=bg, w=w, sgi=sgi, fi=fi, c0=c0: e.tensor_tensor(
                                    act[:, fi, c0:c0 + w], sg[:, sgi, 0:w], ps[bg + 1][:, 0:w], ALU.mult),
                                    R=[Tsg[sgi], Tps[bg + 1]], W=[Tact[fi][ti]])
                            co_step()
                    nf = len(grp) * 2
                    for ti in range(NT):
                        c0, w, kind = tiles[ti]
                        for dc in range(KC):
                            bk = 4 + (dc % 2)
                            for fi in range(nf):
                                so = (gi % 2) * GP + fi // 2
                                S.op("pe", lambda e, fi=fi, so=so, dc=dc, bk=bk, c0=c0, w=w: e.matmul(
                                    ps[bk][:, 0:w], wout[:, so, fi % 2, dc * 128:(dc + 1) * 128], act[:, fi, c0:c0 + w],
                                    start=(fi == 0), stop=(fi == nf - 1)), R=[Twout[so], Tact[fi][ti]], W=[Tps[bk]],
                                    sig=(fi == nf - 1))
                            resid_update(ti, dc, ps[bk][:, 0:w], Tps[bk], m0 + 2, tmp, Ttmp)
                while co is not None:
                    co_step()


        def load_T(src_ap, rows, dst_fn, Wd, stg, Tstg):
            S.dma("sp", stg[0:rows, :], src_ap, W=[Tstg])
            for c in range(KC):
                bk = c % 2
                S.op("pe", lambda e, c=c, bk=bk: e.transpose(ps[bk][:, 0:rows], stg[0:rows, c * 128:(c + 1) * 128],
                                                             identf[0:rows, 0:rows]), R=[Tstg, Tconst], W=[Tps[bk]])
                S.op("act", lambda e, c=c, bk=bk: e.activation(out=dst_fn(c), in_=ps[bk][:, 0:rows], func=AF.Copy),
                     R=[Tps[bk]], W=Wd)

        OSTV = ost[:].rearrange("p c (r s) -> p c r s", s=NSEQ)

        def phase_lru(layer):
            j = layer // 2
            with ExitStack() as ph:
                def pb_(name, shape, dt=F32):
                    UID[0] += 1; return ph.enter_context(nc.sbuf_tensor("%s_%d" % (name, UID[0]), list(shape), dt))
                h = pb_("l_h", [128, KC, T], BF16); Th = [Tl("h%d" % i) for i in range(NT)]
                yin = pb_("l_yin", [128, KC, T], BF16); Tyin = [Tl("yin%d" % i) for i in range(NT)]
                h0s = pb_("l_h0s", [128, KC, NS]); Th0s = Tl("h0s")
                cvh = pb_("l_cvh", [128, KC, NS * 3]); Tcvh = Tl("cvh")
                with ExitStack() as ph1:
                    UID[0] += 1
                    sq = ph1.enter_context(nc.sbuf_tensor("l_sq_%d" % UID[0], [128, 2, 512], BF16)); Tsq = [Tl("sq0"), Tl("sq1")]
                    rstd2 = ph1.enter_context(nc.sbuf_tensor("l_rstd_%d" % UID[0], [128, 2, 512], F32)); Trstd2 = [Tl("r0"), Tl("r1")]
                    stg = ph1.enter_context(nc.sbuf_tensor("l_stg_%d" % UID[0], [NS * 3, D], F32)); Tstg = Tl("stg")
                    tmp2 = ph1.enter_context(nc.sbuf_tensor("l_tmp_%d" % UID[0], [128, 2, 512], F32)); Ttmp2 = [Tl("t0"), Tl("t1")]
                    S.barrier()
                    load_T(A["s_lru_h"][j], NS, lambda c: h0s[:, c, :], [Th0s], stg, Tstg)
                    load_T(A["s_lru_conv"][j], NS * 3, lambda c: cvh[:, c, :], [Tcvh], stg, Tstg)
                    norm_mod_all(3, 4, sq, Tsq, rstd2, Trstd2, tmp2, Ttmp2,
                                 lambda ti: (lambda c, c0=tiles[ti][0], w=tiles[ti][1]: h[:, c, c0:c0 + w]),
                                 lambda ti: (lambda c, ti=ti: [Th[ti]]))
                ph2 = ExitStack()
                def pb2(name, shape, dt=F32):
                    UID[0] += 1; return ph2.enter_context(nc.sbuf_tensor("%s_%d" % (name, UID[0]), list(shape), dt))
                win = pb2("l_win", [128, 2, KC, 2, 256], BF16); Twin = [Tl("w0"), Tl("w1")]
                gw = pb2("l_gw", [128, 2, 2, 512], BF16); Tgw = [Tl("g0"), Tl("g1")]
                rec = pb2("l_rec", [128, 2, 3 + 512]); Trec = Tl("rec")
                recs = pb2("l_recs", [128, 2, NS, 3 + TS]); Trecs = Tl("recs")
                xc = pb2("l_xc", [128, 2, 512]); Txc = Tl("xc")
                xcb = pb2("l_xcb", [128, 2, 512], BF16); Txcb = Tl("xcb")
                ii = pb2("l_ii", [128, 2, 512]); Tii = Tl("ii")
                aa = pb2("l_aa", [128, 2, 512]); Taa = Tl("aa")
                hs = pb2("l_hs", [128, 2, 512]); Ths = Tl("hs")
                gt = pb2("l_gt", [128, 512]); Tgt = Tl("gt")
                carry = pb2("l_carry", [128, 2]); Tcar = Tl("carry")
                t16 = pb2("l_t16", [128, NS]); Tt16 = Tl("t16")
                rr = aa; uu = ii; gb = gt; Tgb = Tgt
                Taa2 = [Tl("aa0"), Tl("aa1")]; Tii2 = [Tl("ii0"), Tl("ii1")]; Ths2 = [Tl("hs0"), Tl("hs1")]; Txc2 = [Tl("xc0"), Tl("xc1")]
                S.barrier()
                wv = A["lru_w_in"][j].rearrange("(k p) n -> p k n", p=128)
                NPT = sum(1 for t_ in tiles if t_[2] == "p")
                for n in range(4):
                    sl = n % 2
                    S.dma_group("pool", [(win[:, sl, :, 0, :], wv[:, :, n * 256:(n + 1) * 256]),
                                         (win[:, sl, :, 1, :], wv[:, :, D + n * 256:D + (n + 1) * 256])], W=[Twin[sl]])
                    S.dma("pool", gw[:, sl, :, :], A["lru_gate_w"][j, n].rearrange("(k p) g -> p k g", p=128), W=[Tgw[sl]])
                    S.op("dve", lambda e: e.memset(carry[:], 0.0), W=[Tcar])
                    S.op("dve", lambda e: e.memset(rec[:, :, 0:3], 0.0), W=[Trec])
                    wprev = 0
                    for ti in range(NT):
                        c0, w, kind = tiles[ti]
                        last_p = (kind == "p" and ti == NPT - 1)
                        if kind == "p" and ti > 0:
                            S.op("dve", lambda e, wprev=wprev: e.tensor_copy(rec[:, :, 0:3], rec[:, :, wprev:wprev + 3]),
                                 R=[Trec], W=[Trec])
                        for ci in range(2):
                            for k in range(KC):
                                S.op("pe", lambda e, k=k, ci=ci: e.matmul(ps[ci][:, 0:w], win[:, sl, k, 1, ci * 128:(ci + 1) * 128],
                                     h[:, k, c0:c0 + w], start=(k == 0), stop=(k == KC - 1)), R=[Twin[sl], Th[ti]], W=[Tps[ci]],
                                     sig=(k == KC - 1))
                            if kind == "p":
                                S.op("act", lambda e, ci=ci: e.activation(out=rec[:, ci, 3:3 + w], in_=ps[ci][:, 0:w], func=AF.Copy),
                                     R=[Tps[ci]], W=[Trec])
                            else:
                                S.op("dve", lambda e, ci=ci: e.tensor_copy(recs[:, ci, :, 0:3],
                                     cvh[:, 2 * n + ci, :].rearrange("p (s r) -> p s r", r=3)), R=[Tcvh], W=[Trecs])
                                S.op("act", lambda e, ci=ci: e.activation(out=recs[:, ci, :, 3:3 + TS],
                                     in_=ps[ci][:, 0:w].rearrange("p (s t) -> p s t", t=TS), func=AF.Copy), R=[Tps[ci]], W=[Trecs])
                        for ci in range(2):
                            bk = 4 + ci
                            for k in range(KC):
                                S.op("pe", lambda e, k=k, ci=ci, bk=bk: e.matmul(ps[bk][:, 0:w], win[:, sl, k, 0, ci * 128:(ci + 1) * 128],
                                     h[:, k, c0:c0 + w], start=(k == 0), stop=(k == KC - 1)), R=[Twin[sl], Th[ti]], W=[Tps[bk]],
                                     sig=(k == KC - 1))
                        for ci in range(2):
                            c = 2 * n + ci
                            cw = VOFF["lru_conv_w"] + j * 32 + c
                            cb = VOFF["lru_conv_b"] + j * 8 + c
                            if kind == "p":
                                XP = lambda jt, ci=ci: rec[:, ci, jt:jt + w]
                                XO = xc[:, ci, 0:w]
                                TR = Trec
                            else:
                                XP = lambda jt, ci=ci: recs[:, ci, :, jt:jt + TS]
                                XO = xc[:, ci, 0:w].rearrange("p (s t) -> p s t", t=TS)
                                TR = Trecs
                            S.op("act", lambda e, XP=XP, XO=XO, cw=cw, cb=cb: e.activation(out=XO, in_=XP(3), func=AF.Identity,
                                 scale=vecs[:, cw + 24:cw + 25], bias=vecs[:, cb:cb + 1]), R=[TR, Tvec], W=[Txc2[ci]])
                            for jt in range(3):
                                S.op("dve", lambda e, XP=XP, XO=XO, cw=cw, jt=jt: e.scalar_tensor_tensor(XO, XP(jt),
                                     vecs[:, cw + jt * 8:cw + jt * 8 + 1], XO, ALU.mult, ALU.add), R=[TR, Tvec, Txc2[ci]], W=[Txc2[ci]])
                            if last_p:
                                S.op("act", lambda e, ci=ci, c=c: e.activation(out=OSTV[:, c, 4 * j + 1:4 * j + 4, 0],
                                     in_=rec[:, ci, w:w + 3], func=AF.Copy), R=[Trec], W=[Tost])
                            if kind == "s":
                                S.op("act", lambda e, ci=ci, c=c: e.activation(out=OSTV[:, c, 4 * j + 1:4 * j + 4, 1:NSEQ],
                                     in_=recs[:, ci, :, TS:TS + 3].rearrange("p s q -> p q s"), func=AF.Copy), R=[Trecs], W=[Tost])
                        S.op("act", lambda e: e.activation(out=xcb[:, :, 0:w], in_=xc[:, :, 0:w], func=AF.Copy), R=Txc2, W=[Txcb])
                        for oc in range(4):
                            bk = 2 + oc % 2
                            for k in range(2):
                                S.op("pe", lambda e, k=k, oc=oc, bk=bk: e.matmul(ps[bk][:, 0:w], gw[:, sl, k, oc * 128:(oc + 1) * 128],
                                     xcb[:, k, 0:w], start=(k == 0), stop=(k == 1)), R=[Tgw[sl], Txcb], W=[Tps[bk]], sig=(k == 1))
                            gbc = VOFF["lru_gate_b"] + j * 16 + n * 4 + oc
                            dst = rr[:, oc, 0:w] if oc < 2 else ii[:, oc - 2, 0:w]
                            S.op("act", lambda e, bk=bk, dst=dst, gbc=gbc: e.activation(out=dst, in_=ps[bk][:, 0:w], func=AF.Sigmoid,
                                 bias=vecs[:, gbc:gbc + 1], scale=1.0), R=[Tps[bk], Tvec], W=[Taa2[oc] if oc < 2 else Tii2[oc - 2]])
                        for ci in range(2):
                            c8c = XO_C8 + j * 8 + 2 * n + ci
                            S.op("act", lambda e, ci=ci, c8c=c8c: e.activation(out=aa[:, ci, 0:w], in_=rr[:, ci, 0:w], func=AF.Exp,
                                 scale=vecs[:, c8c:c8c + 1]), R=[Taa2[ci], Tvec], W=[Taa2[ci]])
                        for ci in range(2):
                            S.op("act", lambda e, ci=ci: e.activation(out=hs[:, ci, 0:w], in_=aa[:, ci, 0:w], func=AF.Square),
                                 R=[Taa2[ci]], W=[Ths2[ci]])
                        for ci in range(2):
                            S.op("act", lambda e, ci=ci: e.activation(out=hs[:, ci, 0:w], in_=hs[:, ci, 0:w], func=AF.Sqrt,
                                 scale=-1.0, bias=1.0), R=[Ths2[ci]], W=[Ths2[ci]])
                        for ci in range(2):
                            S.op("dve", lambda e, ci=ci: e.tensor_tensor(uu[:, ci, 0:w], ii[:, ci, 0:w], hs[:, ci, 0:w], ALU.mult),
                                 R=[Ths2[ci], Tii2[ci]], W=[Tii2[ci]])
                            S.op("dve", lambda e, ci=ci: e.tensor_tensor(uu[:, ci, 0:w], uu[:, ci, 0:w], xc[:, ci, 0:w], ALU.mult),
                                 R=[Tii2[ci], Txc2[ci]], W=[Tii2[ci]])
                        for ci in range(2):
                            c = 2 * n + ci
                            if kind == "p":
                                S.op("dve", lambda e, ci=ci: e.tensor_tensor_scan(hs[:, ci, 0:w], aa[:, ci, 0:w], uu[:, ci, 0:w],
                                     carry[:, ci:ci + 1], ALU.mult, ALU.add), R=[Taa2[ci], Tii2[ci], Tcar], W=[Ths2[ci]])
                                S.op("act", lambda e, ci=ci: e.activation(out=carry[:, ci:ci + 1], in_=hs[:, ci, w - 1:w], func=AF.Copy),
                                     R=[Ths2[ci]], W=[Tcar])
                                if last_p:
                                    S.op("act", lambda e, ci=ci, c=c: e.activation(out=OSTV[:, c, 4 * j, 0:1], in_=hs[:, ci, w - 1:w],
                                         func=AF.Copy), R=[Ths2[ci]], W=[Tost])
                            else:
                                a3 = aa[:, ci, 0:w].rearrange("p (s t) -> p s t", t=TS)
                                u3 = uu[:, ci, 0:w].rearrange("p (s t) -> p s t", t=TS)
                                h3 = hs[:, ci, 0:w].rearrange("p (s t) -> p s t", t=TS)
                                S.op("dve", lambda e, a3=a3, c=c: e.tensor_tensor(t16[:, :], a3[:, :, 0], h0s[:, c, :], ALU.mult),
                                     R=[Taa2[ci], Th0s], W=[Tt16])
                                S.op("dve", lambda e, u3=u3: e.tensor_tensor(u3[:, :, 0], u3[:, :, 0], t16[:, :], ALU.add),
                                     R=[Tt16, Tii2[ci]], W=[Tii2[ci]])
                                S.op("dve", lambda e, a3=a3: e.memset(a3[:, :, 0], 0.0), R=[Tt16], W=[Taa2[ci]])
                                S.op("dve", lambda e, ci=ci: e.tensor_tensor_scan(hs[:, ci, 0:w], aa[:, ci, 0:w], uu[:, ci, 0:w],
                                     0.0, ALU.mult, ALU.add), R=[Taa2[ci], Tii2[ci]], W=[Ths2[ci]])
                                S.op("act", lambda e, h3=h3, c=c: e.activation(out=OSTV[:, c, 4 * j, 1:NSEQ], in_=h3[:, :, TS - 1],
                                     func=AF.Copy), R=[Ths2[ci]], W=[Tost])
                        for ci in range(2):
                            bk = 4 + ci
                            S.op("act", lambda e, bk=bk, ci=ci: e.activation(out=xc[:, ci, 0:w], in_=ps[bk][:, 0:w], func=AF.Square),
                                 R=[Tps[bk]], W=[Txc2[ci]])
                        for ci in range(2):
                            bk = 4 + ci
                            S.op("dve", lambda e, ci=ci: e.tensor_scalar(xc[:, ci, 0:w], xc[:, ci, 0:w], 0.044715, 1.0, ALU.mult, ALU.add),
                                 R=[Txc2[ci]], W=[Txc2[ci]])
                            S.op("dve", lambda e, bk=bk, ci=ci: e.tensor_tensor(xc[:, ci, 0:w], xc[:, ci, 0:w], ps[bk][:, 0:w], ALU.mult),
                                 R=[Txc2[ci], Tps[bk]], W=[Txc2[ci]])
                        for ci in range(2):
                            S.op("act", lambda e, ci=ci: e.activation(out=xc[:, ci, 0:w], in_=xc[:, ci, 0:w], func=AF.Sigmoid,
                                 scale=1.5957691216057308), R=[Txc2[ci]], W=[Txc2[ci]])
                        for ci in range(2):
                            bk = 4 + ci; c = 2 * n + ci
                            S.op("dve", lambda e, bk=bk, ci=ci: e.tensor_tensor(xc[:, ci, 0:w], xc[:, ci, 0:w], ps[bk][:, 0:w], ALU.mult),
                                 R=[Txc2[ci], Tps[bk]], W=[Txc2[ci]])
                            S.op("dve", lambda e, ci=ci, c=c: e.tensor_tensor(yin[:, c, c0:c0 + w], xc[:, ci, 0:w], hs[:, ci, 0:w], ALU.mult),
                                 R=[Txc2[ci], Ths2[ci]], W=[Tyin[ti]])
                        wprev = w
                ph2.close()
                wo = pb_("l_wo", [128, KC, D], BF16); Two = Tl("wo")
                tmp = pb_("l_tmp2", [128, 512]); Ttmp = Tl("tmp2")
                S.barrier()
                S.dma("pool", wo[:, :, :], A["lru_w_out"][j].rearrange("(k p) n -> p k n", p=128), W=[Two])
                for ti in range(NT):
                    c0, w, kind = tiles[ti]
                    for dc in range(KC):
                        bk = dc % 2
                        for k in range(KC):
                            S.op("pe", lambda e, k=k, dc=dc, bk=bk: e.matmul(ps[bk][:, 0:w], wo[:, k, dc * 128:(dc + 1) * 128],
                                 yin[:, k, c0:c0 + w], start=(k == 0), stop=(k == KC - 1)), R=[Two, Tyin[ti]], W=[Tps[bk]],
                                 sig=(k == KC - 1))
                        resid_update(ti, dc, ps[bk][:, 0:w], Tps[bk], 5, tmp, Ttmp)


        def phase_rwkv(layer):
            jl = layer // 2
            scr = SCR[jl]
            HW_ = 1 + TP + NS * (1 + TS)
            NPT = sum(1 for t_ in tiles if t_[2] == "p")
            with ExitStack() as ph:
                def pb_(name, shape, dt=F32):
                    UID[0] += 1; return ph.enter_context(nc.sbuf_tensor("%s_%d" % (name, UID[0]), list(shape), dt))
                h = pb_("r_h", [128, KC, HW_], BF16); Th = Tl("h")
                xm = pb_("r_xm", [128, KC, T], BF16); Txm = Tl("xm")
                shs = pb_("r_shs", [128, KC, NS]); Tshs = Tl("shs")

                def hS(c):
                    return h[:, c, 1 + TP:HW_].rearrange("p (s t) -> p s t", t=1 + TS)
                with ExitStack() as ph1:
                    UID[0] += 1
                    sq = ph1.enter_context(nc.sbuf_tensor("r_sq_%d" % UID[0], [128, 2, 512], BF16)); Tsq = [Tl("sq0"), Tl("sq1")]
                    rstd2 = ph1.enter_context(nc.sbuf_tensor("r_rstd_%d" % UID[0], [128, 2, 512], F32)); Trstd2 = [Tl("r0"), Tl("r1")]
                    tmp2 = ph1.enter_context(nc.sbuf_tensor("r_tmp_%d" % UID[0], [128, 2, 512], F32)); Ttmp2 = [Tl("tmp0"), Tl("tmp1")]
                    stg = ph1.enter_context(nc.sbuf_tensor("r_stg_%d" % UID[0], [NS * 3, D], F32)); Tstg = Tl("stg")
                    S.barrier()
                    load_T(A["s_shift"][jl], NS, lambda c: shs[:, c, :], [Tshs], stg, Tstg)
                    S.op("dve", lambda e: e.memset(h[:, :, 0:1], 0.0), W=[Th])
                    for c in range(KC):
                        S.op("dve", lambda e, c=c: e.tensor_copy(hS(c)[:, :, 0], shs[:, c, :]), R=[Tshs], W=[Th])
                    def outf_t(ti):
                        c0, w, kind = tiles[ti]
                        if kind == "p":
                            return lambda c, c0=c0, w=w: h[:, c, 1 + c0:1 + c0 + w]
                        return lambda c: hS(c)[:, :, 1:1 + TS]

                    def extra_t(ti):
                        c0, w, kind = tiles[ti]

                        def extra(c, tmp, Ttmp):
                            if kind == "p" and ti == NPT - 1:
                                S.op("act", lambda e: e.activation(out=OSTV[:, c, 8 + jl, 0:1], in_=tmp[:, w - 1:w], func=AF.Identity,
                                     scale=modT[:, 4 * 8 + c, 0:1], bias=modT[:, 3 * 8 + c, 0:1]), R=[Ttmp, Tmod], W=[Tost])
                            if kind == "s":
                                S.op("dve", lambda e: e.tensor_tensor(OSTV[:, c, 8 + jl, 1:NSEQ], tv(tmp[:, 0:w], kind)[:, :, TS - 1],
                                     modT[:, 3 * 8 + c, 1:NSEQ], ALU.add), R=[Ttmp, Tmod], W=[Tost])
                        return extra
                    norm_mod_all(3, 4, sq, Tsq, rstd2, Trstd2, tmp2, Ttmp2, outf_t, lambda ti: (lambda c: [Th]), extra_t)
                ph2 = ExitStack()

                def pb2(name, shape, dt=F32):
                    UID[0] += 1; return ph2.enter_context(nc.sbuf_tensor("%s_%d" % (name, UID[0]), list(shape), dt))
                stage = pb2("r_stage", [128, 2, T]); Tstage = [Tl("st0"), Tl("st1")]
                xtmp2 = pb2("r_xtmp", [128, 2, 512]); Txt2 = [Tl("xt0"), Tl("xt1")]; xcnt = [0]
                wpc = pb2("r_wpc", [128, 2, KC, 256], BF16); Twpc = [Tl("wp0"), Tl("wp1")]
                wl1 = pb2("r_wl1", [128, KC, 160], BF16); Twl1 = Tl("wl1")
                wl2 = pb2("r_wl2", [128, 2, D], BF16); Twl2 = Tl("wl2")
                mid = pb2("r_mid", [128, 2, T], BF16); Tmid = Tl("mid")
                S.barrier()
                cnt = {"bank": 0, "st": 0, "pc": 0}

                def project(name, mm_fn, evac_fn):
                    for oc in range(KC):
                        sbi = cnt["st"] % 2; cnt["st"] += 1
                        for ti in range(NT):
                            c0, w, kind = tiles[ti]
                            bk = cnt["bank"] % 4; cnt["bank"] += 1
                            mms = mm_fn(oc, ti)
                            for i, (l_, r_, Rt) in enumerate(mms):
                                S.op("pe", lambda e, l_=l_, r_=r_, i=i, bk=bk, w=w: e.matmul(ps[bk][:, 0:w], l_, r_, start=(i == 0),
                                     stop=(i == len(mms) - 1)), R=Rt, W=[Tps[bk]], sig=(i == len(mms) - 1))
                            eng, fn, Rx = evac_fn(oc, ps[bk][:, 0:w], stage[:, sbi, c0:c0 + w])
                            S.op(eng, fn, R=[Tps[bk]] + Rx, W=[Tstage[sbi]])
                        S.dma("sp", scr[name][oc], stage[:, sbi, :], R=[Tstage[sbi]])

                def copy_evac(oc, src, dst):
                    return ("act", lambda e: e.activation(out=dst, in_=src, func=AF.Copy), [])

                def lora(w1ap, n1, w2ap, midf, outf, bias_col, name):
                    S.dma("pool", wl1[:, :, 0:n1], w1ap.rearrange("(k p) n -> p k n", p=128), W=[Twl1])
                    parts = [(0, min(n1, 128))] + ([(128, n1 - 128)] if n1 > 128 else [])
                    for pi, (r0, rn) in enumerate(parts):
                        S.dma("pool", wl2[0:rn, pi, :], w2ap[r0:r0 + rn, :], W=[Twl2])
                        for ti in range(NT):
                            c0, w, kind = tiles[ti]
                            bk = cnt["bank"] % 4; cnt["bank"] += 1
                            for k in range(KC):
                                S.op("pe", lambda e, k=k, bk=bk, r0=r0, rn=rn, c0=c0, w=w: e.matmul(ps[bk][0:rn, 0:w], wl1[:, k, r0:r0 + rn],
                                     xm[:, k, c0:c0 + w], start=(k == 0), stop=(k == KC - 1)), R=[Twl1, Txm], W=[Tps[bk]], sig=(k == KC - 1))
                            S.op("act", lambda e, bk=bk, rn=rn, pi=pi, c0=c0, w=w: e.activation(out=mid[0:rn, pi, c0:c0 + w],
                                 in_=ps[bk][0:rn, 0:w], func=midf), R=[Tps[bk]], W=[Tmid])

                    def mm_fn(oc, ti):
                        c0, w, kind = tiles[ti]
                        return [(wl2[0:rn, pi, oc * 128:(oc + 1) * 128], mid[0:rn, pi, c0:c0 + w], [Twl2, Tmid])
                                for pi, (r0, rn) in enumerate(parts)]

                    def evac_fn(oc, src, dst):
                        if bias_col is None:
                            return ("act", lambda e: e.activation(out=dst, in_=src, func=outf), [])
                        bc = bias_col + oc
                        return ("act", lambda e: e.activation(out=dst, in_=src, func=outf, bias=vecs[:, bc:bc + 1], scale=1.0), [Tvec])
                    project(name, mm_fn, evac_fn)

                for pj in range(6):
                    for c in range(KC):
                        muc = VOFF["rwkv_mu"] + jl * 48 + pj * 8 + c
                        omc = XO_OMMU + jl * 48 + pj * 8 + c
                        for (pc0, pw, pk) in tiles:
                            if pk != "p":
                                continue
                            xp_ = xcnt[0] % 2; xcnt[0] += 1
                            xtmp = xtmp2[:, xp_, :]; Txt = Txt2[xp_]
                            S.op("act", lambda e, c=c, muc=muc, pc0=pc0, pw=pw, xtmp=xtmp: e.activation(out=xtmp[:, 0:pw], in_=h[:, c, pc0:pc0 + pw],
                                 func=AF.Identity, scale=vecs[:, muc:muc + 1]), R=[Th, Tvec], W=[Txt])
                            S.op("dve", lambda e, c=c, omc=omc, pc0=pc0, pw=pw, xtmp=xtmp: e.scalar_tensor_tensor(xm[:, c, pc0:pc0 + pw],
                                 h[:, c, 1 + pc0:1 + pc0 + pw], vecs[:, omc:omc + 1], xtmp[:, 0:pw], ALU.mult, ALU.add),
                                 R=[Th, Txt, Tvec], W=[Txm])
                        xp_ = xcnt[0] % 2; xcnt[0] += 1
                        xtmp = xtmp2[:, xp_, :]; Txt = Txt2[xp_]
                        xs3 = xtmp[:, 0:NS * TS].rearrange("p (s t) -> p s t", t=TS)
                        S.op("act", lambda e, c=c, muc=muc, xs3=xs3: e.activation(out=xs3, in_=hS(c)[:, :, 0:TS], func=AF.Identity,
                             scale=vecs[:, muc:muc + 1]), R=[Th, Tvec], W=[Txt])
                        S.op("dve", lambda e, c=c, omc=omc, xs3=xs3: e.scalar_tensor_tensor(
                             xm[:, c, TP:T].rearrange("p (s t) -> p s t", t=TS), hS(c)[:, :, 1:1 + TS], vecs[:, omc:omc + 1], xs3,
                             ALU.mult, ALU.add), R=[Th, Txt, Tvec], W=[Txm])
                    if pj < 3:
                        wv = A["rwkv_w_rkv"][jl, pj].rearrange("(k p) n -> p k n", p=128)
                        slots = {}

                        def mm_fn(oc, ti, wv=wv, slots=slots):
                            c0, w, kind = tiles[ti]
                            pc, q = oc // 2, oc % 2
                            if pc not in slots:
                                sl = cnt["pc"] % 2; cnt["pc"] += 1
                                S.dma("pool", wpc[:, sl, :, :], wv[:, :, pc * 256:(pc + 1) * 256], W=[Twpc[sl]])
                                slots[pc] = sl
                            sl = slots[pc]
                            return [(wpc[:, sl, k, q * 128:(q + 1) * 128], xm[:, k, c0:c0 + w], [Twpc[sl], Txm]) for k in range(KC)]
                        project(("r", "k", "v")[pj], mm_fn, copy_evac)
                        if pj == 2 and jl == 1:
                            lora(A["rwkv_v1"][0], 32, A["rwkv_v2"][0], AF.Copy, AF.Sigmoid, VOFF["rwkv_v0"], "sv")
                    elif pj == 3:
                        lora(A["rwkv_w1"][jl], 64, A["rwkv_w2"][jl], AF.Tanh, AF.Sigmoid, VOFF["rwkv_w0"] + jl * 8, "sw")
                    elif pj == 4:
                        lora(A["rwkv_a1"][jl], 64, A["rwkv_a2"][jl], AF.Copy, AF.Sigmoid, VOFF["rwkv_a0"] + jl * 8, "a")
                    else:
                        lora(A["rwkv_g1"][jl], 160, A["rwkv_g2"][jl], AF.Sigmoid, AF.Copy, None, "g")
                ph2.close()
            tilesB = []
            for ti_, (c0_, w_, k_) in enumerate(tiles):
                if k_ == "p":
                    for o_ in range(0, w_, 256):
                        tilesB.append((ti_, c0_ + o_, min(256, w_ - o_), "p", False))
                else:
                    tilesB.append((ti_, c0_, w_, "s", False))
            lp_ = max(i for i, t_ in enumerate(tilesB) if t_[3] == "p")
            tilesB[lp_] = tilesB[lp_][:4] + (True,)
            with ExitStack() as ph:
                def pb_(name, shape, dt=F32):
                    UID[0] += 1; return ph.enter_context(nc.sbuf_tensor("%s_%d" % (name, UID[0]), list(shape), dt))

                def alloc_stream():
                    B = {}
                    B["Lb"] = (pb_("b_L", [128, 2, 6, 256]), [Tl("L0"), Tl("L1")])
                    Wk = {}
                    for nm in ("cum", "ecx", "ecp", "ecn", "etc", "kkn", "kmod", "bv", "tq", "tk"):
                        Wk[nm] = (pb_("b_" + nm, [128, 256]), Tl(nm))
                    Wk["Yt"] = Wk["cum"]; Wk["cen"] = Wk["ecx"]; Wk["bon"] = Wk["etc"]
                    B["Wk"] = Wk
                    B["AR"] = (pb_("b_AR", [128, 512], BF16), Tl("AR"))
                    B["BK"] = (pb_("b_BK", [128, 512], BF16), Tl("BK"))
                    B["BKH"] = (pb_("b_BKH", [128, 3, 256], BF16), Tl("BKH"))
                    B["TM"] = (pb_("b_TM", [128, 3072], BF16), Tl("TM"))
                    Gb = {}
                    for nm in ("Lab", "LabT", "Xa", "XTa", "Ta", "Tb"):
                        Gb[nm] = (pb_("b_" + nm, [128, 256]), Tl(nm))
                    for nm in ("Mbr", "Lkb", "Mkr", "Tbf"):
                        Gb[nm] = (pb_("b_" + nm, [128, 256], BF16), Tl(nm))
                    B["Gb"] = Gb
                    B["WTb"] = (pb_("b_WTb", [128, 8, 64], BF16), Tl("WTb"))
                    B["PTb"] = (pb_("b_PTb", [128, 8, 64], BF16), Tl("PTb"))
                    B["Sio"] = (pb_("b_Sio", [128, NS, 64]), Tl("Sio"))
                    B["STs"] = (pb_("b_STs", [128, NS, 64]), Tl("STs"))
                    B["STsb"] = (pb_("b_STsb", [128, NS, 64], BF16), Tl("STsb"))
                    B["eg"] = (pb_("b_eg", [128, 16]), Tl("eg"))
                    B["ybc"] = (pb_("b_ybc", [128, 256], BF16), Tl("ybc"))
                    B["woc"] = (pb_("b_woc", [128, D], BF16), Tl("woc"))
                    B["rtmp"] = (pb_("b_rtmp", [128, 256]), Tl("rtmp"))
                    return B
                BFS = [alloc_stream(), alloc_stream()]
                S.barrier()
                NEG = -float(np.exp(-0.5))

                def stream(c, B, si):
                    Lb2, TL2 = B["Lb"]; Wk = B["Wk"]; AR, TAR = B["AR"]; BK, TBK = B["BK"]; BKH, TBKH = B["BKH"]; TM, TTM = B["TM"]
                    Gb = B["Gb"]; WTb, TWTb = B["WTb"]; PTb, TPTb = B["PTb"]; Sio, TSio = B["Sio"]; STs, TST = B["STs"]
                    STsb, TSTb = B["STsb"]; eg, Teg = B["eg"]; ybc, Tybc = B["ybc"]; woc, Twoc = B["woc"]; rtmp, Trtmp = B["rtmp"]
                    tq3s = Sio; Ttq3s = TSio
                    b0, b1, b2 = (0, 1, 2) if si == 0 else (3, 4, 5)
                    S.dma("pool", woc[:, :], A["rwkv_w_o"][jl, c * 128:(c + 1) * 128, :], W=[Twoc])
                    kkc = VOFF["rwkv_k_k"] + jl * 8 + c; kac = VOFF["rwkv_k_a"] + jl * 8 + c; omka = XO_OMKA + jl * 8 + c
                    rkc = VOFF["rwkv_r_k"] + jl * 8 + c; lnw = VOFF["rwkv_ln_w"] + jl * 8 + c; lnb = VOFF["rwkv_ln_b"] + jl * 8 + c
                    S.op("dve", lambda e: e.memset(STs[:, 0:1, :], 0.0), W=[TST])
                    S.op("dve", lambda e: e.memset(STsb[:, 0:1, :], 0.0), W=[TSTb])
                    for sub_i, (ti, c0, w, kind, lastp) in enumerate(tilesB):
                        yield
                        Lb = Lb2[:, sub_i % 2, :, :]; TL = TL2[sub_i % 2]
                        C = 64 if kind == "p" else TS
                        nch = w // C
                        gsz = 4 if kind == "p" else 16
                        gsz = min(gsz, nch)
                        v3 = lambda ap: ap[:, 0:w].rearrange("p (n t) -> p n t", t=C)
                        ARv = AR[:, 0:nch * 2 * C].rearrange("p (n a t) -> p n a t", a=2, t=C)
                        BKv = BK[:, 0:nch * 2 * C].rearrange("p (n a t) -> p n a t", a=2, t=C)
                        TMv = TM[:, 0:nch * 192].rearrange("p (n a k) -> p n a k", a=3, k=64)
                        Gv = lambda nm: Gb[nm][0][:, 0:nch * C].rearrange("p (n t) -> p n t", t=C)
                        cmo = 0 if kind == "p" else 512
                        Lr, Lk, Lv, Lsw, La, Lg = [Lb[:, i, 0:w] for i in range(6)]
                        W_ = lambda nm: Wk[nm][0][:, 0:w]
                        TW = lambda nm: Wk[nm][1]
                        if kind == "s":
                            S.dma("sp", Sio[:, :, :], A["s_wkv"][jl, :, 2 * c:2 * c + 2, :, :].rearrange("s h v k -> (h v) s k"), W=[TSio])
                            for half in range(2):
                                bk = (b0, b1)[half]
                                for l in range(8):
                                    s_ = half * 8 + l
                                    for p in range(2):
                                        kr = slice(p * 64, (p + 1) * 64)
                                        S.op("pe", lambda e, l=l, s_=s_, kr=kr, bk=bk, p=p: e.matmul(ps[bk][kr, l * 64:(l + 1) * 64],
                                             Sio[kr, s_, :], identf[kr, kr], start=True, stop=True), R=[TSio, Tconst], W=[Tps[bk]],
                                             sig=(l == 7 and p == 1))
                                S.op("act", lambda e, half=half, bk=bk: e.activation(out=STs[:, half * 8:half * 8 + 8, :],
                                     in_=ps[bk][:, :].rearrange("p (s v) -> p s v", v=64), func=AF.Copy), R=[Tps[bk]], W=[TST])
                            S.op("act", lambda e: e.activation(out=STsb[:, :, :], in_=STs[:, :, :], func=AF.Copy), R=[TST], W=[TSTb])
                        names = ["r", "k", "v", "sw", "a", "g"]
                        S.dma_group("sp", [(Lb[:, i, 0:w], scr[nm][c, :, c0:c0 + w]) for i, nm in enumerate(names)], W=[TL])
                        if jl == 1:
                            S.dma("sp", W_("kkn"), scr["sv"][c, :, c0:c0 + w], W=[TW("kkn")])
                            S.dma("sp", W_("kmod"), SCR[0]["v"][c, :, c0:c0 + w], W=[TW("kmod")])
                            S.op("dve", lambda e: e.tensor_tensor(W_("tq"), W_("kmod"), Lv, ALU.subtract), R=[TL, TW("kmod")], W=[TW("tq")])
                            S.op("dve", lambda e: e.tensor_tensor(W_("tq"), W_("tq"), W_("kkn"), ALU.mult), R=[TW("kkn"), TW("tq")], W=[TW("tq")])
                            S.op("dve", lambda e: e.tensor_tensor(Lv, Lv, W_("tq"), ALU.add), R=[TL, TW("tq")], W=[TL])
                        yield
                        S.op("dve", lambda e: e.tensor_scalar(Lsw, Lsw, NEG, None, ALU.mult), R=[TL], W=[TL])
                        S.op("dve", lambda e: e.tensor_tensor_scan(W_("cum"), cmk[:, cmo:cmo + w], Lsw, 0.0, ALU.mult, ALU.add),
                             R=[TL, Tconst], W=[TW("cum")])
                        S.op("dve", lambda e: e.tensor_tensor(W_("tq"), W_("cum"), Lsw, ALU.subtract), R=[TL, TW("cum")], W=[TW("tq")])
                        yield
                        S.op("act", lambda e: e.activation(out=W_("ecx"), in_=W_("tq"), func=AF.Exp), R=[TW("tq")], W=[TW("ecx")])
                        S.op("act", lambda e: e.activation(out=W_("ecp"), in_=W_("cum"), func=AF.Exp), R=[TW("cum")], W=[TW("ecp")])
                        S.op("act", lambda e: e.activation(out=W_("ecn"), in_=W_("cum"), func=AF.Exp, scale=-1.0), R=[TW("cum")], W=[TW("ecn")])
                        S.op("dve", lambda e: e.tensor_tensor(v3(Wk["tq"][0]), v3(Wk["cum"][0])[:, :, C - 1:C].to_broadcast([128, nch, C]),
                             v3(Wk["cum"][0]), ALU.subtract), R=[TW("cum"), TW("ecx")], W=[TW("tq")])
                        S.op("act", lambda e: e.activation(out=W_("etc"), in_=W_("tq"), func=AF.Exp), R=[TW("tq")], W=[TW("etc")])
                        S.op("act", lambda e: e.activation(out=eg[:, 0:nch], in_=v3(Wk["cum"][0])[:, :, C - 1], func=AF.Exp),
                             R=[TW("cum")], W=[Teg])
                        yield
                        S.op("dve", lambda e: e.tensor_scalar(W_("kkn"), Lk, vecs[:, kkc:kkc + 1], None, ALU.mult), R=[TL, Tvec], W=[TW("kkn")])
                        S.op("act", lambda e: e.activation(out=W_("tk"), in_=W_("kkn"), func=AF.Square), R=[TW("kkn")], W=[TW("tk")])
                        S.op("pe", lambda e: e.matmul(ps[b0][:, 0:w], blk1[:], W_("tk"), start=True, stop=True), R=[TW("tk"), Tconst], W=[Tps[b0]])
                        yield
                        S.op("dve", lambda e: e.tensor_scalar(W_("tk"), ps[b0][:, 0:w], 1e-24, None, ALU.max), R=[Tps[b0]], W=[TW("tk")])
                        S.op("act", lambda e: e.activation(out=W_("tk"), in_=W_("tk"), func=AF.Ln), R=[TW("tk")], W=[TW("tk")])
                        S.op("act", lambda e: e.activation(out=W_("tk"), in_=W_("tk"), func=AF.Exp, scale=-0.5), R=[TW("tk")], W=[TW("tk")])
                        S.op("dve", lambda e: e.tensor_tensor(W_("kkn"), W_("kkn"), W_("tk"), ALU.mult), R=[TW("tk"), TW("kkn")], W=[TW("kkn")])
                        yield
                        S.op("dve", lambda e: e.tensor_scalar(W_("kmod"), La, vecs[:, kac:kac + 1], vecs[:, omka:omka + 1], ALU.mult, ALU.add),
                             R=[TL, Tvec], W=[TW("kmod")])
                        S.op("dve", lambda e: e.tensor_tensor(W_("kmod"), W_("kmod"), Lk, ALU.mult), R=[TL, TW("kmod")], W=[TW("kmod")])
                        S.op("dve", lambda e: e.tensor_tensor(W_("bv"), W_("kkn"), La, ALU.mult), R=[TL, TW("kkn")], W=[TW("bv")])
                        yield
                        S.op("dve", lambda e: e.scalar_tensor_tensor(ARv[:, :, 0, :], v3(Wk["kkn"][0]), -1.0, v3(Wk["ecx"][0]), ALU.mult, ALU.mult),
                             R=[TW("kkn"), TW("ecx")], W=[TAR])
                        S.op("dve", lambda e: e.tensor_tensor(ARv[:, :, 1, :], v3(Lb[:, 0, :]), v3(Wk["ecp"][0]), ALU.mult), R=[TL, TW("ecp")], W=[TAR])
                        S.op("dve", lambda e: e.tensor_tensor(BKv[:, :, 0, :], v3(Wk["bv"][0]), v3(Wk["ecn"][0]), ALU.mult), R=[TW("bv"), TW("ecn")], W=[TBK])
                        S.op("dve", lambda e: e.tensor_tensor(BKv[:, :, 1, :], v3(Wk["kmod"][0]), v3(Wk["ecn"][0]), ALU.mult), R=[TW("kmod"), TW("ecn")], W=[TBK])
                        S.op("dve", lambda e: e.tensor_tensor(BKH[:, 0, 0:w], W_("bv"), W_("etc"), ALU.mult), R=[TW("bv"), TW("etc")], W=[TBKH])
                        S.op("dve", lambda e: e.tensor_tensor(BKH[:, 1, 0:w], W_("kmod"), W_("etc"), ALU.mult), R=[TW("kmod"), TW("etc")], W=[TBKH])
                        S.op("act", lambda e: e.activation(out=BKH[:, 2, 0:w], in_=Lv, func=AF.Copy), R=[TL], W=[TBKH])
                        yield
                        S.op("dve", lambda e: e.tensor_tensor(W_("bv"), Lr, W_("kmod"), ALU.mult), R=[TL, TW("kmod")], W=[TW("bv")])
                        S.op("dve", lambda e: e.tensor_scalar(W_("bv"), W_("bv"), vecs[:, rkc:rkc + 1], None, ALU.mult), R=[TW("bv"), Tvec], W=[TW("bv")])
                        S.op("pe", lambda e: e.matmul(ps[b1][:, 0:w], blk1[:], W_("bv"), start=True, stop=True), R=[TW("bv"), Tconst], W=[Tps[b1]])
                        S.op("dve", lambda e: e.tensor_tensor(W_("bon"), ps[b1][:, 0:w], Lv, ALU.mult), R=[Tps[b1], TL], W=[TW("bon")])
                        yield
                        for g0 in range(0, nch, 4):
                            gn = min(4, nch - g0)
                            for l in range(gn):
                                ch = g0 + l
                                for p in range(2):
                                    kr = slice(p * 64, (p + 1) * 64); cr = slice(p * 64, p * 64 + C)
                                    for a in range(3):
                                        S.op("pe", lambda e, l=l, ch=ch, kr=kr, cr=cr, a=a: e.transpose(
                                             psb[cr, l * 192 + a * 64:l * 192 + (a + 1) * 64], BKH[kr, a, ch * C:(ch + 1) * C], identb[kr, kr]),
                                             R=[TBKH, Tconst], W=[Tpsb], sig=(l == gn - 1 and p == 1 and a == 2))
                            S.op("act", lambda e, g0=g0, gn=gn: e.activation(out=TMv[:, g0:g0 + gn, :, :].rearrange("p n a k -> p (n a k)"),
                                 in_=psb[:, 0:gn * 192], func=AF.Copy), R=[Tpsb], W=[TTM])
                        yield
                        nlev = {64: 6, 8: 3}[C]
                        Tfin = None
                        for g0 in range(0, nch, gsz):
                            for l in range(gsz):
                                ch = g0 + l
                                for p in range(2):
                                    kr = slice(p * 64, (p + 1) * 64); cr = slice(p * 64, p * 64 + C)
                                    lastm = (l == gsz - 1 and p == 1)
                                    S.op("pe", lambda e, l=l, ch=ch, kr=kr, cr=cr: e.matmul(ps[b0][cr, l * 2 * C:(l + 1) * 2 * C], BKv[kr, ch, 0, :],
                                         ARv[kr, ch, :, :].rearrange("p a t -> p (a t)"), start=True, stop=True), R=[TBK, TAR], W=[Tps[b0]], sig=lastm)
                                    S.op("pe", lambda e, l=l, ch=ch, kr=kr, cr=cr: e.matmul(ps[b1][cr, l * 2 * C:(l + 1) * 2 * C], BKv[kr, ch, 1, :],
                                         ARv[kr, ch, :, :].rearrange("p a t -> p (a t)"), start=True, stop=True), R=[TBK, TAR], W=[Tps[b1]], sig=lastm)
                                    S.op("pe", lambda e, l=l, ch=ch, kr=kr, cr=cr: e.matmul(ps[b2][cr, l * C:(l + 1) * C], ARv[kr, ch, 0, :],
                                         BKv[kr, ch, 0, :], start=True, stop=True), R=[TBK, TAR], W=[Tps[b2]], sig=lastm)
                            yield
                            gs = slice(g0, g0 + gsz)
                            p0v = ps[b0][:, 0:gsz * 2 * C].rearrange("p (n a t) -> p n a t", a=2, t=C)
                            p1v = ps[b1][:, 0:gsz * 2 * C].rearrange("p (n a t) -> p n a t", a=2, t=C)
                            pgv = lambda i: ps[i][:, 0:gsz * C].rearrange("p (n t) -> p n t", t=C)
                            mb = lambda i: msk[:, i, 0:C].unsqueeze(1).to_broadcast([128, gsz, C])
                            S.op("dve", lambda e: e.tensor_tensor(Gv("Lab")[:, gs, :], p0v[:, :, 0, :], mb(0), ALU.mult), R=[Tps[b0], Tconst], W=[Gb["Lab"][1]])
                            S.op("dve", lambda e: e.tensor_tensor(Gv("Mbr")[:, gs, :], p0v[:, :, 1, :], mb(1), ALU.mult), R=[Tps[b0], Tconst], W=[Gb["Mbr"][1]])
                            S.op("dve", lambda e: e.tensor_tensor(Gv("Lkb")[:, gs, :], p1v[:, :, 0, :], mb(0), ALU.mult), R=[Tps[b1], Tconst], W=[Gb["Lkb"][1]])
                            S.op("dve", lambda e: e.tensor_tensor(Gv("Mkr")[:, gs, :], p1v[:, :, 1, :], mb(1), ALU.mult), R=[Tps[b1], Tconst], W=[Gb["Mkr"][1]])
                            S.op("dve", lambda e: e.tensor_tensor(Gv("LabT")[:, gs, :], pgv(b2), mb(2), ALU.mult), R=[Tps[b2], Tconst], W=[Gb["LabT"][1]])
                            yield
                            Xn, XTn, Tc, Tn = "Lab", "LabT", "Ta", "Tb"
                            Xo, XTo = "Xa", "XTa"
                            S.op("dve", lambda e, Xn=Xn, Tc=Tc: e.tensor_tensor(Gv(Tc)[:, gs, :], Gv(Xn)[:, gs, :], mb(3), ALU.add),
                                 R=[Gb[Xn][1], Tconst], W=[Gb[Tc][1]])
                            for lev in range(1, nlev):
                                lastl = (lev == nlev - 1)
                                for l in range(gsz):
                                    ch = g0 + l
                                    for p in range(2):
                                        cr = slice(p * 64, p * 64 + C)
                                        lastm = (l == gsz - 1 and p == 1)
                                        if not lastl:
                                            S.op("pe", lambda e, l=l, ch=ch, cr=cr, Xn=Xn, XTn=XTn: e.matmul(ps[b2][cr, l * C:(l + 1) * C], Gv(XTn)[cr, ch, :],
                                                 Gv(Xn)[cr, ch, :], start=True, stop=True), R=[Gb[Xn][1], Gb[XTn][1]], W=[Tps[b2]], sig=lastm)
                                        S.op("pe", lambda e, l=l, ch=ch, cr=cr, Xn=Xn, XTn=XTn: e.matmul(ps[b0][cr, l * C:(l + 1) * C], Gv(Xn)[cr, ch, :],
                                             Gv(XTn)[cr, ch, :], start=True, stop=True), R=[Gb[Xn][1], Gb[XTn][1]], W=[Tps[b0]], sig=lastm)
                                yield
                                if not lastl:
                                    S.op("act", lambda e, Xo=Xo: e.activation(out=Gv(Xo)[:, gs, :], in_=pgv(b2), func=AF.Copy), R=[Tps[b2]], W=[Gb[Xo][1]])
                                S.op("act", lambda e, XTo=XTo: e.activation(out=Gv(XTo)[:, gs, :], in_=pgv(b0), func=AF.Copy), R=[Tps[b0]], W=[Gb[XTo][1]])
                                yield
                                for l in range(gsz):
                                    ch = g0 + l
                                    for p in range(2):
                                        cr = slice(p * 64, p * 64 + C)
                                        S.op("pe", lambda e, l=l, ch=ch, cr=cr, XTo=XTo, Tc=Tc: e.matmul(ps[b1][cr, l * C:(l + 1) * C], Gv(XTo)[cr, ch, :],
                                             Gv(Tc)[cr, ch, :], start=True, stop=True), R=[Gb[XTo][1], Gb[Tc][1]], W=[Tps[b1]], sig=(l == gsz - 1 and p == 1))
                                S.op("dve", lambda e, Tc=Tc, Tn=Tn: e.tensor_tensor(Gv(Tn)[:, gs, :], pgv(b1), Gv(Tc)[:, gs, :], ALU.add),
                                     R=[Tps[b1], Gb[Tc][1]], W=[Gb[Tn][1]])
                                yield
                                Xn, Xo = Xo, Xn
                                XTn, XTo = XTo, XTn
                                Tc, Tn = Tn, Tc
                            S.op("act", lambda e, Tc=Tc: e.activation(out=Gv("Tbf")[:, gs, :], in_=Gv(Tc)[:, gs, :], func=AF.Copy),
                                 R=[Gb[Tc][1]], W=[Gb["Tbf"][1]])
                            Tfin = "Tbf"
                        yield
                        sets = [[ch] for ch in range(nch)] if kind == "p" else [list(range(0, 8)), list(range(8, 16))]
                        for chs in sets:
                            n = len(chs)
                            for l, ch in enumerate(chs):
                                si = 0 if kind == "p" else ch
                                for p in range(2):
                                    kr = slice(p * 64, (p + 1) * 64); cr = slice(p * 64, p * 64 + C)
                                    S.op("pe", lambda e, l=l, ch=ch, si=si, kr=kr, cr=cr: e.matmul(ps[b2][cr, l * 64:(l + 1) * 64], ARv[kr, ch, 0, :],
                                         STsb[kr, si, :], start=True, stop=False), R=[TAR, TSTb], W=[Tps[b2]], sig=False)
                                    S.op("pe", lambda e, l=l, ch=ch, cr=cr: e.matmul(ps[b2][cr, l * 64:(l + 1) * 64], Gv("Lkb")[cr, ch, :],
                                         TMv[cr, ch, 2, :], start=False, stop=True), R=[Gb["Lkb"][1], TTM], W=[Tps[b2]], sig=(l == n - 1 and p == 1))
                            yield
                            S.op("act", lambda e, n=n: e.activation(out=WTb[:, 0:n, :].rearrange("p n v -> p (n v)"), in_=ps[b2][:, 0:n * 64], func=AF.Copy),
                                 R=[Tps[b2]], W=[TWTb])
                            for l, ch in enumerate(chs):
                                for p in range(2):
                                    cr = slice(p * 64, p * 64 + C)
                                    S.op("pe", lambda e, l=l, ch=ch, cr=cr: e.matmul(ps[b0][cr, l * 64:(l + 1) * 64], Gv(Tfin)[cr, ch, :], WTb[cr, l, :],
                                         start=True, stop=True), R=[Gb[Tfin][1], TWTb], W=[Tps[b0]], sig=(l == n - 1 and p == 1))
                            yield
                            S.op("dve", lambda e, n=n: e.tensor_copy(PTb[:, 0:n, :].rearrange("p n v -> p (n v)"), ps[b0][:, 0:n * 64]), R=[Tps[b0]], W=[TPTb])
                            for l, ch in enumerate(chs):
                                si = 0 if kind == "p" else ch
                                for p in range(2):
                                    kr = slice(p * 64, (p + 1) * 64); cr = slice(p * 64, p * 64 + C)
                                    lastm = (l == n - 1 and p == 1)
                                    S.op("pe", lambda e, l=l, ch=ch, si=si, kr=kr: e.matmul(ps[b1][kr, ch * C:(ch + 1) * C], STsb[kr, si, :], ARv[kr, ch, 1, :],
                                         start=True, stop=False), R=[TSTb, TAR], W=[Tps[b1]], sig=False)
                                    S.op("pe", lambda e, l=l, ch=ch, kr=kr, cr=cr: e.matmul(ps[b1][kr, ch * C:(ch + 1) * C], PTb[cr, l, :], Gv("Mbr")[cr, ch, :],
                                         start=False, stop=False), R=[TPTb, Gb["Mbr"][1]], W=[Tps[b1]], sig=False)
                                    S.op("pe", lambda e, l=l, ch=ch, kr=kr, cr=cr: e.matmul(ps[b1][kr, ch * C:(ch + 1) * C], TMv[cr, ch, 2, :], Gv("Mkr")[cr, ch, :],
                                         start=False, stop=True), R=[TTM, Gb["Mkr"][1]], W=[Tps[b1]], sig=lastm)
                                    S.op("pe", lambda e, l=l, ch=ch, kr=kr, cr=cr: e.matmul(ps[b2][kr, l * 64:(l + 1) * 64], TMv[cr, ch, 0, :], PTb[cr, l, :],
                                         start=True, stop=False), R=[TTM, TPTb], W=[Tps[b2]], sig=False)
                                    S.op("pe", lambda e, l=l, ch=ch, kr=kr, cr=cr: e.matmul(ps[b2][kr, l * 64:(l + 1) * 64], TMv[cr, ch, 1, :], TMv[cr, ch, 2, :],
                                         start=False, stop=True), R=[TTM], W=[Tps[b2]], sig=lastm)
                            yield
                            ch0 = chs[0]
                            if kind == "p":
                                S.op("dve", lambda e, ch0=ch0: e.scalar_tensor_tensor(STs[:, 0, :], STs[:, 0, :], eg[:, ch0:ch0 + 1], ps[b2][:, 0:64],
                                     ALU.mult, ALU.add), R=[TST, Teg, Tps[b2]], W=[TST])
                                S.op("act", lambda e: e.activation(out=STsb[:, 0, :], in_=STs[:, 0, :], func=AF.Copy), R=[TST], W=[TSTb])
                            else:
                                S.op("dve", lambda e, ch0=ch0, n=n: e.tensor_tensor(tq3s[:, 0:n, :], STs[:, ch0:ch0 + n, :],
                                     eg[:, ch0:ch0 + n].unsqueeze(2).to_broadcast([128, n, 64]), ALU.mult), R=[TST, Teg], W=[Ttq3s])
                                S.op("dve", lambda e, ch0=ch0, n=n: e.tensor_tensor(STs[:, ch0:ch0 + n, :], tq3s[:, 0:n, :],
                                     ps[b2][:, 0:n * 64].rearrange("p (n v) -> p n v", v=64), ALU.add), R=[Ttq3s, Tps[b2], TST], W=[TST])
                                S.op("act", lambda e, ch0=ch0, n=n: e.activation(out=STsb[:, ch0:ch0 + n, :], in_=STs[:, ch0:ch0 + n, :], func=AF.Copy),
                                     R=[TST], W=[TSTb])
                        yield
                        S.op("act", lambda e: e.activation(out=W_("Yt"), in_=ps[b1][:, 0:w], func=AF.Copy), R=[Tps[b1]], W=[TW("Yt")])
                        S.op("pe", lambda e: e.matmul(ps[b0][:, 0:w], blk64[:], W_("Yt"), start=True, stop=True), R=[TW("Yt"), Tconst], W=[Tps[b0]])
                        S.op("dve", lambda e: e.tensor_tensor(W_("cen"), W_("Yt"), ps[b0][:, 0:w], ALU.subtract), R=[TW("Yt"), Tps[b0]], W=[TW("cen")])
                        S.op("act", lambda e: e.activation(out=W_("tq"), in_=W_("cen"), func=AF.Square), R=[TW("cen")], W=[TW("tq")])
                        S.op("pe", lambda e: e.matmul(ps[b1][:, 0:w], blk64[:], W_("tq"), start=True, stop=True), R=[TW("tq"), Tconst], W=[Tps[b1]])
                        yield
                        S.op("act", lambda e: e.activation(out=W_("tq"), in_=ps[b1][:, 0:w], func=AF.Ln, bias=64e-5, scale=1.0), R=[Tps[b1]], W=[TW("tq")])
                        S.op("act", lambda e: e.activation(out=W_("tq"), in_=W_("tq"), func=AF.Exp, scale=-0.5), R=[TW("tq")], W=[TW("tq")])
                        S.op("dve", lambda e: e.tensor_tensor(W_("cen"), W_("cen"), W_("tq"), ALU.mult), R=[TW("tq"), TW("cen")], W=[TW("cen")])
                        S.op("dve", lambda e: e.tensor_scalar(W_("cen"), W_("cen"), vecs[:, lnw:lnw + 1], vecs[:, lnb:lnb + 1], ALU.mult, ALU.add),
                             R=[TW("cen"), Tvec], W=[TW("cen")])
                        S.op("dve", lambda e: e.tensor_tensor(W_("cen"), W_("cen"), W_("bon"), ALU.add), R=[TW("cen"), TW("bon")], W=[TW("cen")])
                        S.op("dve", lambda e: e.tensor_tensor(ybc[:, 0:w], W_("cen"), Lg, ALU.mult), R=[TW("cen"), TL], W=[Tybc])
                        yield
                        for dc in range(KC):
                            bk = (b0, b1)[dc % 2]
                            S.op("pe", lambda e, dc=dc, bk=bk: e.matmul(ps[bk][:, 0:w], woc[:, dc * 128:(dc + 1) * 128], ybc[:, 0:w], start=True, stop=True),
                                 R=[Twoc, Tybc], W=[Tps[bk]])
                            resid_update(ti, dc, ps[bk][:, 0:w], Tps[bk], 5, rtmp, Trtmp, sub=(c0, w, kind))
                            if dc % 2 == 1:
                                yield
                        if lastp:
                            for p in range(2):
                                kr = slice(p * 64, (p + 1) * 64)
                                S.op("pe", lambda e, kr=kr: e.matmul(ps[b0][kr, 0:64], STs[kr, 0, :], identf[kr, kr], start=True, stop=True),
                                     R=[TST, Tconst], W=[Tps[b0]], sig=(p == 1))
                            S.op("act", lambda e: e.activation(out=Sio[:, 0, :], in_=ps[b0][:, 0:64], func=AF.Copy), R=[Tps[b0]], W=[TSio])
                            S.dma("sp", A["o_p_wkv"][jl, 2 * c:2 * c + 2, :, :].rearrange("h v k -> (h v) k"), Sio[:, 0, :], R=[TSio])
                        if kind == "s":
                            for half in range(2):
                                bk = (b0, b1)[half]
                                for l in range(8):
                                    s_ = half * 8 + l
                                    for p in range(2):
                                        kr = slice(p * 64, (p + 1) * 64)
                                        S.op("pe", lambda e, l=l, s_=s_, kr=kr, bk=bk: e.matmul(ps[bk][kr, l * 64:(l + 1) * 64], STs[kr, s_, :],
                                             identf[kr, kr], start=True, stop=True), R=[TST, Tconst], W=[Tps[bk]], sig=(l == 7 and p == 1))
                                S.op("act", lambda e, half=half, bk=bk: e.activation(out=Sio[:, half * 8:half * 8 + 8, :],
                                     in_=ps[bk][:, :].rearrange("p (s k) -> p s k", k=64), func=AF.Copy), R=[Tps[bk]], W=[TSio])
                            S.dma("sp", A["o_s_wkv"][jl, :, 2 * c:2 * c + 2, :, :].rearrange("s h v k -> (h v) s k"), Sio[:, :, :], R=[TSio])
                for pair in range(0, KC, 2):
                    gens = [stream(pair, BFS[0], 0), stream(pair + 1, BFS[1], 1)]
                    alive = list(gens)
                    while alive:
                        for g_ in list(alive):
                            try:
                                next(g_)
                            except StopIteration:
                                alive.remove(g_)

        def phase_final():
            with ExitStack() as ph:
                def pb_(name, shape, dt=F32):
                    UID[0] += 1; return ph.enter_context(nc.sbuf_tensor("%s_%d" % (name, UID[0]), list(shape), dt))
                sq = pb_("o_sq", [128, 2, 512], BF16); Tsq = [Tl("sq0"), Tl("sq1")]
                rstd = pb_("o_rstd", [128, 512]); Trstd = Tl("rstd")
                yt = pb_("o_yt", [128, KC, 512]); Tyt = Tl("yt")
                yo = pb_("o_yo", [128, 2, D]); Tyo = [Tl("yo0"), Tl("yo1")]
                S.barrier()
                g0 = VOFF["final_gain"]
                bi = 0
                for ti in range(NT):
                    c0, w, kind = tiles[ti]
                    norm_stats(ti, sq, Tsq, rstd, Trstd)
                    for c in range(KC):
                        S.op("dve", lambda e, c=c: e.scalar_tensor_tensor(yt[:, c, 0:w], x[:, c, c0:c0 + w], vecs[:, g0 + c:g0 + c + 1],
                                                                          rstd[:, 0:w], ALU.mult, ALU.mult),
                             R=[Tx[c][ti], Trstd, Tvec], W=[Tyt])
                    for b in range(w // 128):
                        yb = bi % 2; bi += 1
                        for half in range(2):
                            pbk = half
                            for q in range(4):
                                c = half * 4 + q
                                S.op("pe", lambda e, c=c, q=q, b=b, pbk=pbk: e.transpose(ps[pbk][:, q * 128:(q + 1) * 128],
                                     yt[:, c, b * 128:(b + 1) * 128], identf[:]), R=[Tyt, Tconst], W=[Tps[pbk]], sig=(q == 3))
                            if half == 0:
                                S.op("act", lambda e, yb=yb, pbk=pbk: e.activation(out=yo[:, yb, 0:512], in_=ps[pbk][:, :], func=AF.Copy),
                                     R=[Tps[pbk]], W=[Tyo[yb]])
                            else:
                                S.op("dve", lambda e, yb=yb, pbk=pbk: e.tensor_copy(yo[:, yb, 512:1024], ps[pbk][:, :]),
                                     R=[Tps[pbk]], W=[Tyo[yb]])
                        t0 = c0 + b * 128
                        dst = A["yp"][t0:t0 + 128, :] if kind == "p" else A["ys"][:, :]
                        S.dma("sp", dst, yo[:, yb, :], R=[Tyo[yb]])
                with nc.sbuf_tensor("o_sm", [128, 2, D], F32) as osm:
                    Tosm = Tl("osm")
                    S.barrier()
                    NCOL = 10 * NSEQ
                    for blk, (b0, bn) in enumerate([(0, 128), (128, NCOL - 128)]):
                        for c in range(KC):
                            pbk = c % 2
                            S.op("pe", lambda e, c=c, b0=b0, bn=bn, pbk=pbk: e.transpose(ps[pbk][0:bn, 0:128], ost[:, c, b0:b0 + bn],
                                                                                      identf[:]), R=[Tost, Tconst], W=[Tps[pbk]])
                            S.op("act", lambda e, c=c, bn=bn, blk=blk, pbk=pbk: e.activation(
                                out=osm[0:bn, blk, c * 128:(c + 1) * 128], in_=ps[pbk][0:bn, 0:128], func=AF.Copy),
                                R=[Tps[pbk]], W=[Tosm])
                    opairs = []
                    for r in range(10):
                        col = r * NSEQ
                        blk, prow = (0, col) if col < 128 else (1, col - 128)
                        opairs.append((A["o_p_small"][r:r + 1, :], osm[prow:prow + 1, blk, :]))
                        s0_ = 0
                        while s0_ < NS:
                            col = r * NSEQ + 1 + s0_
                            blk, prow = (0, col) if col < 128 else (1, col - 128)
                            m = min(NS - s0_, (128 - prow) if blk == 0 else NS)
                            opairs.append((A["o_s_small"][r, s0_:s0_ + m, :], osm[prow:prow + m, blk, :]))
                            s0_ += m
                    S.dma_group("sp", opairs, R=[Tosm])

        phase_mod(0)
        for layer in range(DEPTH):
            modT = modTs[layer % 2]; Tmod = Tmods[layer % 2]
            if layer == 0:
                phase_ffn(0, 0, co_layer=0, co_range=(MOD0_SPLIT, 36))
            else:
                phase_ffn(layer, 0)
            if layer % 2 == 0 and do_lru:
                phase_lru(layer)
            if layer % 2 == 1 and do_rwkv:
                phase_rwkv(layer)
            phase_ffn(layer, 1, co_layer=(layer + 1 if layer + 1 < DEPTH else None))
        phase_final()
        S.drain("sp")
        build.nins = S.nins
    return nc


def make_consts():
    ident = np.eye(128, dtype=np.float32)
    m = np.zeros((128, 4, 64), np.float32)
    sidx = (np.arange(128) % 64)[:, None]; t = np.arange(64)[None, :]
    m[:, 0, :] = (t > sidx)
    m[:, 1, :] = (t >= sidx)
    m[:, 2, :] = (t < sidx)
    m[:, 3, :] = (t == sidx)
    return ident, m


def pack_inputs(inp, core, TP):
    g = lambda k: np.asarray(inp[k], dtype=np.float32)
    ident, m = make_consts()
    s0, s1 = core * NS, (core + 1) * NS
    cm = np.ones((128, 640), np.float32); cm[:, 0:512:64] = 0.0; cm[:, 512:640:8] = 0.0
    d = {"c_ident": ident, "c_mask": m, "c_cm": cm,
         "xp": np.ascontiguousarray(g("x_prompt")[core, :TP]),
         "xs": np.ascontiguousarray(g("x_sample")[s0:s1].reshape(NS * TS, D)),
         "cc": np.ascontiguousarray(np.concatenate([g("c_prompt")[core:core + 1], g("c_sample")[s0:s1]], 0)),
         "s_lru_h": np.ascontiguousarray(g("state_lru_h")[:, s0:s1]),
         "s_lru_conv": np.ascontiguousarray(g("state_lru_conv")[:, s0:s1].reshape(2, NS * 3, D)),
         "s_shift": np.ascontiguousarray(g("state_rwkv_shift")[:, s0:s1]),
         "s_wkv": np.ascontiguousarray(g("state_rwkv_wkv")[:, s0:s1])}
    for n, r in VEC_SPEC:
        d[n] = np.ascontiguousarray(g(n).reshape(r, 128))
    for n in BIG_W:
        d[n] = np.ascontiguousarray(g(n))
    return d


_NC_CACHE = {}


def run_cores(inp, TP, DEPTH, cores, **kw):
    key = (TP, DEPTH, tuple(sorted(kw.items())))
    if key not in _NC_CACHE:
        _NC_CACHE[key] = build(TP=TP, DEPTH=DEPTH, **kw)
    nc = _NC_CACHE[key]
    maps = [pack_inputs(inp, c, TP) for c in cores]
    res = run_bass_kernel_spmd(nc, maps, core_ids=list(range(len(cores))))
    return res.results


def assemble(results, B, TP):
    NB = len(results)
    y_p = np.stack([r["yp"] for r in results], 0)
    y_s = np.concatenate([r["ys"].reshape(NS, TS, D) for r in results], 0)
    psm = np.stack([r["o_p_small"] for r in results], 0)
    ssm = np.concatenate([r["o_s_small"].transpose(1, 0, 2) for r in results], 0)
    def small(a):
        lru_h = np.stack([a[:, 0], a[:, 4]], 0)
        lru_conv = np.stack([a[:, 1:4], a[:, 5:8]], 0)
        shift = np.stack([a[:, 8], a[:, 9]], 0)
        return lru_h, lru_conv, shift
    p_h, p_c, p_sh = small(psm); s_h, s_c, s_sh = small(ssm)
    p_wkv = np.stack([r["o_p_wkv"] for r in results], 1)
    s_wkv = np.concatenate([r["o_s_wkv"] for r in results], 1)
    f = lambda a: np.ascontiguousarray(a, dtype=np.float32)
    return tuple(f(a) for a in (y_p, y_s, p_h, p_c, p_sh, p_wkv, s_h, s_c, s_sh, s_wkv))


def kernel(**inp):
    res = run_cores(inp, 2048, 4, list(range(NCORE)))
    return assemble(res, NCORE, 2048)
```
